# Optimizing a Trainium2 kernel written in Bass

```python
import math
import jax
import jax.numpy as jnp
from jax import lax
import numpy as np

D_MODEL = 1024
BATCH = 4
SEQ = 4096
DEPTH = 2

N_MIXERS = 2
POOL_WINDOWS = (2, 4, 8, 16)
N_POOL_GROUPS = len(POOL_WINDOWS)
POOL_GROUP_DIM = D_MODEL // N_POOL_GROUPS
HEAD_DIM = 64
N_HEADS = D_MODEL // HEAD_DIM
DIL_CONFIGS = ((128, 1), (512, 4), (2048, 16))
N_DIL_GROUPS = len(DIL_CONFIGS)
ATTN_WIDTH = N_HEADS * HEAD_DIM
QKV_WIDTH = N_DIL_GROUPS * 3 * ATTN_WIDTH
D_FF = 2816
MACARON_WEIGHT = 0.5
ALPHA = (2.0 * DEPTH) ** 0.25
BETA = (8.0 * DEPTH) ** -0.25
LN_EPS = 1e-5
MASK_VALUE = -1e30

kernel_name = "hybrid_pool_dilated_attn_macaron_deepnorm"


def _alibi_slopes():
    n = N_DIL_GROUPS * N_HEADS
    s = 2.0 ** (-8.0 * np.arange(1, n + 1) / n)
    return s.reshape(N_DIL_GROUPS, N_HEADS).astype(np.float32)


def _layer_norm(x, g, b):
    x32 = x.astype(jnp.float32)
    mu = jnp.mean(x32, axis=-1, keepdims=True)
    var = jnp.mean(jnp.square(x32 - mu), axis=-1, keepdims=True)
    y = (x32 - mu) * lax.rsqrt(var + LN_EPS)
    return (y * g.astype(jnp.float32) + b.astype(jnp.float32)).astype(x.dtype)


def _swiglu(x, w_gate, w_up, w_down):
    return (jax.nn.silu(x @ w_gate) * (x @ w_up)) @ w_down


def _pool_mixer(x, w_in, w_group, scale, w_out):
    B, S, _ = x.shape
    u = (x @ w_in).reshape(B, S, N_POOL_GROUPS, POOL_GROUP_DIM).astype(jnp.float32)
    csum = jnp.concatenate([jnp.zeros_like(u[:, :1]), jnp.cumsum(u, axis=1)], axis=1)
    half = jnp.asarray([w // 2 for w in POOL_WINDOWS], dtype=jnp.int32)
    t = jnp.arange(S, dtype=jnp.int32)[:, None]
    lo = jnp.clip(t - half[None, :], 0, S)
    hi = jnp.clip(t + half[None, :], 0, S)
    gidx = jnp.arange(N_POOL_GROUPS)[None, :]
    win_sum = csum[:, hi, gidx] - csum[:, lo, gidx]
    mean = win_sum / (hi - lo).astype(jnp.float32)[None, :, :, None]
    mixed = (mean - u).astype(x.dtype)
    y = jnp.einsum('bsgc,gce->bsge', mixed, w_group).reshape(B, S, D_MODEL) * scale
    return y @ w_out


def _dilated_group(q, k, v, window, dilation, slopes):
    B, S, H, E = q.shape
    d = dilation
    L = S // d
    R = window // (2 * d)
    W = R
    nb = -(-L // W)
    Lp = nb * W

    def to_sub(a):
        return a.reshape(B, L, d, H, E).transpose(0, 2, 3, 1, 4)

    qs = jnp.pad(to_sub(q), ((0, 0),) * 3 + ((0, Lp - L), (0, 0))).reshape(B, d, H, nb, W, E)

    def windows(a):
        ap = jnp.pad(to_sub(a), ((0, 0),) * 3 + ((W, W + Lp - L), (0, 0)))
        ap = ap.reshape(B, d, H, nb + 2, W, E)
        return jnp.concatenate([ap[:, :, :, :-2], ap[:, :, :, 1:-1], ap[:, :, :, 2:]], axis=4)

    kw = windows(k)
    vw = windows(v)
    a_idx = jnp.arange(W)
    c_idx = jnp.arange(3 * W)
    n_idx = jnp.arange(nb)
    rel = c_idx[None, :] - W - a_idx[:, None]
    j = (n_idx[:, None, None] - 1) * W + c_idx[None, None, :]
    valid = (jnp.abs(rel)[None] <= R) & (j >= 0) & (j < L)
    dist = (d * jnp.abs(rel)).astype(jnp.float32)
    bias = -slopes[:, None, None] * dist[None]
    scores = jnp.einsum('bdhnqe,bdhnke->bdhnqk', qs.astype(jnp.float32),
                        kw.astype(jnp.float32)) * (E ** -0.5) + bias[:, None]
    scores = jnp.where(valid, scores, MASK_VALUE)
    lse = jax.nn.logsumexp(scores, axis=-1)
    p = jnp.exp(scores - lse[..., None])
    o = jnp.einsum('bdhnqk,bdhnke->bdhnqe', p, vw.astype(jnp.float32))
    o = o.reshape(B, d, H, Lp, E)[:, :, :, :L].transpose(0, 3, 1, 2, 4).reshape(B, S, H, E)
    lse = lse.reshape(B, d, H, Lp)[..., :L].transpose(0, 3, 1, 2).reshape(B, S, H)
    return o, lse


def _dilated_attention_mixer(x, w_qkv, w_out):
    B, S, _ = x.shape
    qkv = (x @ w_qkv).reshape(B, S, N_DIL_GROUPS, 3, N_HEADS, HEAD_DIM)
    slopes = jnp.asarray(_alibi_slopes())
    outs, lses = [], []
    for g, (window, dil) in enumerate(DIL_CONFIGS):
        o, l = _dilated_group(qkv[:, :, g, 0], qkv[:, :, g, 1], qkv[:, :, g, 2],
                              window, dil, slopes[g])
        outs.append(o)
        lses.append(l)
    wts = jax.nn.softmax(jnp.stack(lses), axis=0)
    o = jnp.sum(wts[..., None] * jnp.stack(outs), axis=0)
    return o.reshape(B, S, ATTN_WIDTH).astype(x.dtype) @ w_out


def setup_inputs(seed: int = 0) -> dict:
    key = jax.random.key(seed)
    ks = jax.random.split(key, 16)
    n_pool = (DEPTH + 1) // 2
    n_attn = DEPTH // 2
    f32 = jnp.float32

    def nrm(k, shape, scale):
        return jax.random.normal(k, shape, f32) * scale

    return {
        "x": jax.random.normal(ks[0], (BATCH, SEQ, D_MODEL), f32),
        "ffn1_w_gate": nrm(ks[1], (DEPTH, D_MODEL, D_FF), D_MODEL ** -0.5),
        "ffn1_w_up": nrm(ks[2], (DEPTH, D_MODEL, D_FF), D_MODEL ** -0.5),
        "ffn1_w_down": nrm(ks[3], (DEPTH, D_FF, D_MODEL), BETA * D_FF ** -0.5),
        "ffn2_w_gate": nrm(ks[4], (DEPTH, D_MODEL, D_FF), D_MODEL ** -0.5),
        "ffn2_w_up": nrm(ks[5], (DEPTH, D_MODEL, D_FF), D_MODEL ** -0.5),
        "ffn2_w_down": nrm(ks[6], (DEPTH, D_FF, D_MODEL), BETA * D_FF ** -0.5),
        "ln_gain": 1.0 + nrm(ks[7], (DEPTH, 3, D_MODEL), 0.02),
        "ln_bias": nrm(ks[8], (DEPTH, 3, D_MODEL), 0.02),
        "pool_w_in": nrm(ks[9], (n_pool, D_MODEL, D_MODEL), D_MODEL ** -0.5),
        "pool_w_group": nrm(ks[10], (n_pool, N_POOL_GROUPS, POOL_GROUP_DIM, POOL_GROUP_DIM),
                            POOL_GROUP_DIM ** -0.5),
        "pool_scale": 1.0 + nrm(ks[11], (n_pool, D_MODEL), 0.1),
        "pool_w_out": nrm(ks[12], (n_pool, D_MODEL, D_MODEL), BETA * D_MODEL ** -0.5),
        "attn_w_qkv": nrm(ks[13], (n_attn, D_MODEL, QKV_WIDTH), D_MODEL ** -0.5),
        "attn_w_out": nrm(ks[14], (n_attn, ATTN_WIDTH, D_MODEL), BETA * ATTN_WIDTH ** -0.5),
    }


def reference(x, ffn1_w_gate, ffn1_w_up, ffn1_w_down, ffn2_w_gate, ffn2_w_up, ffn2_w_down,
              ln_gain, ln_bias, pool_w_in, pool_w_group, pool_scale, pool_w_out,
              attn_w_qkv, attn_w_out):
    for i in range(DEPTH):
        h = _swiglu(x, ffn1_w_gate[i], ffn1_w_up[i], ffn1_w_down[i])
        x = _layer_norm(ALPHA * x + MACARON_WEIGHT * h, ln_gain[i, 0], ln_bias[i, 0])
        if i % N_MIXERS == 0:
            p = i // N_MIXERS
            m = _pool_mixer(x, pool_w_in[p], pool_w_group[p], pool_scale[p], pool_w_out[p])
        else:
            a = i // N_MIXERS
            m = _dilated_attention_mixer(x, attn_w_qkv[a], attn_w_out[a])
        x = _layer_norm(ALPHA * x + m, ln_gain[i, 1], ln_bias[i, 1])
        h = _swiglu(x, ffn2_w_gate[i], ffn2_w_up[i], ffn2_w_down[i])
        x = _layer_norm(ALPHA * x + MACARON_WEIGHT * h, ln_gain[i, 2], ln_bias[i, 2])
    return x
```

```python
import numpy as np
import concourse.bass as bass
import concourse.mybir as mybir
from concourse.bass_utils import run_bass_kernel_spmd

F32 = mybir.dt.float32
BF16 = mybir.dt.bfloat16
AF = mybir.ActivationFunctionType
ALU = mybir.AluOpType

D_MODEL = 1024
SEQ = 4096
BATCH = 4
D_FF = 2816
NFC = 22
OWN = 2048
HALO = 1024
TC = OWN + HALO + 8
TH = OWN + HALO
ALPHA = 4.0 ** 0.25
LN_EPS = 1e-5
EPS_P = LN_EPS / (ALPHA * ALPHA)
DIL = ((128, 1), (512, 4), (2048, 16))
BIGD = 1.0e5

ENGS = ("pe", "act", "dve", "pool", "sp")


class Buf:
    __slots__ = ("name", "wdep", "rdeps")

    def __init__(self, name):
        self.name = name
        self.wdep = None
        self.rdeps = {}


class Sched:
    def __init__(self, nc):
        self.nc = nc
        self.sem = {}
        self.cnt = {}
        for e in ENGS:
            self.sem[e] = nc.alloc_semaphore("s_" + e)
            self.cnt[e] = 0
        self.seen = {e: {} for e in ENGS}
        self.ops = {e: [] for e in ENGS}

    def new_dma_sem(self, key):
        self.sem[key] = self.nc.alloc_semaphore("d_" + key)
        self.cnt[key] = 0
        return key

    def _collect(self, reads, writes):
        w = {}
        for b in reads:
            d = b.wdep
            if d is not None and w.get(d[0], 0) < d[1]:
                w[d[0]] = d[1]
        for b in writes:
            d = b.wdep
            if d is not None and w.get(d[0], 0) < d[1]:
                w[d[0]] = d[1]
            for k, v in b.rdeps.items():
                if w.get(k, 0) < v:
                    w[k] = v
        return w

    def _need(self, e, w):
        need = []
        s = self.seen[e]
        for k, v in w.items():
            if s.get(k, 0) < v:
                need.append((k, v))
                s[k] = v
        return need

    def op(self, e, fn, reads=(), writes=()):
        need = self._need(e, self._collect(reads, writes))
        self.cnt[e] += 1
        v = self.cnt[e]
        self.ops[e].append((need, fn, (e, 1)))
        for b in reads:
            if b.rdeps.get(e, 0) < v:
                b.rdeps[e] = v
        for b in writes:
            b.wdep = (e, v)
            b.rdeps = {}
        return (e, v)

    def dma(self, q, fn, semkey, reads=(), writes=()):
        need = self._need(q, self._collect(reads, writes))
        self.cnt[semkey] += 16
        v = self.cnt[semkey]
        self.ops[q].append((need, fn, (semkey, 16)))
        for b in reads:
            if b.rdeps.get(semkey, 0) < v:
                b.rdeps[semkey] = v
        for b in writes:
            b.wdep = (semkey, v)
            b.rdeps = {}
        return (semkey, v)

    def wait_all(self, e, keys):
        w = {k: self.cnt[k] for k in keys if self.cnt[k] > 0}
        need = self._need(e, w)
        if need:
            self.ops[e].append((need, None, None))

    def barrier(self, engines=("pe", "act", "dve", "pool")):
        for e in engines:
            self.wait_all(e, list(engines))

    def emit(self, block):
        sched = self

        def make(e):
            def body(eng):
                for need, fn, inc in sched.ops[e]:
                    for k, v in need:
                        eng.wait_ge(sched.sem[k], v)
                    if fn is not None:
                        ins = fn(eng)
                        ins.then_inc(sched.sem[inc[0]], inc[1])
            return body
        block.tensor(make("pe"))
        block.scalar(make("act"))
        block.vector(make("dve"))
        block.gpsimd(make("pool"))
        block.sync(make("sp"))


def _halves(nt):
    return [(h0, min(512, nt - h0)) for h0 in range(0, nt, 512)]


def build_program(dbg=None):
    nc = bass.Bass("TRN2", target_bir_lowering=False)
    S = Sched(nc)

    def din(name, shape):
        return nc.dram_tensor(name, list(shape), F32, kind="ExternalInput").ap()

    xT_d = din("xT", (128, 8, TC))
    lnp_d = din("lnp", (128, 96))
    pflag_d = din("pflag", (128, 2))
    pinv_d = din("pinv", (128, 64))
    pscale_d = din("pscale", (128, 8))
    dtile_d = din("dtile", (128, 256))
    ident_d = din("ident", (128, 128))
    w_gu_d = din("w_gu", (88, 128, 2048))
    w_d_d = din("w_d", (32, 128, 2816))
    w_pin_d = din("w_pin", (128, 8192))
    w_pout_d = din("w_pout", (128, 8192))
    w_pgrp_d = din("w_pgrp", (128, 2048))
    w_qkv_d = din("w_qkv", (24, 128, 3072))
    w_ao_d = din("w_ao", (8, 128, 1024))
    if dbg is None:
        y_d = nc.dram_tensor("yT", [128, 8, OWN], F32, kind="ExternalOutput").ap()
    else:
        y_d = nc.dram_tensor("yT", [128, 8, TC], F32, kind="ExternalOutput").ap()

    base = (nc.sbuf_base + 31) // 32 * 32
    OFF_X = 0
    OFF_C = 98560
    OFF_R1 = OFF_C + 4096
    OFF_RING = OFF_R1 + 61440
    OFF_LN = OFF_RING + 23552
    OFF_SP = OFF_LN + 16384
    ARENA = OFF_SP + 4608
    arena = nc.alloc_sbuf_tensor("arena", [128, ARENA // 4], F32)
    abase = base
    assert nc.sbuf_base >= abase + ARENA

    def at(name, shape, dtype, off):
        return nc.alloc_sbuf_tensor_at(name, list(shape), dtype, offset=abase + off)

    xres = at("xres", (128, 8, TC), F32, OFF_X)
    lnp = at("lnp", (128, 96), F32, OFF_C)
    pflag = at("pflag", (128, 2), F32, OFF_C + 384)
    pinv = at("pinv", (128, 64), F32, OFF_C + 416)
    pscale = at("pscale", (128, 8), F32, OFF_C + 672)
    dtile = at("dtile", (128, 256), F32, OFF_C + 704)
    ones_bf = at("ones_bf", (128, 128), BF16, OFF_C + 1728)
    ident_bf = at("ident_bf", (128, 128), BF16, OFF_C + 1984)
    hsave = at("hsave", (128, 8, 8), F32, OFF_C + 2240)
    epst = at("epst", (128, 1), F32, OFF_C + 2496)
    hbuf = at("hbuf", (128, NFC, 1024), BF16, OFF_R1)
    xb = at("xb", (128, 8, 1024), BF16, OFF_R1 + 45056)
    gu_slot = [at(f"gu{i}", (128, 2, 8, 128), BF16, OFF_RING + i * 4096) for i in range(3)]
    d_slot = [at(f"dw{i}", (128, NFC, 128), BF16, OFF_RING + 12288 + i * 5632) for i in range(2)]
    zb = at("zb", (128, 2, 1024), BF16, OFF_LN)
    zq = at("zq", (128, 2, 1024), BF16, OFF_LN + 4096)
    meant = at("meant", (128, 1024), F32, OFF_LN + 8192)
    rstdt = at("rstdt", (128, 1024), F32, OFF_LN + 12288)
    sgt = at("sgt", (128, 1024), F32, OFF_SP)
    p_win = at("p_win", (128, 8, 1024), BF16, OFF_R1)
    p_wout = at("p_wout", (128, 8, 1024), BF16, OFF_R1 + 16384)
    p_wgrp = at("p_wgrp", (128, 4, 2, 256), BF16, OFF_R1 + 32768)
    p_xb = at("p_xb", (128, 8, 512), BF16, OFF_R1 + 36864)
    p_yb = at("p_yb", (128, 8, 512), BF16, OFF_R1 + 45056)
    p_mx = at("p_mx", (128, 8, 512), BF16, OFF_R1 + 53248)
    p_u = at("p_u", (128, 8, 528), F32, OFF_RING)
    p_t = [at(f"p_t{i}", (128, 528), F32, OFF_RING + 16896 + i * 2112) for i in range(2)]
    p_wn = at("p_wn", (128, 512), F32, OFF_RING + 21120)
    x1b = at("x1b", (128, 8, TH), BF16, OFF_R1)
    qT = at("qT", (128, OWN), BF16, OFF_R1 + 49152)
    kT = at("kT", (128, TH), BF16, OFF_R1 + 53248)
    PTt = [at(f"PT{i}", (128, 256), BF16, OFF_R1 + 59392 + i * 512) for i in range(2)]
    wqkv = at("wqkv", (128, 3, 8, 128), BF16, OFF_RING)
    vT = at("vT", (128, TH), BF16, OFF_RING + 12288)
    wao = at("wao", (128, 1024), BF16, OFF_RING + 18432)
    vaug = at("vaug", (128, 32, 192), BF16, OFF_LN)
    o_hp = at("o_hp", (128, OWN), BF16, OFF_LN + 12288)
    tt = [at(f"tt{i}", (128, 256), F32, OFF_SP + i * 1024) for i in range(2)]
    rden = at("rden", (128, 512), F32, OFF_SP + 2048)

    PS = [nc.alloc_psum_tensor(f"ps{i}", [128, 1024], F32) for i in range(4)]
    PSB = [Buf(f"ps{i}") for i in range(4)]

    for k in ["x0", "x1", "x2", "x3", "const", "gu0", "gu1", "gu2", "dw0", "dw1", "pw", "aw", "ao", "out"]:
        S.new_dma_sem(k)

    B_const = Buf("const")
    B_ident = Buf("ident")
    B_ones = Buf("ones")
    TILES = [(0, 1024), (1024, 1024), (2048, 1024), (3072, 8)]
    XR = [[Buf(f"xr{t}_{c}") for c in range(8)] for t in range(4)]
    XB = [Buf(f"xb{c}") for c in range(8)]
    HB = [Buf(f"hb{f}") for f in range(NFC)]
    GU = [Buf(f"gu{i}") for i in range(3)]
    DW = [Buf(f"dw{i}") for i in range(2)]
    B_sg = Buf("sg")
    B_zb = [Buf("zb0"), Buf("zb1")]
    B_zq = [Buf("zq0"), Buf("zq1")]
    B_mean = Buf("mean")
    B_rstd = Buf("rstd")
    ring = {"gu": 0, "dw": 0}

    for (dst, src) in [(lnp, lnp_d), (pflag, pflag_d), (pinv, pinv_d), (pscale, pscale_d), (dtile, dtile_d)]:
        S.dma("sp", (lambda dst, src: lambda e: e.dma_start(out=dst[:], in_=src))(dst, src), "const", writes=[B_const])
    B_const.wdep = ("const", S.cnt["const"])
    S.dma("pool", lambda e: e.dma_start(out=ident_bf[:], in_=ident_d), "pw", writes=[B_ident])
    S.op("dve", lambda e: e.memset(ones_bf[:], 1.0), writes=[B_ones])
    S.op("dve", lambda e: e.memset(epst[:], EPS_P), writes=[B_const])
    B_const.wdep = ("const", S.cnt["const"])
    B_eps = Buf("eps")
    B_eps.wdep = ("dve", S.cnt["dve"])
    for t, (t0, nt) in enumerate(TILES):
        S.dma("sp", (lambda t0, nt: lambda e: e.dma_start(out=xres[:, :, t0:t0 + nt], in_=xT_d[:, :, t0:t0 + nt]))(t0, nt),
              f"x{t}", writes=XR[t])

    def ln_cols(l, s, t0, nt, xr_bufs):
        hv = _halves(nt)
        for c in range(8):
            sl = c % 2
            S.op("act", (lambda c, sl: lambda e: e.activation(out=zb[:, sl, 0:nt], in_=xres[:, c, t0:t0 + nt], func=AF.Copy))(c, sl),
                 reads=[xr_bufs[c]], writes=[B_zb[sl]])
            S.op("act", (lambda c, sl: lambda e: e.activation(out=zq[:, sl, 0:nt], in_=xres[:, c, t0:t0 + nt], func=AF.Square))(c, sl),
                 reads=[xr_bufs[c]], writes=[B_zq[sl]])

            def mm(e, c=c, sl=sl):
                for (h0, hn) in hv:
                    e.matmul(PS[0][:, h0:h0 + hn], lhsT=ones_bf[:, :], rhs=zb[:, sl, h0:h0 + hn], start=(c == 0), stop=(c == 7))
                for (h0, hn) in hv:
                    ins = e.matmul(PS[1][:, h0:h0 + hn], lhsT=ones_bf[:, :], rhs=zq[:, sl, h0:h0 + hn], start=(c == 0), stop=(c == 7))
                return ins
            S.op("pe", mm, reads=[B_zb[sl], B_zq[sl], B_ones], writes=[PSB[0], PSB[1]])
        S.op("dve", lambda e: e.tensor_scalar(out=meant[:, 0:nt], in0=PS[0][:, 0:nt], scalar1=1.0 / D_MODEL, scalar2=None, op0=ALU.mult),
             reads=[PSB[0]], writes=[B_mean])
        S.op("dve", lambda e: e.tensor_tensor(out=rstdt[:, 0:nt], in0=meant[:, 0:nt], in1=meant[:, 0:nt], op=ALU.mult),
             reads=[B_mean], writes=[B_rstd])
        S.op("dve", lambda e: e.scalar_tensor_tensor(out=rstdt[:, 0:nt], in0=PS[1][:, 0:nt], scalar=1.0 / D_MODEL, in1=rstdt[:, 0:nt],
                                                     op0=ALU.mult, op1=ALU.subtract),
             reads=[PSB[1], B_rstd], writes=[B_rstd])
        S.op("act", lambda e: e.activation(out=rstdt[:, 0:nt], in_=rstdt[:, 0:nt], func=AF.Sqrt, bias=epst[:, 0:1], scale=1.0),
             reads=[B_rstd, B_eps], writes=[B_rstd])
        S.op("dve", lambda e: e.reciprocal(out=rstdt[:, 0:nt], in_=rstdt[:, 0:nt]), reads=[B_rstd], writes=[B_rstd])
        gi = (l * 3 + s) * 16
        for c in range(8):
            xs = (lambda c: xres[:, c, t0:t0 + nt])
            S.op("dve", (lambda c: lambda e: e.tensor_tensor(out=xres[:, c, t0:t0 + nt], in0=xres[:, c, t0:t0 + nt], in1=meant[:, 0:nt], op=ALU.subtract))(c),
                 reads=[xr_bufs[c], B_mean], writes=[xr_bufs[c]])
            S.op("dve", (lambda c: lambda e: e.tensor_tensor(out=xres[:, c, t0:t0 + nt], in0=xres[:, c, t0:t0 + nt], in1=rstdt[:, 0:nt], op=ALU.mult))(c),
                 reads=[xr_bufs[c], B_rstd], writes=[xr_bufs[c]])
            S.op("act", (lambda c: lambda e: e.activation(out=xres[:, c, t0:t0 + nt], in_=xres[:, c, t0:t0 + nt], func=AF.Identity,
                                                          bias=lnp[:, gi + 8 + c:gi + 9 + c], scale=lnp[:, gi + c:gi + c + 1]))(c),
                 reads=[xr_bufs[c], B_const], writes=[xr_bufs[c]])

    def ffn_tile(l, fi, s, t, xr_bufs):
        t0, nt = TILES[t]
        hv = _halves(nt)
        for c in range(8):
            eng = "dve" if c % 2 == 0 else "act"
            if eng == "dve":
                S.op("dve", (lambda c: lambda e: e.tensor_copy(out=xb[:, c, 0:nt], in_=xres[:, c, t0:t0 + nt]))(c), reads=[xr_bufs[c]], writes=[XB[c]])
            else:
                S.op("act", (lambda c: lambda e: e.activation(out=xb[:, c, 0:nt], in_=xres[:, c, t0:t0 + nt], func=AF.Copy))(c), reads=[xr_bufs[c]], writes=[XB[c]])
        wbase = (l * 2 + fi) * NFC
        for fc in range(NFC):
            sl = ring["gu"] % 3
            ring["gu"] += 1
            S.dma("pool", (lambda sl, idx: lambda e: e.dma_start(out=gu_slot[sl][:], in_=w_gu_d[idx].rearrange("p (a k f) -> p a k f", a=2, k=8)))(sl, wbase + fc),
                  f"gu{sl}", writes=[GU[sl]])
            pg, pu = (0, 1) if fc % 2 == 0 else (2, 3)

            def mm(e, sl=sl, pg=pg, pu=pu):
                for (pp, a) in ((pg, 0), (pu, 1)):
                    for (h0, hn) in hv:
                        for kc in range(8):
                            ins = e.matmul(PS[pp][:, h0:h0 + hn], lhsT=gu_slot[sl][:, a, kc, :], rhs=xb[:, kc, h0:h0 + hn], start=(kc == 0), stop=(kc == 7))
                return ins
            S.op("pe", mm, reads=[GU[sl]] + XB, writes=[PSB[pg], PSB[pu]])
            S.op("act", (lambda pg: lambda e: e.activation(out=sgt[:, 0:nt], in_=PS[pg][:, 0:nt], func=AF.Silu))(pg), reads=[PSB[pg]], writes=[B_sg])
            S.op("dve", (lambda pu, fc: lambda e: e.tensor_tensor(out=hbuf[:, fc, 0:nt], in0=sgt[:, 0:nt], in1=PS[pu][:, 0:nt], op=ALU.mult))(pu, fc),
                 reads=[B_sg, PSB[pu]], writes=[HB[fc]])
        dbase = (l * 2 + fi) * 8
        for dc in range(8):
            sl = ring["dw"] % 2
            ring["dw"] += 1
            S.dma("pool", (lambda sl, idx: lambda e: e.dma_start(out=d_slot[sl][:], in_=w_d_d[idx].rearrange("p (f d) -> p f d", f=NFC)))(sl, dbase + dc),
                  f"dw{sl}", writes=[DW[sl]])
            py = dc % 4

            def mm2(e, sl=sl, py=py):
                for (h0, hn) in hv:
                    for fc in range(NFC):
                        ins = e.matmul(PS[py][:, h0:h0 + hn], lhsT=d_slot[sl][:, fc, :], rhs=hbuf[:, fc, h0:h0 + hn], start=(fc == 0), stop=(fc == NFC - 1))
                return ins
            S.op("pe", mm2, reads=[DW[sl]] + HB, writes=[PSB[py]])
            S.op("dve", (lambda dc, py: lambda e: e.scalar_tensor_tensor(out=xres[:, dc, t0:t0 + nt], in0=PS[py][:, 0:nt], scalar=0.5 / ALPHA,
                                                                           in1=xres[:, dc, t0:t0 + nt], op0=ALU.mult, op1=ALU.add))(dc, py),
                 reads=[PSB[py], xr_bufs[dc]], writes=[xr_bufs[dc]])
        ln_cols(l, s, t0, nt, xr_bufs)

    def dump_and_finish():
        ncols = OWN if dbg is None else TC
        S.barrier()
        S.wait_all("sp", ["pe", "act", "dve"])
        for c in range(8):
            S.dma("sp", (lambda c: lambda e: e.dma_start(out=y_d[:, c, :], in_=xres[:, c, 0:ncols]))(c), "out",
                  reads=[b for t in range(4) for b in XR[t]])
        S.wait_all("sp", ["out"])
        S.wait_all("pool", ["gu0", "gu1", "gu2", "dw0", "dw1", "pw", "aw", "ao"])
        with nc.Block() as block:
            S.emit(block)
        return nc

    for t in range(4):
        ffn_tile(0, 0, 0, t, XR[t])
    if dbg == "l0ffn1":
        return dump_and_finish()

    S.barrier()
    ALLX = [b for t in range(4) for b in XR[t]]
    B_pw = Buf("pw")
    S.dma("pool", lambda e: e.dma_start(out=p_win[:], in_=w_pin_d.rearrange("p (k e) -> p k e", k=8)), "pw", writes=[B_pw])
    S.dma("pool", lambda e: e.dma_start(out=p_wout[:], in_=w_pout_d.rearrange("p (k e) -> p k e", k=8)), "pw", writes=[B_pw])
    S.dma("pool", lambda e: e.dma_start(out=p_wgrp[:], in_=w_pgrp_d.rearrange("p (g c e) -> p g c e", g=4, c=2)), "pw", writes=[B_pw])
    B_pw.wdep = ("pw", S.cnt["pw"])
    B_pxb = [Buf(f"pxb{c}") for c in range(8)]
    B_pu = [Buf(f"pu{c}") for c in range(8)]
    B_pmx = [Buf(f"pmx{c}") for c in range(8)]
    B_pyb = [Buf(f"pyb{c}") for c in range(8)]
    B_pt = [Buf("pt0"), Buf("pt1")]
    B_pwn = Buf("pwn")
    B_hs = Buf("hsave")
    S.op("dve", lambda e: e.memset(p_u[:, :, 0:8], 0.0), writes=B_pu)
    TP = 448
    ptiles = [(a, min(a + TP, TH)) for a in range(0, TH, TP)]
    XP = [Buf(f"xp{c}") for c in range(8)]
    def pool_tile(ti, a, b):
        n_o = b - a
        first = (ti == 0)
        ua = 0 if first else a - 8
        ub = b + 8
        n_u = ub - ua
        loff = 8 if first else 0
        for c in range(8):
            if first:
                S.op("act", (lambda c: lambda e: e.activation(out=p_xb[:, c, 0:n_u], in_=xres[:, c, ua:ub], func=AF.Copy))(c), reads=[XP[c]], writes=[B_pxb[c]])
            else:
                S.op("act", (lambda c: lambda e: e.activation(out=p_xb[:, c, 0:8], in_=hsave[:, c, :], func=AF.Copy))(c), reads=[B_hs], writes=[B_pxb[c]])
                S.op("act", (lambda c: lambda e: e.activation(out=p_xb[:, c, 8:n_u], in_=xres[:, c, a:ub], func=AF.Copy))(c), reads=[XP[c]], writes=[B_pxb[c]])
        S.op("dve", lambda e: e.tensor_copy(out=hsave[:, :, :], in_=xres[:, :, b - 8:b]), reads=XP + B_pxb, writes=[B_hs])
        for c in range(8):
            pp = c % 4

            def mmu(e, c=c, pp=pp):
                for kc in range(8):
                    ins = e.matmul(PS[pp][:, 0:n_u], lhsT=p_win[:, kc, c * 128:(c + 1) * 128], rhs=p_xb[:, kc, 0:n_u], start=(kc == 0), stop=(kc == 7))
                return ins
            S.op("pe", mmu, reads=[B_pw] + B_pxb, writes=[PSB[pp]])
            S.op("act", (lambda c, pp: lambda e: e.activation(out=p_u[:, c, loff:loff + n_u], in_=PS[pp][:, 0:n_u], func=AF.Copy))(c, pp),
                 reads=[PSB[pp]], writes=[B_pu[c]])
        n_loc = n_u + loff
        lo0 = 8
        for c in range(8):
            g = c // 2
            h = 1 << g
            w = 2 * h
            src = p_u[:, c, :]
            srcb = B_pu[c]
            ln = 1
            k = 0
            while ln < w:
                dst = p_t[k % 2]
                cnt = n_loc - 2 * ln + 1
                S.op("dve", (lambda src, dst, ln, cnt: lambda e: e.tensor_tensor(out=dst[:, 0:cnt], in0=src[:, 0:cnt], in1=src[:, ln:ln + cnt], op=ALU.add))(src, dst, ln, cnt),
                     reads=[srcb], writes=[B_pt[k % 2]])
                src = dst
                srcb = B_pt[k % 2]
                ln *= 2
                k += 1
            s0 = lo0 - h
            S.op("dve", (lambda src, s0: lambda e: e.tensor_scalar(out=p_wn[:, 0:n_o], in0=src[:, s0 + 1:s0 + 1 + n_o], scalar1=pflag[:, 1:2], scalar2=None, op0=ALU.mult))(src, s0),
                 reads=[srcb, B_const], writes=[B_pwn])
            S.op("dve", (lambda src, s0: lambda e: e.scalar_tensor_tensor(out=p_wn[:, 0:n_o], in0=src[:, s0:s0 + n_o], scalar=pflag[:, 0:1], in1=p_wn[:, 0:n_o],
                                                                            op0=ALU.mult, op1=ALU.add))(src, s0),
                 reads=[srcb, B_const, B_pwn], writes=[B_pwn])
            S.op("dve", (lambda c, w: lambda e: e.scalar_tensor_tensor(out=p_mx[:, c, 0:n_o], in0=p_wn[:, 0:n_o], scalar=1.0 / w, in1=p_u[:, c, lo0:lo0 + n_o],
                                                                         op0=ALU.mult, op1=ALU.subtract))(c, w),
                 reads=[B_pwn, B_pu[c]], writes=[B_pmx[c]])
            if first:
                S.op("dve", (lambda c: lambda e: e.tensor_tensor(out=p_wn[:, 0:8], in0=p_wn[:, 0:8], in1=pinv[:, c * 8:c * 8 + 8], op=ALU.mult))(c),
                     reads=[B_pwn, B_const], writes=[B_pwn])
                S.op("dve", (lambda c: lambda e: e.tensor_tensor(out=p_mx[:, c, 0:8], in0=p_wn[:, 0:8], in1=p_u[:, c, lo0:lo0 + 8], op=ALU.subtract))(c),
                     reads=[B_pwn, B_pu[c]], writes=[B_pmx[c]])
        for ec in range(8):
            g = ec // 2
            eh = ec % 2
            pp = ec % 4

            def mmg(e, g=g, eh=eh, pp=pp):
                for cc in range(2):
                    ins = e.matmul(PS[pp][:, 0:n_o], lhsT=p_wgrp[:, g, cc, eh * 128:(eh + 1) * 128], rhs=p_mx[:, 2 * g + cc, 0:n_o], start=(cc == 0), stop=(cc == 1))
                return ins
            S.op("pe", mmg, reads=[B_pw, B_pmx[2 * g], B_pmx[2 * g + 1]], writes=[PSB[pp]])
            S.op("act", (lambda ec, pp: lambda e: e.activation(out=p_yb[:, ec, 0:n_o], in_=PS[pp][:, 0:n_o], func=AF.Identity, scale=pscale[:, ec:ec + 1]))(ec, pp),
                 reads=[PSB[pp], B_const], writes=[B_pyb[ec]])
        if dbg == "pooldbg" and ti == 0:
            S.barrier()
            Bd = Buf("dbgout")
            S.dma("pool", lambda e: e.dma_start(out=y_d[:, :, 0:528], in_=p_u[:, :, :]), "out", reads=B_pu)
            S.dma("pool", lambda e: e.dma_start(out=y_d[:, :, 600:1048], in_=p_mx[:, :, 0:448]), "out", reads=B_pmx)
            S.dma("pool", lambda e: e.dma_start(out=y_d[:, :, 1100:1548], in_=p_yb[:, :, 0:448]), "out", reads=B_pyb)
            S.dma("pool", lambda e: e.dma_start(out=y_d[:, :, 1600:2056], in_=p_xb[:, :, 0:456]), "out", reads=B_pxb)
            S.dma("pool", lambda e: e.dma_start(out=y_d[:, :, 2056:3080], in_=p_wout[:, :, :]), "out", reads=[B_pw])
            S.wait_all("pool", ["out"])
            with nc.Block() as block:
                S.emit(block)
            return nc
        for dc in range(8):
            pp = dc % 4

            def mmo(e, dc=dc, pp=pp):
                for ec in range(8):
                    ins = e.matmul(PS[pp][:, 0:n_o], lhsT=p_wout[:, ec, dc * 128:(dc + 1) * 128], rhs=p_yb[:, ec, 0:n_o], start=(ec == 0), stop=(ec == 7))
                return ins
            S.op("pe", mmo, reads=[B_pw] + B_pyb, writes=[PSB[pp]])
            S.op("dve", (lambda dc, pp: lambda e: e.scalar_tensor_tensor(out=xres[:, dc, a:b], in0=PS[pp][:, 0:n_o], scalar=1.0 / ALPHA, in1=xres[:, dc, a:b],
                                                                           op0=ALU.mult, op1=ALU.add))(dc, pp),
                 reads=[PSB[pp], XP[dc], B_hs], writes=[XP[dc]])
        if dbg == "pooldbg2" and ti == 0:
            S.barrier()
            for c in range(8):
                S.dma("sp", (lambda c: lambda e: e.dma_start(out=y_d[:, c, 0:448], in_=xres[:, c, 0:448]))(c), "out", reads=XP)
            S.wait_all("sp", ["out"])
            with nc.Block() as block:
                S.emit(block)
            return nc
        ln_cols(0, 1, a, n_o, XP)
        if dbg is not None and dbg.startswith("pooldbg3:") and ti == int(dbg.split(":")[1]):
            S.barrier()
            for c in range(8):
                S.dma("sp", (lambda c: lambda e: e.dma_start(out=y_d[:, c, :], in_=xres[:, c, :]))(c), "out", reads=XP)
            S.wait_all("sp", ["out"])
            with nc.Block() as block:
                S.emit(block)
            return nc
        return None
    for ti, (a, b) in enumerate(ptiles):
        _r = pool_tile(ti, a, b)
        if _r is not None:
            return _r
    S.barrier()
    if dbg == "l0pool":
        return dump_and_finish()

    for t in range(3):
        ffn_tile(0, 1, 2, t, XR[t])
    if dbg == "l0":
        return dump_and_finish()
    for t in range(3):
        ffn_tile(1, 0, 0, t, XR[t])
    S.barrier()
    if dbg == "l1ffn1":
        return dump_and_finish()

    B_x1b = [Buf(f"x1b{c}") for c in range(8)]
    for c in range(8):
        if c % 2 == 0:
            S.op("dve", (lambda c: lambda e: e.tensor_copy(out=x1b[:, c, :], in_=xres[:, c, 0:TH]))(c), reads=ALLX, writes=[B_x1b[c]])
        else:
            S.op("act", (lambda c: lambda e: e.activation(out=x1b[:, c, :], in_=xres[:, c, 0:TH], func=AF.Copy))(c), reads=ALLX, writes=[B_x1b[c]])
    S.barrier()
    B_vaug = Buf("vaug")
    S.op("dve", lambda e: e.memset(vaug[:, :, 64:128], 1.0), writes=[B_vaug])
    B_wq = Buf("wqkv")
    B_q = Buf("qT")
    B_k = Buf("kT")
    B_v = Buf("vT")
    B_pt2 = [Buf("PT0"), Buf("PT1")]
    B_tt = [Buf("tt0"), Buf("tt1")]
    B_os = [[Buf(f"os{h}_{q}") for q in range(4)] for h in range(2)]
    B_rd = Buf("rden")
    B_ohp = Buf("ohp")
    B_wao = Buf("wao")
    XO = [[Buf(f"xo{dc}_{q}") for q in range(4)] for dc in range(8)]
    PSA = [Buf(f"psa{i}") for i in range(8)]

    def osum(hd, qt):
        return xres[:, hd * 4 + qt, OWN:OWN + 512]

    slopes = 2.0 ** (-8.0 * np.arange(1, 49) / 48.0)
    slopes = slopes.reshape(3, 16)
    cnt_sc = [0]
    def attn_group(hp, g, win, d):
        Lq = OWN // d
        Lk = Lq + 64
        ntok_k = Lk * d
        S.dma("pool", (lambda idx: lambda e: e.dma_start(out=wqkv[:], in_=w_qkv_d[idx].rearrange("p (m k f) -> p m k f", m=3, k=8)))(hp * 3 + g),
              "aw", writes=[B_wq])
        for m, (dstT, dstB, ntok) in enumerate(((qT, B_q, OWN), (kT, B_k, ntok_k), (vT, B_v, ntok_k))):
            Ld = ntok // d
            dview = dstT[:, 0:ntok].rearrange("p (r i) -> p r i", r=d)
            for ti, j0 in enumerate(range(0, ntok, 512)):
                n = min(512, ntok - j0)
                pbank = 4 + (cnt_sc[0] % 2)
                cnt_sc[0] += 1
                ph = (pbank - 4) * 512

                def mmp(e, m=m, j0=j0, n=n, ph=ph):
                    for kc in range(8):
                        ins = e.matmul(PS[2][:, ph:ph + n], lhsT=wqkv[:, m, kc, :], rhs=x1b[:, kc, j0:j0 + n], start=(kc == 0), stop=(kc == 7))
                    return ins
                S.op("pe", mmp, reads=[B_wq] + B_x1b, writes=[PSA[pbank]])
                i0 = j0 // d
                ni = n // d
                src = PS[2][:, ph:ph + n].rearrange("p (i r) -> p r i", r=d)
                eng = "act" if (ti % 2 == 0) else "dve"
                if eng == "act":
                    S.op("act", (lambda dview, src, i0, ni: lambda e: e.activation(out=dview[:, :, i0:i0 + ni], in_=src, func=AF.Copy))(dview, src, i0, ni),
                         reads=[PSA[pbank]], writes=[dstB])
                else:
                    S.op("dve", (lambda dview, src, i0, ni: lambda e: e.tensor_copy(out=dview[:, :, i0:i0 + ni], in_=src))(dview, src, i0, ni),
                         reads=[PSA[pbank]], writes=[dstB])
        chunks = []
        for r in range(d):
            for k0 in range(0, Lk, 128):
                chunks.append((r, k0, min(128, Lk - k0)))
        psT = PS[3][:, :].bitcast(BF16)
        for ci, (r, k0, nk) in enumerate(chunks):
            pbank = 6 + (ci % 2)
            pcol = (ci % 2) * 1024

            def tr(e, r=r, k0=k0, nk=nk, pcol=pcol):
                return e.transpose(psT[0:nk, pcol:pcol + 128], vT[:, r * Lk + k0:r * Lk + k0 + nk], ident_bf[:, :])
            S.op("pe", tr, reads=[B_v, B_ident], writes=[PSA[pbank]])
            S.op("act", (lambda ci, nk, pcol: lambda e: e.activation(out=vaug[0:nk, ci, 0:64], in_=psT[0:nk, pcol:pcol + 64], func=AF.Copy))(ci, nk, pcol),
                 reads=[PSA[pbank]], writes=[B_vaug])
            S.op("dve", (lambda ci, nk, pcol: lambda e: e.tensor_copy(out=vaug[0:nk, ci, 128:192], in_=psT[0:nk, pcol + 64:pcol + 128]))(ci, nk, pcol),
                 reads=[PSA[pbank]], writes=[B_vaug])
        for hd in range(2):
            hg = 2 * hp + hd
            cneg = -float(slopes[g, hg]) * d * 8.0
            hrow = slice(64 * hd, 64 * hd + 64)
            started = [False] * 4
            for ci, (r, k0, nk) in enumerate(chunks):
                i0 = max(0, k0 - 64)
                i1 = min(Lq, k0 + nk + 64)
                if i1 <= i0:
                    continue
                nq = i1 - i0
                doff = i0 - (k0 - 64)
                sb = 4 + (cnt_sc[0] % 2)
                cnt_sc[0] += 1
                sph = (sb - 4) * 512
                sl = ci % 2

                def mms(e, r=r, k0=k0, nk=nk, i0=i0, nq=nq, sph=sph, hrow=hrow):
                    return e.matmul(PS[2][0:nk, sph:sph + nq], lhsT=kT[hrow, r * Lk + k0:r * Lk + k0 + nk], rhs=qT[hrow, r * Lq + i0:r * Lq + i0 + nq],
                                    start=True, stop=True)
                S.op("pe", mms, reads=[B_k, B_q], writes=[PSA[sb]])
                S.op("dve", (lambda nk, nq, doff, sph, sl, cneg: lambda e: e.scalar_tensor_tensor(
                    out=tt[sl][0:nk, 0:nq], in0=dtile[0:nk, doff:doff + nq], scalar=cneg, in1=PS[2][0:nk, sph:sph + nq], op0=ALU.mult, op1=ALU.add))(nk, nq, doff, sph, sl, cneg),
                    reads=[B_const, PSA[sb]], writes=[B_tt[sl]])
                S.op("act", (lambda nk, nq, sl: lambda e: e.activation(out=PTt[sl][0:nk, 0:nq], in_=tt[sl][0:nk, 0:nq], func=AF.Exp, scale=0.125))(nk, nq, sl),
                     reads=[B_tt[sl]], writes=[B_pt2[sl]])
                segs = []
                ia = i0
                while ia < i1:
                    bk = (ia * d + r) // 512
                    ib = min(i1, (512 * (bk + 1) - r + d - 1) // d)
                    segs.append((bk, ia, ib))
                    ia = ib
                vsel = (slice(0, 128) if hd == 0 else slice(64, 192))

                def mmv(e, ci=ci, nk=nk, segs=segs, i0=i0, r=r, sl=sl, vsel=vsel, st=list(started)):
                    for (bk, ia, ib) in segs:
                        c0 = ia * d + r - 512 * bk
                        n = ib - ia
                        pst = PS[bk // 2]
                        cb = (bk % 2) * 512
                        if d == 1:
                            oap = pst[:, cb + c0:cb + c0 + n]
                        else:
                            oap = pst[:, cb:cb + 512].rearrange("p (i r) -> p r i", r=d)[:, c0 % d, c0 // d:c0 // d + n]
                        ins = e.matmul(oap, lhsT=vaug[0:nk, ci, vsel], rhs=PTt[sl][0:nk, ia - i0:ib - i0], start=(not st[bk]), stop=True,
                                       skip_group_check=True)
                        st[bk] = True
                    return ins
                S.op("pe", mmv, reads=[B_vaug, B_pt2[sl]], writes=[PSA[bk] for (bk, _, _) in segs])
                for (bk, _, _) in segs:
                    started[bk] = True
            for qt in range(4):
                pst = PS[qt // 2]
                cb = (qt % 2) * 512
                if g == 0:
                    S.op("act", (lambda hd, qt, pst, cb: lambda e: e.activation(out=osum(hd, qt), in_=pst[:, cb:cb + 512], func=AF.Copy))(hd, qt, pst, cb),
                         reads=[PSA[qt]], writes=[B_os[hd][qt]])
                else:
                    S.op("dve", (lambda hd, qt, pst, cb: lambda e: e.tensor_tensor(out=osum(hd, qt), in0=osum(hd, qt), in1=pst[:, cb:cb + 512], op=ALU.add))(hd, qt, pst, cb),
                         reads=[PSA[qt], B_os[hd][qt]], writes=[B_os[hd][qt]])

    def attn_finish(hp):
        for hd in range(2):
            nrow = slice(0, 64) if hd == 0 else slice(64, 128)
            drow = slice(64, 128) if hd == 0 else slice(0, 64)
            for qt in range(4):
                S.op("act", (lambda hd, qt, nrow, drow: lambda e: e.activation(out=rden[nrow, :], in_=osum(hd, qt)[drow, :], func=AF.Copy))(hd, qt, nrow, drow),
                     reads=[B_os[hd][qt]], writes=[B_rd])
                S.op("dve", (lambda nrow: lambda e: e.reciprocal(out=rden[nrow, :], in_=rden[nrow, :]))(nrow), reads=[B_rd], writes=[B_rd])
                S.op("dve", (lambda hd, qt, nrow: lambda e: e.tensor_tensor(out=o_hp[nrow, qt * 512:(qt + 1) * 512], in0=osum(hd, qt)[nrow, :], in1=rden[nrow, :], op=ALU.mult))(hd, qt, nrow),
                     reads=[B_os[hd][qt], B_rd], writes=[B_ohp])
        S.dma("pool", (lambda hp: lambda e: e.dma_start(out=wao[:], in_=w_ao_d[hp]))(hp), "ao", writes=[B_wao])
        for dc in range(8):
            for qt in range(4):
                pbank = 4 + (cnt_sc[0] % 2)
                cnt_sc[0] += 1
                ph = (pbank - 4) * 512

                def mmo2(e, dc=dc, qt=qt, ph=ph):
                    return e.matmul(PS[2][:, ph:ph + 512], lhsT=wao[:, dc * 128:(dc + 1) * 128], rhs=o_hp[:, qt * 512:(qt + 1) * 512], start=True, stop=True)
                S.op("pe", mmo2, reads=[B_wao, B_ohp], writes=[PSA[pbank]])
                S.op("dve", (lambda dc, qt, ph: lambda e: e.scalar_tensor_tensor(out=xres[:, dc, qt * 512:(qt + 1) * 512], in0=PS[2][:, ph:ph + 512], scalar=1.0 / ALPHA,
                                                                                   in1=xres[:, dc, qt * 512:(qt + 1) * 512], op0=ALU.mult, op1=ALU.add))(dc, qt, ph),
                     reads=[PSA[pbank], XO[dc][qt]], writes=[XO[dc][qt]])
    for hp in range(8):
        for g, (win, d) in enumerate(DIL):
            attn_group(hp, g, win, d)
        attn_finish(hp)
    S.barrier()
    for t in range(2):
        t0, nt = TILES[t]
        ln_cols(1, 1, t0, nt, XR[t])
    S.barrier()
    if dbg == "l1attn":
        return dump_and_finish()
    for t in range(2):
        ffn_tile(1, 1, 2, t, XR[t])
    return dump_and_finish()


def _prep_shared(ffn1_w_gate, ffn1_w_up, ffn1_w_down, ffn2_w_gate, ffn2_w_up, ffn2_w_down,
                 ln_gain, ln_bias, pool_w_in, pool_w_group, pool_scale, pool_w_out, attn_w_qkv, attn_w_out):
    f = np.float32
    gates = [ffn1_w_gate, ffn2_w_gate]
    ups = [ffn1_w_up, ffn2_w_up]
    downs = [ffn1_w_down, ffn2_w_down]
    w_gu = np.empty((2, 2, NFC, 128, 2, 8, 128), f)
    w_d = np.empty((2, 2, 8, 128, NFC, 128), f)
    for l in range(2):
        for fi in range(2):
            g = np.asarray(gates[fi][l], f).reshape(8, 128, NFC, 128)
            u = np.asarray(ups[fi][l], f).reshape(8, 128, NFC, 128)
            w_gu[l, fi, :, :, 0] = g.transpose(2, 1, 0, 3)
            w_gu[l, fi, :, :, 1] = u.transpose(2, 1, 0, 3)
            dn = np.asarray(downs[fi][l], f).reshape(NFC, 128, 8, 128)
            w_d[l, fi] = dn.transpose(2, 1, 0, 3)
    lnp = np.empty((128, 2, 3, 2, 8), f)
    lnp[:, :, :, 0, :] = np.asarray(ln_gain, f).reshape(2, 3, 8, 128).transpose(3, 0, 1, 2)
    lnp[:, :, :, 1, :] = np.asarray(ln_bias, f).reshape(2, 3, 8, 128).transpose(3, 0, 1, 2)
    w_pin = np.asarray(pool_w_in[0], f).reshape(8, 128, 1024).transpose(1, 0, 2)
    w_pout = np.asarray(pool_w_out[0], f).reshape(8, 128, 1024).transpose(1, 0, 2)
    w_pgrp = np.asarray(pool_w_group[0], f).reshape(4, 2, 128, 256).transpose(2, 0, 1, 3)
    pscale = np.asarray(pool_scale[0], f).reshape(8, 128).T
    wq = np.asarray(attn_w_qkv[0], f).reshape(8, 128, 3, 3, 8, 128)
    w_qkv = wq.transpose(4, 2, 1, 3, 0, 5)
    w_ao = np.asarray(attn_w_out[0], f).reshape(8, 128, 1024)
    r = np.arange(128)[:, None]
    i = np.arange(256)[None, :]
    dd = np.abs(r - i + 64).astype(f)
    dtile = np.where(dd <= 64, dd, BIGD).astype(f)
    c = np.ascontiguousarray
    return {
        "lnp": c(lnp.reshape(128, 96)), "pscale": c(pscale), "dtile": c(dtile), "ident": np.eye(128, dtype=f),
        "w_gu": c(w_gu.reshape(88, 128, 2048)), "w_d": c(w_d.reshape(32, 128, 2816)),
        "w_pin": c(w_pin.reshape(128, 8192)), "w_pout": c(w_pout.reshape(128, 8192)), "w_pgrp": c(w_pgrp.reshape(128, 2048)),
        "w_qkv": c(w_qkv.reshape(24, 128, 3072)), "w_ao": c(w_ao),
    }


def _prep_core(x, core):
    f = np.float32
    b, half = core // 2, core % 2
    xs = np.asarray(x[b], f)
    if half == 0:
        loc = xs[0:TC]
    else:
        loc = xs[::-1][0:TC]
    xT = np.ascontiguousarray(loc.T.reshape(8, 128, TC).transpose(1, 0, 2))
    pflag = np.zeros((128, 2), f)
    pflag[:, half] = 1.0
    pinv = np.empty((128, 8, 8), f)
    for cidx in range(8):
        h = 1 << (cidx // 2)
        w = 2 * h
        for jo in range(8):
            if half == 0:
                cnt = h + min(h, jo)
            else:
                cnt = h + min(h, jo + 1)
            pinv[:, cidx, jo] = 1.0 / cnt
    return {"xT": xT, "pflag": pflag, "pinv": np.ascontiguousarray(pinv.reshape(128, 64))}


_NC_CACHE = {}


def _get_nc(dbg=None):
    if dbg not in _NC_CACHE:
        _NC_CACHE[dbg] = build_program(dbg)
    return _NC_CACHE[dbg]


def run_cores(inputs, dbg=None, cores=range(8), trace=False):
    x = np.asarray(inputs["x"], np.float32)
    shared = _prep_shared(**{k: v for k, v in inputs.items() if k != "x"})
    in_maps = []
    for core in cores:
        m = dict(shared)
        m.update(_prep_core(x, core))
        in_maps.append(m)
    nc = _get_nc(dbg)
    res = run_bass_kernel_spmd(nc, in_maps, core_ids=list(range(len(in_maps))), **({"trace": True} if trace else {}))
    return res


def kernel(**inputs):
    res = run_cores(inputs)
    out = np.empty((BATCH, SEQ, D_MODEL), np.float32)
    for core in range(8):
        yT = np.asarray(res.results[core]["yT"], np.float32)
        y = yT.transpose(2, 1, 0).reshape(OWN, D_MODEL)
        b, half = core // 2, core % 2
        if half == 0:
            out[b, 0:OWN] = y
        else:
            out[b, OWN:SEQ] = y[::-1]
    return out
```

```python
import numpy as np
import concourse.bass as bass
import concourse.mybir as mybir
from concourse.bass_utils import run_bass_kernel_spmd

F32 = mybir.dt.float32
BF16 = mybir.dt.bfloat16
AF = mybir.ActivationFunctionType
ALU = mybir.AluOpType

D_MODEL = 1024
SEQ = 4096
BATCH = 4
D_FF = 2816
NFC = 22
OWN = 2048
HALO = 1024
TC = OWN + HALO + 8
TH = OWN + HALO
ALPHA = 4.0 ** 0.25
LN_EPS = 1e-5
EPS_P = LN_EPS / (ALPHA * ALPHA)
DIL = ((128, 1), (512, 4), (2048, 16))
BIGD = 1.0e5

ENGS = ("pe", "act", "dve", "pool", "sp")


class Buf:
    __slots__ = ("name", "wdep", "rdeps")

    def __init__(self, name):
        self.name = name
        self.wdep = None
        self.rdeps = {}


class Sched:
    def __init__(self, nc):
        self.nc = nc
        self.sem = {}
        self.cnt = {}
        for e in ENGS:
            self.sem[e] = nc.alloc_semaphore("s_" + e)
            self.cnt[e] = 0
        self.seen = {e: {} for e in ENGS}
        self.ops = {e: [] for e in ENGS}

    def new_dma_sem(self, key):
        self.sem[key] = self.nc.alloc_semaphore("d_" + key)
        self.cnt[key] = 0
        return key

    def _collect(self, reads, writes):
        w = {}
        for b in reads:
            d = b.wdep
            if d is not None and w.get(d[0], 0) < d[1]:
                w[d[0]] = d[1]
        for b in writes:
            d = b.wdep
            if d is not None and w.get(d[0], 0) < d[1]:
                w[d[0]] = d[1]
            for k, v in b.rdeps.items():
                if w.get(k, 0) < v:
                    w[k] = v
        return w

    def _need(self, e, w):
        need = []
        s = self.seen[e]
        for k, v in w.items():
            if s.get(k, 0) < v:
                need.append((k, v))
                s[k] = v
        return need

    def op(self, e, fn, reads=(), writes=()):
        need = self._need(e, self._collect(reads, writes))
        self.cnt[e] += 1
        v = self.cnt[e]
        self.ops[e].append((need, fn, (e, 1)))
        for b in reads:
            if b.rdeps.get(e, 0) < v:
                b.rdeps[e] = v
        for b in writes:
            b.wdep = (e, v)
            b.rdeps = {}
        return (e, v)

    def dma(self, q, fn, semkey, reads=(), writes=()):
        need = self._need(q, self._collect(reads, writes))
        self.cnt[semkey] += 16
        v = self.cnt[semkey]
        self.ops[q].append((need, fn, (semkey, 16)))
        for b in reads:
            if b.rdeps.get(semkey, 0) < v:
                b.rdeps[semkey] = v
        for b in writes:
            b.wdep = (semkey, v)
            b.rdeps = {}
        return (semkey, v)

    def wait_all(self, e, keys):
        w = {k: self.cnt[k] for k in keys if self.cnt[k] > 0}
        need = self._need(e, w)
        if need:
            self.ops[e].append((need, None, None))

    def barrier(self, engines=("pe", "act", "dve", "pool")):
        for e in engines:
            self.wait_all(e, list(engines))

    def emit(self, block):
        sched = self

        def make(e):
            def body(eng):
                for need, fn, inc in sched.ops[e]:
                    for k, v in need:
                        eng.wait_ge(sched.sem[k], v)
                    if fn is not None:
                        ins = fn(eng)
                        ins.then_inc(sched.sem[inc[0]], inc[1])
            return body
        block.tensor(make("pe"))
        block.scalar(make("act"))
        block.vector(make("dve"))
        block.gpsimd(make("pool"))
        block.sync(make("sp"))


def _halves(nt):
    return [(h0, min(512, nt - h0)) for h0 in range(0, nt, 512)]


def build_program(dbg=None):
    nc = bass.Bass("TRN2", target_bir_lowering=False)
    S = Sched(nc)

    def din(name, shape):
        return nc.dram_tensor(name, list(shape), F32, kind="ExternalInput").ap()

    xT_d = din("xT", (128, 8, TC))
    lnp_d = din("lnp", (128, 96))
    pflag_d = din("pflag", (128, 2))
    pinv_d = din("pinv", (128, 64))
    pscale_d = din("pscale", (128, 8))
    dtile_d = din("dtile", (128, 512))
    ident_d = din("ident", (128, 128))
    w_gu_d = din("w_gu", (88, 128, 2048))
    w_d_d = din("w_d", (32, 128, 2816))
    w_pin_d = din("w_pin", (128, 8192))
    w_pout_d = din("w_pout", (128, 8192))
    w_pgrp_d = din("w_pgrp", (128, 2048))
    w_qkv_d = din("w_qkv", (24, 128, 3072))
    w_ao_d = din("w_ao", (8, 128, 1024))
    if dbg is None:
        y_d = nc.dram_tensor("yT", [128, 8, OWN], F32, kind="ExternalOutput").ap()
    else:
        y_d = nc.dram_tensor("yT", [128, 8, TC], F32, kind="ExternalOutput").ap()

    base = (nc.sbuf_base + 31) // 32 * 32
    OFF_X = 0
    OFF_C = 98560
    OFF_R1 = OFF_C + 4096
    OFF_RING = OFF_R1 + 61440
    OFF_LN = OFF_RING + 23552
    OFF_SP = OFF_LN + 16384
    ARENA = OFF_SP + 4608
    arena = nc.alloc_sbuf_tensor("arena", [128, ARENA // 4], F32)
    abase = base
    assert nc.sbuf_base >= abase + ARENA

    def at(name, shape, dtype, off):
        return nc.alloc_sbuf_tensor_at(name, list(shape), dtype, offset=abase + off)

    xres = at("xres", (128, 8, TC), F32, OFF_X)
    lnp = at("lnp", (128, 96), F32, OFF_C)
    pflag = at("pflag", (128, 2), F32, OFF_C + 384)
    pinv = at("pinv", (128, 64), F32, OFF_C + 416)
    pscale = at("pscale", (128, 8), F32, OFF_C + 672)
    dtile = at("dtile", (128, 512), F32, OFF_C + 704)
    ones_bf = at("ones_bf", (128, 128), BF16, OFF_C + 2752)
    ident_bf = at("ident_bf", (128, 128), BF16, OFF_C + 3008)
    hsave = at("hsave", (128, 8, 8), F32, OFF_C + 3264)
    epst = at("epst", (128, 1), F32, OFF_C + 3520)
    hbuf = at("hbuf", (128, NFC, 1024), BF16, OFF_R1)
    xb = at("xb", (128, 8, 1024), BF16, OFF_R1 + 45056)
    gu_slot = [at(f"gu{i}", (128, 2, 8, 128), BF16, OFF_RING + i * 4096) for i in range(3)]
    d_slot = [at(f"dw{i}", (128, NFC, 128), BF16, OFF_RING + 12288 + i * 5632) for i in range(2)]
    zb = at("zb", (128, 2, 1024), BF16, OFF_LN)
    zq = at("zq", (128, 2, 1024), BF16, OFF_LN + 4096)
    meant = at("meant", (128, 1024), F32, OFF_LN + 8192)
    rstdt = at("rstdt", (128, 1024), F32, OFF_LN + 12288)
    sgt = at("sgt", (128, 1024), F32, OFF_SP)
    p_win = at("p_win", (128, 8, 1024), BF16, OFF_R1)
    p_wout = at("p_wout", (128, 8, 1024), BF16, OFF_R1 + 16384)
    p_wgrp = at("p_wgrp", (128, 4, 2, 256), BF16, OFF_R1 + 32768)
    p_xb = at("p_xb", (128, 8, 512), BF16, OFF_R1 + 36864)
    p_yb = at("p_yb", (128, 8, 512), BF16, OFF_R1 + 45056)
    p_mx = at("p_mx", (128, 8, 512), BF16, OFF_R1 + 53248)
    p_u = at("p_u", (128, 8, 528), F32, OFF_RING)
    p_t = [at(f"p_t{i}", (128, 528), F32, OFF_RING + 16896 + i * 2112) for i in range(2)]
    p_wn = at("p_wn", (128, 512), F32, OFF_RING + 21120)
    x1b = at("x1b", (128, 8, TH), BF16, OFF_R1)
    qT = at("qT", (128, OWN), BF16, OFF_R1 + 49152)
    kT = at("kT", (128, TH), BF16, OFF_R1 + 53248)
    PTt = [at(f"PT{i}", (128, 512), BF16, OFF_RING + 20480 + i * 1024) for i in range(3)]
    wqkv2 = [at(f"wqkv{i}", (128, 3, 8, 128), BF16, OFF_RING + i * 6144) for i in range(2)]
    vT = at("vT", (128, TH), BF16, OFF_RING + 12288)
    wao = at("wao", (128, 1024), BF16, OFF_RING + 18432)
    vaug = at("vaug", (128, 32, 192), BF16, OFF_LN)
    o_hp = at("o_hp", (128, OWN), BF16, OFF_LN + 12288)
    tt = [at(f"tt{i}", (128, 512), F32, OFF_SP + i * 2048) for i in range(2)]

    PSLO = nc.alloc_psum_tensor("pslo", [128, 2048], F32)
    PSHI = nc.alloc_psum_tensor("pshi", [128, 2048], F32)
    PS = [PSLO[:, 0:1024], PSLO[:, 1024:2048], PSHI[:, 0:1024], PSHI[:, 1024:2048]]
    PSB = [Buf(f"ps{i}") for i in range(4)]

    for k in ["x0", "x1", "x2", "x3", "const", "gu0", "gu1", "gu2", "dw0", "dw1", "pw", "aw0", "aw1", "ao", "out"]:
        S.new_dma_sem(k)

    B_const = Buf("const")
    B_ident = Buf("ident")
    B_ones = Buf("ones")
    TILES = [(0, 1024), (1024, 1024), (2048, 1024), (3072, 8)]
    XR = [[Buf(f"xr{t}_{c}") for c in range(8)] for t in range(4)]
    XB = [Buf(f"xb{c}") for c in range(8)]
    HB = [Buf(f"hb{f}") for f in range(NFC)]
    GU = [Buf(f"gu{i}") for i in range(3)]
    DW = [Buf(f"dw{i}") for i in range(2)]
    B_sg = Buf("sg")
    B_zb = [Buf("zb0"), Buf("zb1")]
    B_zq = [Buf("zq0"), Buf("zq1")]
    B_mean = Buf("mean")
    B_rstd = Buf("rstd")
    ring = {"gu": 0, "dw": 0}

    for (dst, src) in [(lnp, lnp_d), (pflag, pflag_d), (pinv, pinv_d), (pscale, pscale_d), (dtile, dtile_d)]:
        S.dma("sp", (lambda dst, src: lambda e: e.dma_start(out=dst[:], in_=src))(dst, src), "const", writes=[B_const])
    B_const.wdep = ("const", S.cnt["const"])
    S.dma("pool", lambda e: e.dma_start(out=ident_bf[:], in_=ident_d), "pw", writes=[B_ident])
    S.op("dve", lambda e: e.memset(ones_bf[:], 1.0), writes=[B_ones])
    S.op("dve", lambda e: e.memset(epst[:], EPS_P), writes=[B_const])
    B_const.wdep = ("const", S.cnt["const"])
    B_eps = Buf("eps")
    B_eps.wdep = ("dve", S.cnt["dve"])
    for t, (t0, nt) in enumerate(TILES):
        S.dma("sp", (lambda t0, nt: lambda e: e.dma_start(out=xres[:, :, t0:t0 + nt], in_=xT_d[:, :, t0:t0 + nt]))(t0, nt),
              f"x{t}", writes=XR[t])

    def ln_cols(l, s, t0, nt, xr_bufs):
        hv = _halves(nt)
        for c in range(8):
            sl = c % 2
            S.op("act", (lambda c, sl: lambda e: e.activation(out=zb[:, sl, 0:nt], in_=xres[:, c, t0:t0 + nt], func=AF.Copy))(c, sl),
                 reads=[xr_bufs[c]], writes=[B_zb[sl]])
            S.op("act", (lambda c, sl: lambda e: e.activation(out=zq[:, sl, 0:nt], in_=xres[:, c, t0:t0 + nt], func=AF.Square))(c, sl),
                 reads=[xr_bufs[c]], writes=[B_zq[sl]])

            def mm(e, c=c, sl=sl):
                for (h0, hn) in hv:
                    e.matmul(PS[0][:, h0:h0 + hn], lhsT=ones_bf[:, :], rhs=zb[:, sl, h0:h0 + hn], start=(c == 0), stop=(c == 7))
                for (h0, hn) in hv:
                    ins = e.matmul(PS[1][:, h0:h0 + hn], lhsT=ones_bf[:, :], rhs=zq[:, sl, h0:h0 + hn], start=(c == 0), stop=(c == 7))
                return ins
            S.op("pe", mm, reads=[B_zb[sl], B_zq[sl], B_ones], writes=[PSB[0], PSB[1]])
        S.op("dve", lambda e: e.tensor_scalar(out=meant[:, 0:nt], in0=PS[0][:, 0:nt], scalar1=1.0 / D_MODEL, scalar2=None, op0=ALU.mult),
             reads=[PSB[0]], writes=[B_mean])
        S.op("dve", lambda e: e.tensor_tensor(out=rstdt[:, 0:nt], in0=meant[:, 0:nt], in1=meant[:, 0:nt], op=ALU.mult),
             reads=[B_mean], writes=[B_rstd])
        S.op("dve", lambda e: e.scalar_tensor_tensor(out=rstdt[:, 0:nt], in0=PS[1][:, 0:nt], scalar=1.0 / D_MODEL, in1=rstdt[:, 0:nt],
                                                     op0=ALU.mult, op1=ALU.subtract),
             reads=[PSB[1], B_rstd], writes=[B_rstd])
        S.op("act", lambda e: e.activation(out=rstdt[:, 0:nt], in_=rstdt[:, 0:nt], func=AF.Sqrt, bias=epst[:, 0:1], scale=1.0),
             reads=[B_rstd, B_eps], writes=[B_rstd])
        S.op("dve", lambda e: e.reciprocal(out=rstdt[:, 0:nt], in_=rstdt[:, 0:nt]), reads=[B_rstd], writes=[B_rstd])
        gi = (l * 3 + s) * 16
        for c in range(8):
            xs = (lambda c: xres[:, c, t0:t0 + nt])
            S.op("dve", (lambda c: lambda e: e.tensor_tensor(out=xres[:, c, t0:t0 + nt], in0=xres[:, c, t0:t0 + nt], in1=meant[:, 0:nt], op=ALU.subtract))(c),
                 reads=[xr_bufs[c], B_mean], writes=[xr_bufs[c]])
            S.op("dve", (lambda c: lambda e: e.tensor_tensor(out=xres[:, c, t0:t0 + nt], in0=xres[:, c, t0:t0 + nt], in1=rstdt[:, 0:nt], op=ALU.mult))(c),
                 reads=[xr_bufs[c], B_rstd], writes=[xr_bufs[c]])
            S.op("act", (lambda c: lambda e: e.activation(out=xres[:, c, t0:t0 + nt], in_=xres[:, c, t0:t0 + nt], func=AF.Identity,
                                                          bias=lnp[:, gi + 8 + c:gi + 9 + c], scale=lnp[:, gi + c:gi + c + 1]))(c),
                 reads=[xr_bufs[c], B_const], writes=[xr_bufs[c]])

    def ffn_tile(l, fi, s, t, xr_bufs):
        t0, nt = TILES[t]
        hv = _halves(nt)
        for c in range(8):
            eng = "dve" if c % 2 == 0 else "act"
            if eng == "dve":
                S.op("dve", (lambda c: lambda e: e.tensor_copy(out=xb[:, c, 0:nt], in_=xres[:, c, t0:t0 + nt]))(c), reads=[xr_bufs[c]], writes=[XB[c]])
            else:
                S.op("act", (lambda c: lambda e: e.activation(out=xb[:, c, 0:nt], in_=xres[:, c, t0:t0 + nt], func=AF.Copy))(c), reads=[xr_bufs[c]], writes=[XB[c]])
        wbase = (l * 2 + fi) * NFC
        for fc in range(NFC):
            sl = ring["gu"] % 3
            ring["gu"] += 1
            S.dma("pool", (lambda sl, idx: lambda e: e.dma_start(out=gu_slot[sl][:], in_=w_gu_d[idx].rearrange("p (a k f) -> p a k f", a=2, k=8)))(sl, wbase + fc),
                  f"gu{sl}", writes=[GU[sl]])
            pg, pu = (0, 1) if fc % 2 == 0 else (2, 3)

            def mm(e, sl=sl, pg=pg, pu=pu):
                for (pp, a) in ((pg, 0), (pu, 1)):
                    for (h0, hn) in hv:
                        for kc in range(8):
                            ins = e.matmul(PS[pp][:, h0:h0 + hn], lhsT=gu_slot[sl][:, a, kc, :], rhs=xb[:, kc, h0:h0 + hn], start=(kc == 0), stop=(kc == 7))
                return ins
            S.op("pe", mm, reads=[GU[sl]] + XB, writes=[PSB[pg], PSB[pu]])
            S.op("act", (lambda pg: lambda e: e.activation(out=sgt[:, 0:nt], in_=PS[pg][:, 0:nt], func=AF.Silu))(pg), reads=[PSB[pg]], writes=[B_sg])
            S.op("dve", (lambda pu, fc: lambda e: e.tensor_tensor(out=hbuf[:, fc, 0:nt], in0=sgt[:, 0:nt], in1=PS[pu][:, 0:nt], op=ALU.mult))(pu, fc),
                 reads=[B_sg, PSB[pu]], writes=[HB[fc]])
        dbase = (l * 2 + fi) * 8
        for dc in range(8):
            sl = ring["dw"] % 2
            ring["dw"] += 1
            S.dma("pool", (lambda sl, idx: lambda e: e.dma_start(out=d_slot[sl][:], in_=w_d_d[idx].rearrange("p (f d) -> p f d", f=NFC)))(sl, dbase + dc),
                  f"dw{sl}", writes=[DW[sl]])
            py = dc % 4

            def mm2(e, sl=sl, py=py):
                for (h0, hn) in hv:
                    for fc in range(NFC):
                        ins = e.matmul(PS[py][:, h0:h0 + hn], lhsT=d_slot[sl][:, fc, :], rhs=hbuf[:, fc, h0:h0 + hn], start=(fc == 0), stop=(fc == NFC - 1))
                return ins
            S.op("pe", mm2, reads=[DW[sl]] + HB, writes=[PSB[py]])
            S.op("dve", (lambda dc, py: lambda e: e.scalar_tensor_tensor(out=xres[:, dc, t0:t0 + nt], in0=PS[py][:, 0:nt], scalar=0.5 / ALPHA,
                                                                           in1=xres[:, dc, t0:t0 + nt], op0=ALU.mult, op1=ALU.add))(dc, py),
                 reads=[PSB[py], xr_bufs[dc]], writes=[xr_bufs[dc]])
        ln_cols(l, s, t0, nt, xr_bufs)

    def dump_and_finish():
        ncols = OWN if dbg is None else TC
        S.barrier()
        S.wait_all("sp", ["pe", "act", "dve"])
        for c in range(8):
            S.dma("sp", (lambda c: lambda e: e.dma_start(out=y_d[:, c, :], in_=xres[:, c, 0:ncols]))(c), "out",
                  reads=[b for t in range(4) for b in XR[t]])
        S.wait_all("sp", ["out"])
        S.wait_all("pool", ["gu0", "gu1", "gu2", "dw0", "dw1", "pw", "aw0", "aw1", "ao"])
        with nc.Block() as block:
            S.emit(block)
        return nc

    for t in range(4):
        ffn_tile(0, 0, 0, t, XR[t])
    if dbg == "l0ffn1":
        return dump_and_finish()

    S.barrier()
    ALLX = [b for t in range(4) for b in XR[t]]
    B_pw = Buf("pw")
    S.dma("pool", lambda e: e.dma_start(out=p_win[:], in_=w_pin_d.rearrange("p (k e) -> p k e", k=8)), "pw", writes=[B_pw])
    S.dma("pool", lambda e: e.dma_start(out=p_wout[:], in_=w_pout_d.rearrange("p (k e) -> p k e", k=8)), "pw", writes=[B_pw])
    S.dma("pool", lambda e: e.dma_start(out=p_wgrp[:], in_=w_pgrp_d.rearrange("p (g c e) -> p g c e", g=4, c=2)), "pw", writes=[B_pw])
    B_pw.wdep = ("pw", S.cnt["pw"])
    B_pxb = [Buf(f"pxb{c}") for c in range(8)]
    B_pu = [Buf(f"pu{c}") for c in range(8)]
    B_pmx = [Buf(f"pmx{c}") for c in range(8)]
    B_pyb = [Buf(f"pyb{c}") for c in range(8)]
    B_pt = [Buf("pt0"), Buf("pt1")]
    B_pwn = Buf("pwn")
    B_hs = Buf("hsave")
    S.op("dve", lambda e: e.memset(p_u[:, :, 0:8], 0.0), writes=B_pu)
    TP = 448
    ptiles = [(a, min(a + TP, TH)) for a in range(0, TH, TP)]
    XP = [Buf(f"xp{c}") for c in range(8)]
    def pool_tile(ti, a, b):
        n_o = b - a
        first = (ti == 0)
        ua = 0 if first else a - 8
        ub = b + 8
        n_u = ub - ua
        loff = 8 if first else 0
        for c in range(8):
            if first:
                S.op("act", (lambda c: lambda e: e.activation(out=p_xb[:, c, 0:n_u], in_=xres[:, c, ua:ub], func=AF.Copy))(c), reads=[XP[c]], writes=[B_pxb[c]])
            else:
                S.op("act", (lambda c: lambda e: e.activation(out=p_xb[:, c, 0:8], in_=hsave[:, c, :], func=AF.Copy))(c), reads=[B_hs], writes=[B_pxb[c]])
                S.op("act", (lambda c: lambda e: e.activation(out=p_xb[:, c, 8:n_u], in_=xres[:, c, a:ub], func=AF.Copy))(c), reads=[XP[c]], writes=[B_pxb[c]])
        S.op("dve", lambda e: e.tensor_copy(out=hsave[:, :, :], in_=xres[:, :, b - 8:b]), reads=XP + B_pxb, writes=[B_hs])
        for c in range(8):
            pp = c % 4

            def mmu(e, c=c, pp=pp):
                for kc in range(8):
                    ins = e.matmul(PS[pp][:, 0:n_u], lhsT=p_win[:, kc, c * 128:(c + 1) * 128], rhs=p_xb[:, kc, 0:n_u], start=(kc == 0), stop=(kc == 7))
                return ins
            S.op("pe", mmu, reads=[B_pw] + B_pxb, writes=[PSB[pp]])
            S.op("act", (lambda c, pp: lambda e: e.activation(out=p_u[:, c, loff:loff + n_u], in_=PS[pp][:, 0:n_u], func=AF.Copy))(c, pp),
                 reads=[PSB[pp]], writes=[B_pu[c]])
        n_loc = n_u + loff
        lo0 = 8
        for c in range(8):
            g = c // 2
            h = 1 << g
            w = 2 * h
            src = p_u[:, c, :]
            srcb = B_pu[c]
            ln = 1
            k = 0
            while ln < w:
                dst = p_t[k % 2]
                cnt = n_loc - 2 * ln + 1
                S.op("dve", (lambda src, dst, ln, cnt: lambda e: e.tensor_tensor(out=dst[:, 0:cnt], in0=src[:, 0:cnt], in1=src[:, ln:ln + cnt], op=ALU.add))(src, dst, ln, cnt),
                     reads=[srcb], writes=[B_pt[k % 2]])
                src = dst
                srcb = B_pt[k % 2]
                ln *= 2
                k += 1
            s0 = lo0 - h
            S.op("dve", (lambda src, s0: lambda e: e.tensor_scalar(out=p_wn[:, 0:n_o], in0=src[:, s0 + 1:s0 + 1 + n_o], scalar1=pflag[:, 1:2], scalar2=None, op0=ALU.mult))(src, s0),
                 reads=[srcb, B_const], writes=[B_pwn])
            S.op("dve", (lambda src, s0: lambda e: e.scalar_tensor_tensor(out=p_wn[:, 0:n_o], in0=src[:, s0:s0 + n_o], scalar=pflag[:, 0:1], in1=p_wn[:, 0:n_o],
                                                                            op0=ALU.mult, op1=ALU.add))(src, s0),
                 reads=[srcb, B_const, B_pwn], writes=[B_pwn])
            S.op("dve", (lambda c, w: lambda e: e.scalar_tensor_tensor(out=p_mx[:, c, 0:n_o], in0=p_wn[:, 0:n_o], scalar=1.0 / w, in1=p_u[:, c, lo0:lo0 + n_o],
                                                                         op0=ALU.mult, op1=ALU.subtract))(c, w),
                 reads=[B_pwn, B_pu[c]], writes=[B_pmx[c]])
            if first:
                S.op("dve", (lambda c: lambda e: e.tensor_tensor(out=p_wn[:, 0:8], in0=p_wn[:, 0:8], in1=pinv[:, c * 8:c * 8 + 8], op=ALU.mult))(c),
                     reads=[B_pwn, B_const], writes=[B_pwn])
                S.op("dve", (lambda c: lambda e: e.tensor_tensor(out=p_mx[:, c, 0:8], in0=p_wn[:, 0:8], in1=p_u[:, c, lo0:lo0 + 8], op=ALU.subtract))(c),
                     reads=[B_pwn, B_pu[c]], writes=[B_pmx[c]])
        for ec in range(8):
            g = ec // 2
            eh = ec % 2
            pp = ec % 4

            def mmg(e, g=g, eh=eh, pp=pp):
                for cc in range(2):
                    ins = e.matmul(PS[pp][:, 0:n_o], lhsT=p_wgrp[:, g, cc, eh * 128:(eh + 1) * 128], rhs=p_mx[:, 2 * g + cc, 0:n_o], start=(cc == 0), stop=(cc == 1))
                return ins
            S.op("pe", mmg, reads=[B_pw, B_pmx[2 * g], B_pmx[2 * g + 1]], writes=[PSB[pp]])
            S.op("act", (lambda ec, pp: lambda e: e.activation(out=p_yb[:, ec, 0:n_o], in_=PS[pp][:, 0:n_o], func=AF.Identity, scale=pscale[:, ec:ec + 1]))(ec, pp),
                 reads=[PSB[pp], B_const], writes=[B_pyb[ec]])
        if dbg == "pooldbg" and ti == 0:
            S.barrier()
            Bd = Buf("dbgout")
            S.dma("pool", lambda e: e.dma_start(out=y_d[:, :, 0:528], in_=p_u[:, :, :]), "out", reads=B_pu)
            S.dma("pool", lambda e: e.dma_start(out=y_d[:, :, 600:1048], in_=p_mx[:, :, 0:448]), "out", reads=B_pmx)
            S.dma("pool", lambda e: e.dma_start(out=y_d[:, :, 1100:1548], in_=p_yb[:, :, 0:448]), "out", reads=B_pyb)
            S.dma("pool", lambda e: e.dma_start(out=y_d[:, :, 1600:2056], in_=p_xb[:, :, 0:456]), "out", reads=B_pxb)
            S.dma("pool", lambda e: e.dma_start(out=y_d[:, :, 2056:3080], in_=p_wout[:, :, :]), "out", reads=[B_pw])
            S.wait_all("pool", ["out"])
            with nc.Block() as block:
                S.emit(block)
            return nc
        for dc in range(8):
            pp = dc % 4

            def mmo(e, dc=dc, pp=pp):
                for ec in range(8):
                    ins = e.matmul(PS[pp][:, 0:n_o], lhsT=p_wout[:, ec, dc * 128:(dc + 1) * 128], rhs=p_yb[:, ec, 0:n_o], start=(ec == 0), stop=(ec == 7))
                return ins
            S.op("pe", mmo, reads=[B_pw] + B_pyb, writes=[PSB[pp]])
            S.op("dve", (lambda dc, pp: lambda e: e.scalar_tensor_tensor(out=xres[:, dc, a:b], in0=PS[pp][:, 0:n_o], scalar=1.0 / ALPHA, in1=xres[:, dc, a:b],
                                                                           op0=ALU.mult, op1=ALU.add))(dc, pp),
                 reads=[PSB[pp], XP[dc], B_hs], writes=[XP[dc]])
        if dbg == "pooldbg2" and ti == 0:
            S.barrier()
            for c in range(8):
                S.dma("sp", (lambda c: lambda e: e.dma_start(out=y_d[:, c, 0:448], in_=xres[:, c, 0:448]))(c), "out", reads=XP)
            S.wait_all("sp", ["out"])
            with nc.Block() as block:
                S.emit(block)
            return nc
        ln_cols(0, 1, a, n_o, XP)
        if dbg is not None and dbg.startswith("pooldbg3:") and ti == int(dbg.split(":")[1]):
            S.barrier()
            for c in range(8):
                S.dma("sp", (lambda c: lambda e: e.dma_start(out=y_d[:, c, :], in_=xres[:, c, :]))(c), "out", reads=XP)
            S.wait_all("sp", ["out"])
            with nc.Block() as block:
                S.emit(block)
            return nc
        return None
    for ti, (a, b) in enumerate(ptiles):
        _r = pool_tile(ti, a, b)
        if _r is not None:
            return _r
    S.barrier()
    if dbg == "l0pool":
        return dump_and_finish()

    for t in range(3):
        ffn_tile(0, 1, 2, t, XR[t])
    if dbg == "l0":
        return dump_and_finish()
    for t in range(3):
        ffn_tile(1, 0, 0, t, XR[t])
    S.barrier()
    if dbg == "l1ffn1":
        return dump_and_finish()

    B_x1b = [Buf(f"x1b{c}") for c in range(8)]
    for c in range(8):
        if c % 2 == 0:
            S.op("dve", (lambda c: lambda e: e.tensor_copy(out=x1b[:, c, :], in_=xres[:, c, 0:TH]))(c), reads=ALLX, writes=[B_x1b[c]])
        else:
            S.op("act", (lambda c: lambda e: e.activation(out=x1b[:, c, :], in_=xres[:, c, 0:TH], func=AF.Copy))(c), reads=ALLX, writes=[B_x1b[c]])
    S.barrier()
    B_vaug = Buf("vaug")
    S.op("dve", lambda e: e.memset(vaug[:, :, 64:128], 1.0), writes=[B_vaug])
    B_wq = [Buf("wqkv0"), Buf("wqkv1")]
    B_q = Buf("qT")
    B_k = Buf("kT")
    B_v = Buf("vT")
    B_pt2 = [Buf("PT0"), Buf("PT1"), Buf("PT2")]
    B_tt = [Buf("tt0"), Buf("tt1")]
    B_os = [[Buf(f"os{h}_{q}") for q in range(4)] for h in range(2)]
    B_rd = Buf("rden")
    B_ohp = Buf("ohp")
    B_wao = Buf("wao")
    XO = [[Buf(f"xo{dc}_{q}") for q in range(4)] for dc in range(8)]
    PSA = [Buf(f"psa{i}") for i in range(8)]
    rden = xres[:, 0, OWN + 512:OWN + 1024]
    psT = PSHI[:, :].bitcast(BF16)

    def osum(hd, qt):
        return xres[:, hd * 4 + qt, OWN:OWN + 512]

    slopes = 2.0 ** (-8.0 * np.arange(1, 49) / 48.0)
    slopes = slopes.reshape(3, 16)
    rot = {"wk": 0, "tt": 0, "pt": 0, "wq": 0}

    def wk_slot():
        b = rot["wk"] % 4
        rot["wk"] += 1
        return b

    def attn_group(hp, g, win, d):
        Lq = OWN // d
        Lk = Lq + 64
        ntok_k = Lk * d
        ws = rot["wq"] % 2
        rot["wq"] += 1
        wq = wqkv2[ws]
        S.dma("pool", (lambda idx, wq: lambda e: e.dma_start(out=wq[:], in_=w_qkv_d[idx].rearrange("p (m k f) -> p m k f", m=3, k=8)))(hp * 3 + g, wq),
              f"aw{ws}", writes=[B_wq[ws]])
        ev = 0
        for m, (dstT, dstB, ntok) in enumerate(((qT, B_q, OWN), (kT, B_k, ntok_k), (vT, B_v, ntok_k))):
            dview = dstT[:, 0:ntok].rearrange("p (r i) -> p r i", r=d)
            for j0 in range(0, ntok, 512):
                n = min(512, ntok - j0)
                bk = wk_slot()
                ph = bk * 512

                def mmp(e, m=m, j0=j0, n=n, ph=ph):
                    for kc in range(8):
                        ins = e.matmul(PSHI[:, ph:ph + n], lhsT=wq[:, m, kc, :], rhs=x1b[:, kc, j0:j0 + n], start=(kc == 0), stop=(kc == 7))
                    return ins
                S.op("pe", mmp, reads=[B_wq[ws]] + B_x1b, writes=[PSA[4 + bk]])
                i0 = j0 // d
                ni = n // d
                src = PSHI[:, ph:ph + n].rearrange("p (i r) -> p r i", r=d)
                if ev % 2 == 0:
                    S.op("act", (lambda dview, src, i0, ni: lambda e: e.activation(out=dview[:, :, i0:i0 + ni], in_=src, func=AF.Copy))(dview, src, i0, ni),
                         reads=[PSA[4 + bk]], writes=[dstB])
                else:
                    S.op("dve", (lambda dview, src, i0, ni: lambda e: e.tensor_copy(out=dview[:, :, i0:i0 + ni], in_=src))(dview, src, i0, ni),
                         reads=[PSA[4 + bk]], writes=[dstB])
                ev += 1
        chunks = []
        for r in range(d):
            for k0 in range(0, Lk, 128):
                chunks.append((r, k0, min(128, Lk - k0)))
        for c0 in range(0, len(chunks), 4):
            grp = chunks[c0:c0 + 4]
            ng = len(grp)
            bk = wk_slot()
            pbase = bk * 1024

            def tr(e, grp=grp, pbase=pbase):
                for q, (r, k0, nk) in enumerate(grp):
                    ins = e.transpose(psT[0:nk, pbase + q * 128:pbase + (q + 1) * 128], vT[:, r * Lk + k0:r * Lk + k0 + nk], ident_bf[:, :])
                return ins
            S.op("pe", tr, reads=[B_v, B_ident], writes=[PSA[4 + bk]])
            srcv = psT[:, pbase:pbase + ng * 128].rearrange("p (q f) -> p q f", f=128)
            S.op("act", (lambda c0, ng, srcv: lambda e: e.activation(out=vaug[:, c0:c0 + ng, 0:64], in_=srcv[:, :, 0:64], func=AF.Copy))(c0, ng, srcv),
                 reads=[PSA[4 + bk]], writes=[B_vaug])
            S.op("dve", (lambda c0, ng, srcv: lambda e: e.tensor_copy(out=vaug[:, c0:c0 + ng, 128:192], in_=srcv[:, :, 64:128]))(c0, ng, srcv),
                 reads=[PSA[4 + bk]], writes=[B_vaug])
        items = []
        for hd in range(2):
            work = []
            for ci, (r, k0, nk) in enumerate(chunks):
                i0 = max(0, k0 - 64)
                i1 = min(Lq, k0 + nk + 64)
                if i1 > i0:
                    work.append((ci, r, k0, nk, i0, i1))
            npair = (len(work) + 1) // 2
            for pi in range(npair):
                items.append({"hd": hd, "pair": work[2 * pi:2 * pi + 2], "last": pi == npair - 1})
        started = {0: [False] * 4, 1: [False] * 4}
        accv = PSLO[:, :].rearrange("p (r i) -> p i r", r=d)

        def emit_scores(it):
            hd = it["hd"]
            pair = it["pair"]
            hg = 2 * hp + hd
            cneg = -float(slopes[g, hg]) * d * 8.0
            hrow = slice(64 * hd, 64 * hd + 64)
            bk = wk_slot()
            ph = bk * 512

            def mms(e, pair=pair, ph=ph, hrow=hrow):
                for q, (ci, r, k0, nk, i0, i1) in enumerate(pair):
                    doff = i0 - (k0 - 64)
                    nq = i1 - i0
                    ins = e.matmul(PSHI[0:nk, ph + q * 256 + doff:ph + q * 256 + doff + nq], lhsT=kT[hrow, r * Lk + k0:r * Lk + k0 + nk],
                                   rhs=qT[hrow, r * Lq + i0:r * Lq + i0 + nq], start=True, stop=True, skip_group_check=True)
                return ins
            S.op("pe", mms, reads=[B_k, B_q], writes=[PSA[4 + bk]])
            ts = rot["tt"] % 2
            rot["tt"] += 1
            ps_ = rot["pt"] % 3
            rot["pt"] += 1
            it["ps"] = ps_
            S.op("dve", (lambda ph, ts, cneg: lambda e: e.scalar_tensor_tensor(
                out=tt[ts][:, :], in0=dtile[:, :], scalar=cneg, in1=PSHI[:, ph:ph + 512], op0=ALU.mult, op1=ALU.add))(ph, ts, cneg),
                reads=[B_const, PSA[4 + bk]], writes=[B_tt[ts]])
            S.op("act", (lambda ts, ps_: lambda e: e.activation(out=PTt[ps_][:, :], in_=tt[ts][:, :], func=AF.Exp, scale=0.125))(ts, ps_),
                 reads=[B_tt[ts]], writes=[B_pt2[ps_]])

        def emit_pv(it):
            hd = it["hd"]
            pair = it["pair"]
            ps_ = it["ps"]
            vsel = (slice(0, 128) if hd == 0 else slice(64, 192))
            st = started[hd]
            plan = []
            wb = set()
            for q, (ci, r, k0, nk, i0, i1) in enumerate(pair):
                doff = i0 - (k0 - 64)
                pos = r * Lq + i0
                end = r * Lq + i1
                while pos < end:
                    bb = pos // 512
                    nx = min(end, (bb + 1) * 512)
                    plan.append((ci, nk, q * 256 + doff + (pos - (r * Lq + i0)), pos, nx - pos, not st[bb]))
                    st[bb] = True
                    wb.add(bb)
                    pos = nx

            def mmv(e, plan=plan, ps_=ps_, vsel=vsel):
                for (ci, nk, pcol, pos, n, stt) in plan:
                    ins = e.matmul(PSLO[:, pos:pos + n], lhsT=vaug[0:nk, ci, vsel], rhs=PTt[ps_][0:nk, pcol:pcol + n], start=stt, stop=True,
                                   skip_group_check=True)
                return ins
            S.op("pe", mmv, reads=[B_vaug, B_pt2[ps_]], writes=[PSA[bb] for bb in sorted(wb)])
            if it["last"]:
                for qt in range(4):
                    ia = 512 * qt // d
                    nn = 512 // d
                    srcp = accv[:, ia:ia + nn, :]
                    dst = osum(hd, qt).rearrange("p (i r) -> p i r", r=d)
                    rb = [PSA[qt]] if d == 1 else [PSA[0], PSA[1], PSA[2], PSA[3]]
                    if g == 0:
                        S.op("act", (lambda dst, srcp: lambda e: e.activation(out=dst, in_=srcp, func=AF.Copy))(dst, srcp),
                             reads=rb, writes=[B_os[hd][qt]])
                    else:
                        S.op("dve", (lambda dst, srcp: lambda e: e.tensor_tensor(out=dst, in0=dst, in1=srcp, op=ALU.add))(dst, srcp),
                             reads=rb + [B_os[hd][qt]], writes=[B_os[hd][qt]])

        LA = 2
        for idx in range(len(items) + LA):
            if idx < len(items):
                emit_scores(items[idx])
            if idx - LA >= 0:
                emit_pv(items[idx - LA])

    def attn_finish(hp):
        for hd in range(2):
            nrow = slice(0, 64) if hd == 0 else slice(64, 128)
            drow = slice(64, 128) if hd == 0 else slice(0, 64)
            for qt in range(4):
                S.op("act", (lambda hd, qt, nrow, drow: lambda e: e.activation(out=rden[nrow, :], in_=osum(hd, qt)[drow, :], func=AF.Copy))(hd, qt, nrow, drow),
                     reads=[B_os[hd][qt]], writes=[B_rd])
                S.op("dve", (lambda nrow: lambda e: e.reciprocal(out=rden[nrow, :], in_=rden[nrow, :]))(nrow), reads=[B_rd], writes=[B_rd])
                S.op("dve", (lambda hd, qt, nrow: lambda e: e.tensor_tensor(out=o_hp[nrow, qt * 512:(qt + 1) * 512], in0=osum(hd, qt)[nrow, :], in1=rden[nrow, :], op=ALU.mult))(hd, qt, nrow),
                     reads=[B_os[hd][qt], B_rd], writes=[B_ohp])
        S.dma("pool", (lambda hp: lambda e: e.dma_start(out=wao[:], in_=w_ao_d[hp]))(hp), "ao", writes=[B_wao])
        for dc in range(8):
            for qt in range(4):
                bk = wk_slot()
                ph = bk * 512

                def mmo2(e, dc=dc, qt=qt, ph=ph):
                    return e.matmul(PSHI[:, ph:ph + 512], lhsT=wao[:, dc * 128:(dc + 1) * 128], rhs=o_hp[:, qt * 512:(qt + 1) * 512], start=True, stop=True)
                S.op("pe", mmo2, reads=[B_wao, B_ohp], writes=[PSA[4 + bk]])
                S.op("dve", (lambda dc, qt, ph: lambda e: e.scalar_tensor_tensor(out=xres[:, dc, qt * 512:(qt + 1) * 512], in0=PSHI[:, ph:ph + 512], scalar=1.0 / ALPHA,
                                                                               in1=xres[:, dc, qt * 512:(qt + 1) * 512], op0=ALU.mult, op1=ALU.add))(dc, qt, ph),
                     reads=[PSA[4 + bk], XO[dc][qt]], writes=[XO[dc][qt]])
    for hp in range(8):
        for g, (win, d) in enumerate(DIL):
            attn_group(hp, g, win, d)
        attn_finish(hp)
    S.barrier()
    for t in range(2):
        t0, nt = TILES[t]
        ln_cols(1, 1, t0, nt, XR[t])
    S.barrier()
    if dbg == "l1attn":
        return dump_and_finish()
    for t in range(2):
        ffn_tile(1, 1, 2, t, XR[t])
    return dump_and_finish()


def _prep_shared(ffn1_w_gate, ffn1_w_up, ffn1_w_down, ffn2_w_gate, ffn2_w_up, ffn2_w_down,
                 ln_gain, ln_bias, pool_w_in, pool_w_group, pool_scale, pool_w_out, attn_w_qkv, attn_w_out):
    f = np.float32
    gates = [ffn1_w_gate, ffn2_w_gate]
    ups = [ffn1_w_up, ffn2_w_up]
    downs = [ffn1_w_down, ffn2_w_down]
    w_gu = np.empty((2, 2, NFC, 128, 2, 8, 128), f)
    w_d = np.empty((2, 2, 8, 128, NFC, 128), f)
    for l in range(2):
        for fi in range(2):
            g = np.asarray(gates[fi][l], f).reshape(8, 128, NFC, 128)
            u = np.asarray(ups[fi][l], f).reshape(8, 128, NFC, 128)
            w_gu[l, fi, :, :, 0] = g.transpose(2, 1, 0, 3)
            w_gu[l, fi, :, :, 1] = u.transpose(2, 1, 0, 3)
            dn = np.asarray(downs[fi][l], f).reshape(NFC, 128, 8, 128)
            w_d[l, fi] = dn.transpose(2, 1, 0, 3)
    lnp = np.empty((128, 2, 3, 2, 8), f)
    lnp[:, :, :, 0, :] = np.asarray(ln_gain, f).reshape(2, 3, 8, 128).transpose(3, 0, 1, 2)
    lnp[:, :, :, 1, :] = np.asarray(ln_bias, f).reshape(2, 3, 8, 128).transpose(3, 0, 1, 2)
    w_pin = np.asarray(pool_w_in[0], f).reshape(8, 128, 1024).transpose(1, 0, 2)
    w_pout = np.asarray(pool_w_out[0], f).reshape(8, 128, 1024).transpose(1, 0, 2)
    w_pgrp = np.asarray(pool_w_group[0], f).reshape(4, 2, 128, 256).transpose(2, 0, 1, 3)
    pscale = np.asarray(pool_scale[0], f).reshape(8, 128).T
    wq = np.asarray(attn_w_qkv[0], f).reshape(8, 128, 3, 3, 8, 128)
    w_qkv = wq.transpose(4, 2, 1, 3, 0, 5)
    w_ao = np.asarray(attn_w_out[0], f).reshape(8, 128, 1024)
    r = np.arange(128)[:, None]
    i = np.arange(256)[None, :]
    dd = np.abs(r - i + 64).astype(f)
    dtile = np.where(dd <= 64, dd, BIGD).astype(f)
    dtile = np.concatenate([dtile, dtile], axis=1)
    c = np.ascontiguousarray
    return {
        "lnp": c(lnp.reshape(128, 96)), "pscale": c(pscale), "dtile": c(dtile), "ident": np.eye(128, dtype=f),
        "w_gu": c(w_gu.reshape(88, 128, 2048)), "w_d": c(w_d.reshape(32, 128, 2816)),
        "w_pin": c(w_pin.reshape(128, 8192)), "w_pout": c(w_pout.reshape(128, 8192)), "w_pgrp": c(w_pgrp.reshape(128, 2048)),
        "w_qkv": c(w_qkv.reshape(24, 128, 3072)), "w_ao": c(w_ao),
    }


def _prep_core(x, core):
    f = np.float32
    b, half = core // 2, core % 2
    xs = np.asarray(x[b], f)
    if half == 0:
        loc = xs[0:TC]
    else:
        loc = xs[::-1][0:TC]
    xT = np.ascontiguousarray(loc.T.reshape(8, 128, TC).transpose(1, 0, 2))
    pflag = np.zeros((128, 2), f)
    pflag[:, half] = 1.0
    pinv = np.empty((128, 8, 8), f)
    for cidx in range(8):
        h = 1 << (cidx // 2)
        w = 2 * h
        for jo in range(8):
            if half == 0:
                cnt = h + min(h, jo)
            else:
                cnt = h + min(h, jo + 1)
            pinv[:, cidx, jo] = 1.0 / cnt
    return {"xT": xT, "pflag": pflag, "pinv": np.ascontiguousarray(pinv.reshape(128, 64))}


_NC_CACHE = {}


def _get_nc(dbg=None):
    if dbg not in _NC_CACHE:
        _NC_CACHE[dbg] = build_program(dbg)
    return _NC_CACHE[dbg]


def run_cores(inputs, dbg=None, cores=range(8), trace=False):
    x = np.asarray(inputs["x"], np.float32)
    shared = _prep_shared(**{k: v for k, v in inputs.items() if k != "x"})
    in_maps = []
    for core in cores:
        m = dict(shared)
        m.update(_prep_core(x, core))
        in_maps.append(m)
    nc = _get_nc(dbg)
    res = run_bass_kernel_spmd(nc, in_maps, core_ids=list(range(len(in_maps))), **({"trace": True} if trace else {}))
    return res


def kernel(**inputs):
    res = run_cores(inputs)
    out = np.empty((BATCH, SEQ, D_MODEL), np.float32)
    for core in range(8):
        yT = np.asarray(res.results[core]["yT"], np.float32)
        y = yT.transpose(2, 1, 0).reshape(OWN, D_MODEL)
        b, half = core // 2, core % 2
        if half == 0:
            out[b, 0:OWN] = y
        else:
            out[b, OWN:SEQ] = y[::-1]
    return out
```

```python
import numpy as np
import concourse.bass as bass
import concourse.mybir as mybir
from concourse.bass_utils import run_bass_kernel_spmd

F32 = mybir.dt.float32
BF16 = mybir.dt.bfloat16
AF = mybir.ActivationFunctionType
ALU = mybir.AluOpType

D_MODEL = 1024
SEQ = 4096
BATCH = 4
D_FF = 2816
NFC = 22
OWN = 2048
HALO = 1024
TC = OWN + HALO + 8
TH = OWN + HALO
ALPHA = 4.0 ** 0.25
LN_EPS = 1e-5
EPS_P = LN_EPS / (ALPHA * ALPHA)
DIL = ((128, 1), (512, 4), (2048, 16))
BIGD = 1.0e5

ENGS = ("pe", "act", "dve", "pool", "sp")


class Buf:
    __slots__ = ("name", "wdep", "rdeps")

    def __init__(self, name):
        self.name = name
        self.wdep = None
        self.rdeps = {}


class Sched:
    def __init__(self, nc):
        self.nc = nc
        self.sem = {}
        self.cnt = {}
        for e in ENGS:
            self.sem[e] = nc.alloc_semaphore("s_" + e)
            self.cnt[e] = 0
        self.seen = {e: {} for e in ENGS}
        self.ops = {e: [] for e in ENGS}

    def new_dma_sem(self, key):
        self.sem[key] = self.nc.alloc_semaphore("d_" + key)
        self.cnt[key] = 0
        return key

    def _collect(self, reads, writes):
        w = {}
        for b in reads:
            d = b.wdep
            if d is not None and w.get(d[0], 0) < d[1]:
                w[d[0]] = d[1]
        for b in writes:
            d = b.wdep
            if d is not None and w.get(d[0], 0) < d[1]:
                w[d[0]] = d[1]
            for k, v in b.rdeps.items():
                if w.get(k, 0) < v:
                    w[k] = v
        return w

    def _need(self, e, w):
        need = []
        s = self.seen[e]
        for k, v in w.items():
            if s.get(k, 0) < v:
                need.append((k, v))
                s[k] = v
        return need

    def op(self, e, fn, reads=(), writes=()):
        need = self._need(e, self._collect(reads, writes))
        self.cnt[e] += 1
        v = self.cnt[e]
        self.ops[e].append((need, fn, (e, 1)))
        for b in reads:
            if b.rdeps.get(e, 0) < v:
                b.rdeps[e] = v
        for b in writes:
            b.wdep = (e, v)
            b.rdeps = {}
        return (e, v)

    def dma(self, q, fn, semkey, reads=(), writes=()):
        need = self._need(q, self._collect(reads, writes))
        self.cnt[semkey] += 16
        v = self.cnt[semkey]
        self.ops[q].append((need, fn, (semkey, 16)))
        for b in reads:
            if b.rdeps.get(semkey, 0) < v:
                b.rdeps[semkey] = v
        for b in writes:
            b.wdep = (semkey, v)
            b.rdeps = {}
        return (semkey, v)

    def wait_all(self, e, keys):
        w = {k: self.cnt[k] for k in keys if self.cnt[k] > 0}
        need = self._need(e, w)
        if need:
            self.ops[e].append((need, None, None))

    def barrier(self, engines=("pe", "act", "dve", "pool")):
        for e in engines:
            self.wait_all(e, list(engines))

    def emit(self, block):
        sched = self

        def make(e):
            def body(eng):
                for need, fn, inc in sched.ops[e]:
                    for k, v in need:
                        eng.wait_ge(sched.sem[k], v)
                    if fn is not None:
                        ins = fn(eng)
                        ins.then_inc(sched.sem[inc[0]], inc[1])
            return body
        block.tensor(make("pe"))
        block.scalar(make("act"))
        block.vector(make("dve"))
        block.gpsimd(make("pool"))
        block.sync(make("sp"))


def _halves(nt):
    return [(h0, min(512, nt - h0)) for h0 in range(0, nt, 512)]


def build_program(dbg=None):
    nc = bass.Bass("TRN2", target_bir_lowering=False)
    S = Sched(nc)

    def din(name, shape):
        return nc.dram_tensor(name, list(shape), F32, kind="ExternalInput").ap()

    xT_d = din("xT", (128, 8, TC))
    lnp_d = din("lnp", (128, 96))
    pflag_d = din("pflag", (128, 2))
    pinv_d = din("pinv", (128, 64))
    pscale_d = din("pscale", (128, 8))
    dtile_d = din("dtile", (128, 512))
    ident_d = din("ident", (128, 128))
    w_gu_d = din("w_gu", (88, 128, 2048))
    w_d_d = din("w_d", (32, 128, 2816))
    w_pin_d = din("w_pin", (128, 8192))
    w_pout_d = din("w_pout", (128, 8192))
    w_pgrp_d = din("w_pgrp", (128, 2048))
    w_qkv_d = din("w_qkv", (24, 128, 3072))
    w_ao_d = din("w_ao", (8, 128, 1024))
    if dbg is None:
        y_d = nc.dram_tensor("yT", [128, 8, OWN], F32, kind="ExternalOutput").ap()
    else:
        y_d = nc.dram_tensor("yT", [128, 8, TC], F32, kind="ExternalOutput").ap()

    base = (nc.sbuf_base + 31) // 32 * 32
    OFF_X = 0
    OFF_C = 98560
    OFF_R1 = OFF_C + 4096
    OFF_RING = OFF_R1 + 61440
    OFF_LN = OFF_RING + 23552
    OFF_SP = OFF_LN + 16384
    ARENA = OFF_SP + 4608
    arena = nc.alloc_sbuf_tensor("arena", [128, ARENA // 4], F32)
    abase = base
    assert nc.sbuf_base >= abase + ARENA

    def at(name, shape, dtype, off):
        return nc.alloc_sbuf_tensor_at(name, list(shape), dtype, offset=abase + off)

    xres = at("xres", (128, 8, TC), F32, OFF_X)
    lnp = at("lnp", (128, 96), F32, OFF_C)
    pflag = at("pflag", (128, 2), F32, OFF_C + 384)
    pinv = at("pinv", (128, 64), F32, OFF_C + 416)
    pscale = at("pscale", (128, 8), F32, OFF_C + 672)
    dtile = at("dtile", (128, 512), F32, OFF_C + 704)
    ones_bf = at("ones_bf", (128, 128), BF16, OFF_C + 2752)
    ident_bf = at("ident_bf", (128, 128), BF16, OFF_C + 3008)
    hsave = at("hsave", (128, 8, 8), F32, OFF_C + 3264)
    epst = at("epst", (128, 1), F32, OFF_C + 3520)
    negone = at("negone", (128, 1), F32, OFF_C + 3552)
    hbuf = at("hbuf", (128, NFC, 1024), BF16, OFF_R1)
    xb = at("xb", (128, 8, 1024), BF16, OFF_R1 + 45056)
    gu_slot = [at(f"gu{i}", (128, 2, 8, 128), BF16, OFF_RING + i * 4096) for i in range(3)]
    d_slot = [at(f"dw{i}", (128, NFC, 128), BF16, OFF_RING + 12288 + i * 5632) for i in range(2)]
    zb = at("zb", (128, 2, 1024), BF16, OFF_LN)
    zq = at("zq", (128, 2, 1024), BF16, OFF_LN + 4096)
    meant = at("meant", (128, 1024), F32, OFF_LN + 8192)
    rstdt = at("rstdt", (128, 1024), F32, OFF_LN + 12288)
    sgt = at("sgt", (128, 1024), F32, OFF_SP)
    p_win = at("p_win", (128, 8, 1024), BF16, OFF_R1)
    p_wout = at("p_wout", (128, 8, 1024), BF16, OFF_R1 + 16384)
    p_wgrp = at("p_wgrp", (128, 4, 2, 256), BF16, OFF_R1 + 32768)
    p_xb = at("p_xb", (128, 8, 512), BF16, OFF_R1 + 36864)
    p_yb = at("p_yb", (128, 8, 512), BF16, OFF_R1 + 45056)
    p_mx = at("p_mx", (128, 8, 512), BF16, OFF_R1 + 53248)
    p_u = at("p_u", (128, 8, 528), F32, OFF_RING)
    p_t = [at(f"p_t{i}", (128, 528), F32, OFF_RING + 16896 + i * 2112) for i in range(2)]
    p_wn = at("p_wn", (128, 512), F32, OFF_RING + 21120)
    x1b = at("x1b", (128, 8, TH), BF16, OFF_R1)
    qT = at("qT", (128, OWN), BF16, OFF_R1 + 49152)
    kT = at("kT", (128, TH), BF16, OFF_R1 + 53248)
    PTt = [at(f"PT{i}", (128, 512), BF16, OFF_RING + 20480 + i * 1024) for i in range(3)]
    wqkv2 = [at(f"wqkv{i}", (128, 3, 8, 128), BF16, OFF_RING + i * 6144) for i in range(2)]
    vT = at("vT", (128, TH), BF16, OFF_RING + 12288)
    wao = at("wao", (128, 1024), BF16, OFF_RING + 18432)
    vaug = at("vaug", (128, 32, 192), BF16, OFF_LN)
    o_hp = at("o_hp", (128, OWN), BF16, OFF_LN + 12288)
    tt = [at(f"tt{i}", (128, 512), F32, OFF_SP + i * 2048) for i in range(2)]

    PSLO = nc.alloc_psum_tensor("pslo", [128, 2048], F32)
    PSHI = nc.alloc_psum_tensor("pshi", [128, 2048], F32)
    PS = [PSLO[:, 0:1024], PSLO[:, 1024:2048], PSHI[:, 0:1024], PSHI[:, 1024:2048]]
    PSB = [Buf(f"ps{i}") for i in range(4)]

    for k in ["x0", "x1", "x2", "x3", "const", "gu0", "gu1", "gu2", "dw0", "dw1", "pw", "aw0", "aw1", "ao", "out"]:
        S.new_dma_sem(k)

    B_const = Buf("const")
    B_ident = Buf("ident")
    B_ones = Buf("ones")
    TILES = [(0, 1024), (1024, 1024), (2048, 1024), (3072, 8)]
    XR = [[Buf(f"xr{t}_{c}") for c in range(8)] for t in range(4)]
    XB = [Buf(f"xb{c}") for c in range(8)]
    HB = [Buf(f"hb{f}") for f in range(NFC)]
    GU = [Buf(f"gu{i}") for i in range(3)]
    DW = [Buf(f"dw{i}") for i in range(2)]
    B_sg = Buf("sg")
    B_zb = [Buf("zb0"), Buf("zb1")]
    B_zq = [Buf("zq0"), Buf("zq1")]
    B_mean = Buf("mean")
    B_rstd = Buf("rstd")
    ring = {"gu": 0, "dw": 0}

    for (dst, src) in [(lnp, lnp_d), (pflag, pflag_d), (pinv, pinv_d), (pscale, pscale_d), (dtile, dtile_d)]:
        S.dma("sp", (lambda dst, src: lambda e: e.dma_start(out=dst[:], in_=src))(dst, src), "const", writes=[B_const])
    B_const.wdep = ("const", S.cnt["const"])
    S.dma("pool", lambda e: e.dma_start(out=ident_bf[:], in_=ident_d), "pw", writes=[B_ident])
    S.op("dve", lambda e: e.memset(ones_bf[:], 1.0), writes=[B_ones])
    S.op("dve", lambda e: e.memset(negone[:], -1.0), writes=[B_const])
    S.op("dve", lambda e: e.memset(epst[:], EPS_P), writes=[B_const])
    B_const.wdep = ("const", S.cnt["const"])
    B_eps = Buf("eps")
    B_eps.wdep = ("dve", S.cnt["dve"])
    for t, (t0, nt) in enumerate(TILES):
        S.dma("sp", (lambda t0, nt: lambda e: e.dma_start(out=xres[:, :, t0:t0 + nt], in_=xT_d[:, :, t0:t0 + nt]))(t0, nt),
              f"x{t}", writes=XR[t])

    def ln_stats_chunk(c, t0, nt, xr_bufs):
        hv = _halves(nt)
        sl = c % 2
        S.op("act", (lambda c, sl: lambda e: e.activation(out=zb[:, sl, 0:nt], in_=xres[:, c, t0:t0 + nt], func=AF.Copy))(c, sl),
             reads=[xr_bufs[c]], writes=[B_zb[sl]])
        S.op("act", (lambda c, sl: lambda e: e.activation(out=zq[:, sl, 0:nt], in_=xres[:, c, t0:t0 + nt], func=AF.Square))(c, sl),
             reads=[xr_bufs[c]], writes=[B_zq[sl]])

        def mm(e, c=c, sl=sl):
            for (h0, hn) in hv:
                e.matmul(PS[0][:, h0:h0 + hn], lhsT=ones_bf[:, :], rhs=zb[:, sl, h0:h0 + hn], start=(c == 0), stop=(c == 7))
            for (h0, hn) in hv:
                ins = e.matmul(PS[1][:, h0:h0 + hn], lhsT=ones_bf[:, :], rhs=zq[:, sl, h0:h0 + hn], start=(c == 0), stop=(c == 7))
            return ins
        S.op("pe", mm, reads=[B_zb[sl], B_zq[sl], B_ones], writes=[PSB[0], PSB[1]])

    def ln_cols(l, s, t0, nt, xr_bufs):
        for c in range(8):
            ln_stats_chunk(c, t0, nt, xr_bufs)
        ln_finish(l, s, t0, nt, xr_bufs)

    def ln_finish_head(nt):
        S.op("dve", lambda e: e.tensor_scalar(out=meant[:, 0:nt], in0=PS[0][:, 0:nt], scalar1=1.0 / D_MODEL, scalar2=None, op0=ALU.mult),
             reads=[PSB[0]], writes=[B_mean])
        S.op("dve", lambda e: e.tensor_tensor(out=rstdt[:, 0:nt], in0=meant[:, 0:nt], in1=meant[:, 0:nt], op=ALU.mult),
             reads=[B_mean], writes=[B_rstd])
        S.op("dve", lambda e: e.scalar_tensor_tensor(out=rstdt[:, 0:nt], in0=PS[1][:, 0:nt], scalar=1.0 / D_MODEL, in1=rstdt[:, 0:nt],
                                                     op0=ALU.mult, op1=ALU.subtract),
             reads=[PSB[1], B_rstd], writes=[B_rstd])
        S.op("act", lambda e: e.activation(out=rstdt[:, 0:nt], in_=rstdt[:, 0:nt], func=AF.Ln, bias=epst[:, 0:1], scale=1.0),
             reads=[B_rstd, B_eps], writes=[B_rstd])
        S.op("act", lambda e: e.activation(out=rstdt[:, 0:nt], in_=rstdt[:, 0:nt], func=AF.Exp, scale=-0.5),
             reads=[B_rstd], writes=[B_rstd])

    def ln_finish_chunk(l, s, c, t0, nt, xr_bufs):
        gi = (l * 3 + s) * 16
        S.op("dve", lambda e: e.tensor_tensor(out=xres[:, c, t0:t0 + nt], in0=xres[:, c, t0:t0 + nt], in1=meant[:, 0:nt], op=ALU.subtract),
             reads=[xr_bufs[c], B_mean], writes=[xr_bufs[c]])
        S.op("dve", lambda e: e.tensor_tensor(out=xres[:, c, t0:t0 + nt], in0=xres[:, c, t0:t0 + nt], in1=rstdt[:, 0:nt], op=ALU.mult),
             reads=[xr_bufs[c], B_rstd], writes=[xr_bufs[c]])
        S.op("act", lambda e: e.activation(out=xres[:, c, t0:t0 + nt], in_=xres[:, c, t0:t0 + nt], func=AF.Identity,
                                           bias=lnp[:, gi + 8 + c:gi + 9 + c], scale=lnp[:, gi + c:gi + c + 1]),
             reads=[xr_bufs[c], B_const], writes=[xr_bufs[c]])

    def ln_finish(l, s, t0, nt, xr_bufs):
        ln_finish_head(nt)
        for c in range(8):
            ln_finish_chunk(l, s, c, t0, nt, xr_bufs)

    def ffn_cast(t):
        t0, nt = TILES[t]
        xr_bufs = XR[t]
        for c in range(8):
            eng = "dve" if c % 2 == 0 else "act"
            if eng == "dve":
                S.op("dve", (lambda c: lambda e: e.tensor_copy(out=xb[:, c, 0:nt], in_=xres[:, c, t0:t0 + nt]))(c), reads=[xr_bufs[c]], writes=[XB[c]])
            else:
                S.op("act", (lambda c: lambda e: e.activation(out=xb[:, c, 0:nt], in_=xres[:, c, t0:t0 + nt], func=AF.Copy))(c), reads=[xr_bufs[c]], writes=[XB[c]])

    def ffn_tile(l, fi, s, t, xr_bufs, do_cast=True, next_t=None, deferred=None, defer=False):
        t0, nt = TILES[t]
        hv = _halves(nt)
        if do_cast:
            ffn_cast(t)
        wbase = (l * 2 + fi) * NFC
        for fc in range(NFC):
            sl = ring["gu"] % 3
            ring["gu"] += 1
            S.dma("pool", (lambda sl, idx: lambda e: e.dma_start(out=gu_slot[sl][:], in_=w_gu_d[idx].rearrange("p (a k f) -> p a k f", a=2, k=8)))(sl, wbase + fc),
                  f"gu{sl}", writes=[GU[sl]])
            pg, pu = (2, 3) if fc % 2 == 0 else (0, 1)

            def mm(e, sl=sl, pg=pg, pu=pu):
                for (pp, a) in ((pg, 0), (pu, 1)):
                    for (h0, hn) in hv:
                        for kc in range(8):
                            ins = e.matmul(PS[pp][:, h0:h0 + hn], lhsT=gu_slot[sl][:, a, kc, :], rhs=xb[:, kc, h0:h0 + hn], start=(kc == 0), stop=(kc == 7))
                return ins
            S.op("pe", mm, reads=[GU[sl]] + XB, writes=[PSB[pg], PSB[pu]])
            S.op("act", (lambda pg: lambda e: e.activation(out=sgt[:, 0:nt], in_=PS[pg][:, 0:nt], func=AF.Silu))(pg), reads=[PSB[pg]], writes=[B_sg])
            S.op("dve", (lambda pu, fc: lambda e: e.tensor_tensor(out=hbuf[:, fc, 0:nt], in0=sgt[:, 0:nt], in1=PS[pu][:, 0:nt], op=ALU.mult))(pu, fc),
                 reads=[B_sg, PSB[pu]], writes=[HB[fc]])
            if deferred:
                deferred.pop(0)()
        while deferred:
            deferred.pop(0)()
        if next_t is not None:
            ffn_cast(next_t)
        dbase = (l * 2 + fi) * 8
        for dc in range(8):
            sl = ring["dw"] % 2
            ring["dw"] += 1
            S.dma("pool", (lambda sl, idx: lambda e: e.dma_start(out=d_slot[sl][:], in_=w_d_d[idx].rearrange("p (f d) -> p f d", f=NFC)))(sl, dbase + dc),
                  f"dw{sl}", writes=[DW[sl]])
            py = 2 + dc % 2

            def mm2(e, sl=sl, py=py):
                for (h0, hn) in hv:
                    for fc in range(NFC):
                        ins = e.matmul(PS[py][:, h0:h0 + hn], lhsT=d_slot[sl][:, fc, :], rhs=hbuf[:, fc, h0:h0 + hn], start=(fc == 0), stop=(fc == NFC - 1))
                return ins
            S.op("pe", mm2, reads=[DW[sl]] + HB, writes=[PSB[py]])
            S.op("dve", (lambda dc, py: lambda e: e.scalar_tensor_tensor(out=xres[:, dc, t0:t0 + nt], in0=PS[py][:, 0:nt], scalar=0.5 / ALPHA,
                                                                           in1=xres[:, dc, t0:t0 + nt], op0=ALU.mult, op1=ALU.add))(dc, py),
                 reads=[PSB[py], xr_bufs[dc]], writes=[xr_bufs[dc]])
            if dc >= 1:
                ln_stats_chunk(dc - 1, t0, nt, xr_bufs)
        ln_stats_chunk(7, t0, nt, xr_bufs)
        ln_finish_head(nt)
        pieces = [(lambda c: lambda: ln_finish_chunk(l, s, c, t0, nt, xr_bufs))(c) for c in range(8)]
        if defer:
            return pieces
        for p in pieces:
            p()
        return None

    def dump_and_finish():
        ncols = OWN if dbg is None else TC
        S.barrier()
        S.wait_all("sp", ["pe", "act", "dve"])
        for c in range(8):
            S.dma("sp", (lambda c: lambda e: e.dma_start(out=y_d[:, c, :], in_=xres[:, c, 0:ncols]))(c), "out",
                  reads=[b for t in range(4) for b in XR[t]])
        S.wait_all("sp", ["out"])
        S.wait_all("pool", ["gu0", "gu1", "gu2", "dw0", "dw1", "pw", "aw0", "aw1", "ao"])
        with nc.Block() as block:
            S.emit(block)
        return nc

    def ffn_jobs(jobs, dbg_after=None):
        pend = None
        for i, (l, fi, s_, t) in enumerate(jobs):
            nxt = jobs[i + 1][3] if i + 1 < len(jobs) else None
            pend = ffn_tile(l, fi, s_, t, XR[t], do_cast=(i == 0), next_t=nxt, deferred=(pend if i > 0 else None), defer=(nxt is not None))

    ffn_jobs([(0, 0, 0, t) for t in range(4)])
    if dbg == "l0ffn1":
        return dump_and_finish()

    S.barrier()
    ALLX = [b for t in range(4) for b in XR[t]]
    B_pw = Buf("pw")
    S.dma("pool", lambda e: e.dma_start(out=p_win[:], in_=w_pin_d.rearrange("p (k e) -> p k e", k=8)), "pw", writes=[B_pw])
    S.dma("pool", lambda e: e.dma_start(out=p_wout[:], in_=w_pout_d.rearrange("p (k e) -> p k e", k=8)), "pw", writes=[B_pw])
    S.dma("pool", lambda e: e.dma_start(out=p_wgrp[:], in_=w_pgrp_d.rearrange("p (g c e) -> p g c e", g=4, c=2)), "pw", writes=[B_pw])
    B_pw.wdep = ("pw", S.cnt["pw"])
    B_pxb = [Buf(f"pxb{c}") for c in range(8)]
    B_pu = [Buf(f"pu{c}") for c in range(8)]
    B_pmx = [Buf(f"pmx{c}") for c in range(8)]
    B_pyb = [Buf(f"pyb{c}") for c in range(8)]
    B_pt = [Buf("pt0"), Buf("pt1")]
    B_pwn = Buf("pwn")
    B_hs = Buf("hsave")
    S.op("dve", lambda e: e.memset(p_u[:, :, 0:8], 0.0), writes=B_pu)
    TP = 448
    ptiles = [(a, min(a + TP, TH)) for a in range(0, TH, TP)]
    XP = [Buf(f"xp{c}") for c in range(8)]
    def pool_tile(ti, a, b):
        n_o = b - a
        first = (ti == 0)
        ua = 0 if first else a - 8
        ub = b + 8
        n_u = ub - ua
        loff = 8 if first else 0
        for c in range(8):
            if first:
                S.op("act", (lambda c: lambda e: e.activation(out=p_xb[:, c, 0:n_u], in_=xres[:, c, ua:ub], func=AF.Copy))(c), reads=[XP[c]], writes=[B_pxb[c]])
            else:
                S.op("act", (lambda c: lambda e: e.activation(out=p_xb[:, c, 0:8], in_=hsave[:, c, :], func=AF.Copy))(c), reads=[B_hs], writes=[B_pxb[c]])
                S.op("act", (lambda c: lambda e: e.activation(out=p_xb[:, c, 8:n_u], in_=xres[:, c, a:ub], func=AF.Copy))(c), reads=[XP[c]], writes=[B_pxb[c]])
        S.op("dve", lambda e: e.tensor_copy(out=hsave[:, :, :], in_=xres[:, :, b - 8:b]), reads=XP + B_pxb, writes=[B_hs])
        for c in range(8):
            pp = c % 4

            def mmu(e, c=c, pp=pp):
                for kc in range(8):
                    ins = e.matmul(PS[pp][:, 0:n_u], lhsT=p_win[:, kc, c * 128:(c + 1) * 128], rhs=p_xb[:, kc, 0:n_u], start=(kc == 0), stop=(kc == 7))
                return ins
            S.op("pe", mmu, reads=[B_pw] + B_pxb, writes=[PSB[pp]])
            S.op("act", (lambda c, pp: lambda e: e.activation(out=p_u[:, c, loff:loff + n_u], in_=PS[pp][:, 0:n_u], func=AF.Copy))(c, pp),
                 reads=[PSB[pp]], writes=[B_pu[c]])
        n_loc = n_u + loff
        lo0 = 8
        for c in range(8):
            g = c // 2
            h = 1 << g
            w = 2 * h
            src = p_u[:, c, :]
            srcb = B_pu[c]
            ln = 1
            k = 0
            while ln < w:
                dst = p_t[k % 2]
                cnt = n_loc - 2 * ln + 1
                S.op("dve", (lambda src, dst, ln, cnt: lambda e: e.tensor_tensor(out=dst[:, 0:cnt], in0=src[:, 0:cnt], in1=src[:, ln:ln + cnt], op=ALU.add))(src, dst, ln, cnt),
                     reads=[srcb], writes=[B_pt[k % 2]])
                src = dst
                srcb = B_pt[k % 2]
                ln *= 2
                k += 1
            s0 = lo0 - h
            S.op("dve", (lambda src, s0: lambda e: e.tensor_scalar(out=p_wn[:, 0:n_o], in0=src[:, s0 + 1:s0 + 1 + n_o], scalar1=pflag[:, 1:2], scalar2=None, op0=ALU.mult))(src, s0),
                 reads=[srcb, B_const], writes=[B_pwn])
            S.op("dve", (lambda src, s0: lambda e: e.scalar_tensor_tensor(out=p_wn[:, 0:n_o], in0=src[:, s0:s0 + n_o], scalar=pflag[:, 0:1], in1=p_wn[:, 0:n_o],
                                                                            op0=ALU.mult, op1=ALU.add))(src, s0),
                 reads=[srcb, B_const, B_pwn], writes=[B_pwn])
            S.op("dve", (lambda c, w: lambda e: e.scalar_tensor_tensor(out=p_mx[:, c, 0:n_o], in0=p_wn[:, 0:n_o], scalar=1.0 / w, in1=p_u[:, c, lo0:lo0 + n_o],
                                                                         op0=ALU.mult, op1=ALU.subtract))(c, w),
                 reads=[B_pwn, B_pu[c]], writes=[B_pmx[c]])
            if first:
                S.op("dve", (lambda c: lambda e: e.tensor_tensor(out=p_wn[:, 0:8], in0=p_wn[:, 0:8], in1=pinv[:, c * 8:c * 8 + 8], op=ALU.mult))(c),
                     reads=[B_pwn, B_const], writes=[B_pwn])
                S.op("dve", (lambda c: lambda e: e.tensor_tensor(out=p_mx[:, c, 0:8], in0=p_wn[:, 0:8], in1=p_u[:, c, lo0:lo0 + 8], op=ALU.subtract))(c),
                     reads=[B_pwn, B_pu[c]], writes=[B_pmx[c]])
        for ec in range(8):
            g = ec // 2
            eh = ec % 2
            pp = ec % 4

            def mmg(e, g=g, eh=eh, pp=pp):
                for cc in range(2):
                    ins = e.matmul(PS[pp][:, 0:n_o], lhsT=p_wgrp[:, g, cc, eh * 128:(eh + 1) * 128], rhs=p_mx[:, 2 * g + cc, 0:n_o], start=(cc == 0), stop=(cc == 1))
                return ins
            S.op("pe", mmg, reads=[B_pw, B_pmx[2 * g], B_pmx[2 * g + 1]], writes=[PSB[pp]])
            S.op("act", (lambda ec, pp: lambda e: e.activation(out=p_yb[:, ec, 0:n_o], in_=PS[pp][:, 0:n_o], func=AF.Identity, scale=pscale[:, ec:ec + 1]))(ec, pp),
                 reads=[PSB[pp], B_const], writes=[B_pyb[ec]])
        if dbg == "pooldbg" and ti == 0:
            S.barrier()
            Bd = Buf("dbgout")
            S.dma("pool", lambda e: e.dma_start(out=y_d[:, :, 0:528], in_=p_u[:, :, :]), "out", reads=B_pu)
            S.dma("pool", lambda e: e.dma_start(out=y_d[:, :, 600:1048], in_=p_mx[:, :, 0:448]), "out", reads=B_pmx)
            S.dma("pool", lambda e: e.dma_start(out=y_d[:, :, 1100:1548], in_=p_yb[:, :, 0:448]), "out", reads=B_pyb)
            S.dma("pool", lambda e: e.dma_start(out=y_d[:, :, 1600:2056], in_=p_xb[:, :, 0:456]), "out", reads=B_pxb)
            S.dma("pool", lambda e: e.dma_start(out=y_d[:, :, 2056:3080], in_=p_wout[:, :, :]), "out", reads=[B_pw])
            S.wait_all("pool", ["out"])
            with nc.Block() as block:
                S.emit(block)
            return nc
        for dc in range(8):
            pp = dc % 4

            def mmo(e, dc=dc, pp=pp):
                for ec in range(8):
                    ins = e.matmul(PS[pp][:, 0:n_o], lhsT=p_wout[:, ec, dc * 128:(dc + 1) * 128], rhs=p_yb[:, ec, 0:n_o], start=(ec == 0), stop=(ec == 7))
                return ins
            S.op("pe", mmo, reads=[B_pw] + B_pyb, writes=[PSB[pp]])
            S.op("dve", (lambda dc, pp: lambda e: e.scalar_tensor_tensor(out=xres[:, dc, a:b], in0=PS[pp][:, 0:n_o], scalar=1.0 / ALPHA, in1=xres[:, dc, a:b],
                                                                           op0=ALU.mult, op1=ALU.add))(dc, pp),
                 reads=[PSB[pp], XP[dc], B_hs], writes=[XP[dc]])
        if dbg == "pooldbg2" and ti == 0:
            S.barrier()
            for c in range(8):
                S.dma("sp", (lambda c: lambda e: e.dma_start(out=y_d[:, c, 0:448], in_=xres[:, c, 0:448]))(c), "out", reads=XP)
            S.wait_all("sp", ["out"])
            with nc.Block() as block:
                S.emit(block)
            return nc
        ln_cols(0, 1, a, n_o, XP)
        if dbg is not None and dbg.startswith("pooldbg3:") and ti == int(dbg.split(":")[1]):
            S.barrier()
            for c in range(8):
                S.dma("sp", (lambda c: lambda e: e.dma_start(out=y_d[:, c, :], in_=xres[:, c, :]))(c), "out", reads=XP)
            S.wait_all("sp", ["out"])
            with nc.Block() as block:
                S.emit(block)
            return nc
        return None
    for ti, (a, b) in enumerate(ptiles):
        _r = pool_tile(ti, a, b)
        if _r is not None:
            return _r
    S.barrier()
    if dbg == "l0pool":
        return dump_and_finish()

    if dbg == "l0":
        ffn_jobs([(0, 1, 2, t) for t in range(3)])
        return dump_and_finish()
    ffn_jobs([(0, 1, 2, t) for t in range(3)] + [(1, 0, 0, t) for t in range(3)])
    S.barrier()
    if dbg == "l1ffn1":
        return dump_and_finish()

    B_x1b = [Buf(f"x1b{c}") for c in range(8)]
    for c in range(8):
        if c % 2 == 0:
            S.op("dve", (lambda c: lambda e: e.tensor_copy(out=x1b[:, c, :], in_=xres[:, c, 0:TH]))(c), reads=ALLX, writes=[B_x1b[c]])
        else:
            S.op("act", (lambda c: lambda e: e.activation(out=x1b[:, c, :], in_=xres[:, c, 0:TH], func=AF.Copy))(c), reads=ALLX, writes=[B_x1b[c]])
    S.barrier()
    B_vaug = Buf("vaug")
    S.op("dve", lambda e: e.memset(vaug[:, :, 64:128], 1.0), writes=[B_vaug])
    B_wq = [Buf("wqkv0"), Buf("wqkv1")]
    B_q = Buf("qT")
    B_k = Buf("kT")
    B_v = Buf("vT")
    B_pt2 = [Buf("PT0"), Buf("PT1"), Buf("PT2")]
    B_tt = [Buf("tt0"), Buf("tt1")]
    B_os = [[Buf(f"os{h}_{q}") for q in range(4)] for h in range(2)]
    B_rd = [Buf("rden0"), Buf("rden1")]
    B_ohp = Buf("ohp")
    B_wao = Buf("wao")
    XO = [[Buf(f"xo{dc}_{q}") for q in range(4)] for dc in range(8)]
    PSA = [Buf(f"psa{i}") for i in range(8)]
    rden = xres[:, 0, OWN + 512:OWN + 1024]
    psT = PSHI[:, :].bitcast(BF16)

    def osum(hd, qt):
        return xres[:, hd * 4 + qt, OWN:OWN + 512]

    slopes = 2.0 ** (-8.0 * np.arange(1, 49) / 48.0)
    slopes = slopes.reshape(3, 16)
    rot = {"wk": 0, "tt": 0, "pt": 0, "wq": 0}

    def wk_slot():
        b = rot["wk"] % 4
        rot["wk"] += 1
        return b

    def attn_proj(hp, g, win, d):
        Lq = OWN // d
        Lk = Lq + 64
        ntok_k = Lk * d
        ws = rot["wq"] % 2
        rot["wq"] += 1
        wq = wqkv2[ws]
        S.dma("pool", (lambda idx, wq: lambda e: e.dma_start(out=wq[:], in_=w_qkv_d[idx].rearrange("p (m k f) -> p m k f", m=3, k=8)))(hp * 3 + g, wq),
              f"aw{ws}", writes=[B_wq[ws]])
        ev = 0
        for m, (dstT, dstB, ntok) in enumerate(((qT, B_q, OWN), (kT, B_k, ntok_k), (vT, B_v, ntok_k))):
            dview = dstT[:, 0:ntok].rearrange("p (r i) -> p r i", r=d)
            for j0 in range(0, ntok, 512):
                n = min(512, ntok - j0)
                bk = wk_slot()
                ph = bk * 512

                def mmp(e, m=m, j0=j0, n=n, ph=ph):
                    for kc in range(8):
                        ins = e.matmul(PSHI[:, ph:ph + n], lhsT=wq[:, m, kc, :], rhs=x1b[:, kc, j0:j0 + n], start=(kc == 0), stop=(kc == 7))
                    return ins
                S.op("pe", mmp, reads=[B_wq[ws]] + B_x1b, writes=[PSA[4 + bk]])
                i0 = j0 // d
                ni = n // d
                src = PSHI[:, ph:ph + n].rearrange("p (i r) -> p r i", r=d)
                if ev % 2 == 0:
                    S.op("act", (lambda dview, src, i0, ni: lambda e: e.activation(out=dview[:, :, i0:i0 + ni], in_=src, func=AF.Copy))(dview, src, i0, ni),
                         reads=[PSA[4 + bk]], writes=[dstB])
                else:
                    S.op("dve", (lambda dview, src, i0, ni: lambda e: e.tensor_copy(out=dview[:, :, i0:i0 + ni], in_=src))(dview, src, i0, ni),
                         reads=[PSA[4 + bk]], writes=[dstB])
                ev += 1
        chunks = []
        for r in range(d):
            for k0 in range(0, Lk, 128):
                chunks.append((r, k0, min(128, Lk - k0)))
        for c0 in range(0, len(chunks), 4):
            grp = chunks[c0:c0 + 4]
            ng = len(grp)
            bk = wk_slot()
            pbase = bk * 1024

            def tr(e, grp=grp, pbase=pbase):
                for q, (r, k0, nk) in enumerate(grp):
                    ins = e.transpose(psT[0:nk, pbase + q * 128:pbase + (q + 1) * 128], vT[:, r * Lk + k0:r * Lk + k0 + nk], ident_bf[:, :])
                return ins
            S.op("pe", tr, reads=[B_v, B_ident], writes=[PSA[4 + bk]])
            srcv = psT[:, pbase:pbase + ng * 128].rearrange("p (q f) -> p q f", f=128)
            S.op("act", (lambda c0, ng, srcv: lambda e: e.activation(out=vaug[:, c0:c0 + ng, 0:64], in_=srcv[:, :, 0:64], func=AF.Copy))(c0, ng, srcv),
                 reads=[PSA[4 + bk]], writes=[B_vaug])
            S.op("dve", (lambda c0, ng, srcv: lambda e: e.tensor_copy(out=vaug[:, c0:c0 + ng, 128:192], in_=srcv[:, :, 64:128]))(c0, ng, srcv),
                 reads=[PSA[4 + bk]], writes=[B_vaug])
        return {"Lq": Lq, "Lk": Lk, "chunks": chunks}

    def attn_core(hp, g, win, d, ctx):
        Lq, Lk, chunks = ctx["Lq"], ctx["Lk"], ctx["chunks"]
        items = []
        for hd in range(2):
            work = []
            for ci, (r, k0, nk) in enumerate(chunks):
                i0 = max(0, k0 - 64)
                i1 = min(Lq, k0 + nk + 64)
                if i1 > i0:
                    work.append((ci, r, k0, nk, i0, i1))
            npair = (len(work) + 1) // 2
            for pi in range(npair):
                items.append({"hd": hd, "pair": work[2 * pi:2 * pi + 2], "last": pi == npair - 1})
        started = {0: [False] * 4, 1: [False] * 4}
        accv = PSLO[:, :].rearrange("p (r i) -> p i r", r=d)

        def emit_scores(it):
            hd = it["hd"]
            pair = it["pair"]
            hg = 2 * hp + hd
            cneg = -float(slopes[g, hg]) * d * 8.0
            hrow = slice(64 * hd, 64 * hd + 64)
            bk = wk_slot()
            ph = bk * 512

            def mms(e, pair=pair, ph=ph, hrow=hrow):
                for q, (ci, r, k0, nk, i0, i1) in enumerate(pair):
                    doff = i0 - (k0 - 64)
                    nq = i1 - i0
                    ins = e.matmul(PSHI[0:nk, ph + q * 256 + doff:ph + q * 256 + doff + nq], lhsT=kT[hrow, r * Lk + k0:r * Lk + k0 + nk],
                                   rhs=qT[hrow, r * Lq + i0:r * Lq + i0 + nq], start=True, stop=True, skip_group_check=True)
                return ins
            S.op("pe", mms, reads=[B_k, B_q], writes=[PSA[4 + bk]])
            ts = rot["tt"] % 2
            rot["tt"] += 1
            ps_ = rot["pt"] % 3
            rot["pt"] += 1
            it["ps"] = ps_
            S.op("dve", (lambda ph, ts, cneg: lambda e: e.scalar_tensor_tensor(
                out=tt[ts][:, :], in0=dtile[:, :], scalar=cneg, in1=PSHI[:, ph:ph + 512], op0=ALU.mult, op1=ALU.add))(ph, ts, cneg),
                reads=[B_const, PSA[4 + bk]], writes=[B_tt[ts]])
            S.op("act", (lambda ts, ps_: lambda e: e.activation(out=PTt[ps_][:, :], in_=tt[ts][:, :], func=AF.Exp, scale=0.125))(ts, ps_),
                 reads=[B_tt[ts]], writes=[B_pt2[ps_]])

        def emit_pv(it):
            hd = it["hd"]
            pair = it["pair"]
            ps_ = it["ps"]
            vsel = (slice(0, 128) if hd == 0 else slice(64, 192))
            st = started[hd]
            plan = []
            wb = set()
            for q, (ci, r, k0, nk, i0, i1) in enumerate(pair):
                doff = i0 - (k0 - 64)
                pos = r * Lq + i0
                end = r * Lq + i1
                while pos < end:
                    bb = pos // 512
                    nx = min(end, (bb + 1) * 512)
                    plan.append((ci, nk, q * 256 + doff + (pos - (r * Lq + i0)), pos, nx - pos, not st[bb]))
                    st[bb] = True
                    wb.add(bb)
                    pos = nx

            def mmv(e, plan=plan, ps_=ps_, vsel=vsel):
                for (ci, nk, pcol, pos, n, stt) in plan:
                    ins = e.matmul(PSLO[:, pos:pos + n], lhsT=vaug[0:nk, ci, vsel], rhs=PTt[ps_][0:nk, pcol:pcol + n], start=stt, stop=True,
                                   skip_group_check=True)
                return ins
            S.op("pe", mmv, reads=[B_vaug, B_pt2[ps_]], writes=[PSA[bb] for bb in sorted(wb)])
            if it["last"]:
                for qt in range(4):
                    ia = 512 * qt // d
                    nn = 512 // d
                    srcp = accv[:, ia:ia + nn, :]
                    dst = osum(hd, qt).rearrange("p (i r) -> p i r", r=d)
                    rb = [PSA[qt]] if d == 1 else [PSA[0], PSA[1], PSA[2], PSA[3]]
                    if g == 0:
                        S.op("act", (lambda dst, srcp: lambda e: e.activation(out=dst, in_=srcp, func=AF.Copy))(dst, srcp),
                             reads=rb, writes=[B_os[hd][qt]])
                    else:
                        S.op("dve", (lambda dst, srcp: lambda e: e.tensor_tensor(out=dst, in0=dst, in1=srcp, op=ALU.add))(dst, srcp),
                             reads=rb + [B_os[hd][qt]], writes=[B_os[hd][qt]])

        LA = 2
        for idx in range(len(items) + LA):
            if idx < len(items):
                emit_scores(items[idx])
            if idx - LA >= 0:
                emit_pv(items[idx - LA])

    def attn_finish(hp):
        for hd in range(2):
            nrow = slice(0, 64) if hd == 0 else slice(64, 128)
            drow = slice(64, 128) if hd == 0 else slice(0, 64)
            osv = xres[:, hd * 4:hd * 4 + 4, OWN:OWN + 512]
            rdv = xres[:, hd * 4:hd * 4 + 4, OWN + 512:OWN + 1024]
            ohv = o_hp[:, :].rearrange("p (q n) -> p q n", q=4)
            allos = [B_os[hd][q] for q in range(4)]
            S.op("act", (lambda osv, rdv, nrow, drow: lambda e: e.activation(out=rdv[nrow], in_=osv[drow], func=AF.Ln))(osv, rdv, nrow, drow),
                 reads=allos, writes=[B_rd[hd]])
            S.op("act", (lambda rdv, nrow: lambda e: e.activation(out=rdv[nrow], in_=rdv[nrow], func=AF.Exp, scale=-1.0))(rdv, nrow),
                 reads=[B_rd[hd]], writes=[B_rd[hd]])
            S.op("dve", (lambda osv, rdv, ohv, nrow: lambda e: e.tensor_tensor(out=ohv[nrow], in0=osv[nrow], in1=rdv[nrow], op=ALU.mult))(osv, rdv, ohv, nrow),
                 reads=allos + [B_rd[hd]], writes=[B_ohp])

    def attn_oproj(hp):
        S.dma("pool", (lambda hp: lambda e: e.dma_start(out=wao[:], in_=w_ao_d[hp]))(hp), "ao", writes=[B_wao])
        for dc in range(8):
            for qt in range(4):
                bk = wk_slot()
                ph = bk * 512

                def mmo2(e, dc=dc, qt=qt, ph=ph):
                    return e.matmul(PSHI[:, ph:ph + 512], lhsT=wao[:, dc * 128:(dc + 1) * 128], rhs=o_hp[:, qt * 512:(qt + 1) * 512], start=True, stop=True)
                S.op("pe", mmo2, reads=[B_wao, B_ohp], writes=[PSA[4 + bk]])
                S.op("dve", (lambda dc, qt, ph: lambda e: e.scalar_tensor_tensor(out=xres[:, dc, qt * 512:(qt + 1) * 512], in0=PSHI[:, ph:ph + 512], scalar=1.0 / ALPHA,
                                                                               in1=xres[:, dc, qt * 512:(qt + 1) * 512], op0=ALU.mult, op1=ALU.add))(dc, qt, ph),
                     reads=[PSA[4 + bk], XO[dc][qt]], writes=[XO[dc][qt]])
    ctx_next = attn_proj(0, 0, DIL[0][0], DIL[0][1])
    for hp in range(8):
        for g, (win, d) in enumerate(DIL):
            ctx = ctx_next
            attn_core(hp, g, win, d, ctx)
            if g < 2:
                ctx_next = attn_proj(hp, g + 1, DIL[g + 1][0], DIL[g + 1][1])
        attn_finish(hp)
        if hp < 7:
            ctx_next = attn_proj(hp + 1, 0, DIL[0][0], DIL[0][1])
        attn_oproj(hp)
    S.barrier()
    for t in range(2):
        t0, nt = TILES[t]
        ln_cols(1, 1, t0, nt, XR[t])
    S.barrier()
    if dbg == "l1attn":
        return dump_and_finish()
    ffn_jobs([(1, 1, 2, t) for t in range(2)])
    return dump_and_finish()


def _prep_shared(ffn1_w_gate, ffn1_w_up, ffn1_w_down, ffn2_w_gate, ffn2_w_up, ffn2_w_down,
                 ln_gain, ln_bias, pool_w_in, pool_w_group, pool_scale, pool_w_out, attn_w_qkv, attn_w_out):
    f = np.float32
    gates = [ffn1_w_gate, ffn2_w_gate]
    ups = [ffn1_w_up, ffn2_w_up]
    downs = [ffn1_w_down, ffn2_w_down]
    w_gu = np.empty((2, 2, NFC, 128, 2, 8, 128), f)
    w_d = np.empty((2, 2, 8, 128, NFC, 128), f)
    for l in range(2):
        for fi in range(2):
            g = np.asarray(gates[fi][l], f).reshape(8, 128, NFC, 128)
            u = np.asarray(ups[fi][l], f).reshape(8, 128, NFC, 128)
            w_gu[l, fi, :, :, 0] = g.transpose(2, 1, 0, 3)
            w_gu[l, fi, :, :, 1] = u.transpose(2, 1, 0, 3)
            dn = np.asarray(downs[fi][l], f).reshape(NFC, 128, 8, 128)
            w_d[l, fi] = dn.transpose(2, 1, 0, 3)
    lnp = np.empty((128, 2, 3, 2, 8), f)
    lnp[:, :, :, 0, :] = np.asarray(ln_gain, f).reshape(2, 3, 8, 128).transpose(3, 0, 1, 2)
    lnp[:, :, :, 1, :] = np.asarray(ln_bias, f).reshape(2, 3, 8, 128).transpose(3, 0, 1, 2)
    w_pin = np.asarray(pool_w_in[0], f).reshape(8, 128, 1024).transpose(1, 0, 2)
    w_pout = np.asarray(pool_w_out[0], f).reshape(8, 128, 1024).transpose(1, 0, 2)
    w_pgrp = np.asarray(pool_w_group[0], f).reshape(4, 2, 128, 256).transpose(2, 0, 1, 3)
    pscale = np.asarray(pool_scale[0], f).reshape(8, 128).T
    wq = np.asarray(attn_w_qkv[0], f).reshape(8, 128, 3, 3, 8, 128)
    w_qkv = wq.transpose(4, 2, 1, 3, 0, 5)
    w_ao = np.asarray(attn_w_out[0], f).reshape(8, 128, 1024)
    r = np.arange(128)[:, None]
    i = np.arange(256)[None, :]
    dd = np.abs(r - i + 64).astype(f)
    dtile = np.where(dd <= 64, dd, BIGD).astype(f)
    dtile = np.concatenate([dtile, dtile], axis=1)
    c = np.ascontiguousarray
    return {
        "lnp": c(lnp.reshape(128, 96)), "pscale": c(pscale), "dtile": c(dtile), "ident": np.eye(128, dtype=f),
        "w_gu": c(w_gu.reshape(88, 128, 2048)), "w_d": c(w_d.reshape(32, 128, 2816)),
        "w_pin": c(w_pin.reshape(128, 8192)), "w_pout": c(w_pout.reshape(128, 8192)), "w_pgrp": c(w_pgrp.reshape(128, 2048)),
        "w_qkv": c(w_qkv.reshape(24, 128, 3072)), "w_ao": c(w_ao),
    }


def _prep_core(x, core):
    f = np.float32
    b, half = core // 2, core % 2
    xs = np.asarray(x[b], f)
    if half == 0:
        loc = xs[0:TC]
    else:
        loc = xs[::-1][0:TC]
    xT = np.ascontiguousarray(loc.T.reshape(8, 128, TC).transpose(1, 0, 2))
    pflag = np.zeros((128, 2), f)
    pflag[:, half] = 1.0
    pinv = np.empty((128, 8, 8), f)
    for cidx in range(8):
        h = 1 << (cidx // 2)
        w = 2 * h
        for jo in range(8):
            if half == 0:
                cnt = h + min(h, jo)
            else:
                cnt = h + min(h, jo + 1)
            pinv[:, cidx, jo] = 1.0 / cnt
    return {"xT": xT, "pflag": pflag, "pinv": np.ascontiguousarray(pinv.reshape(128, 64))}


_NC_CACHE = {}


def _get_nc(dbg=None):
    if dbg not in _NC_CACHE:
        _NC_CACHE[dbg] = build_program(dbg)
    return _NC_CACHE[dbg]


def run_cores(inputs, dbg=None, cores=range(8), trace=False):
    x = np.asarray(inputs["x"], np.float32)
    shared = _prep_shared(**{k: v for k, v in inputs.items() if k != "x"})
    in_maps = []
    for core in cores:
        m = dict(shared)
        m.update(_prep_core(x, core))
        in_maps.append(m)
    nc = _get_nc(dbg)
    res = run_bass_kernel_spmd(nc, in_maps, core_ids=list(range(len(in_maps))), **({"trace": True} if trace else {}))
    return res


def kernel(**inputs):
    res = run_cores(inputs)
    out = np.empty((BATCH, SEQ, D_MODEL), np.float32)
    for core in range(8):
        yT = np.asarray(res.results[core]["yT"], np.float32)
        y = yT.transpose(2, 1, 0).reshape(OWN, D_MODEL)
        b, half = core // 2, core % 2
        if half == 0:
            out[b, 0:OWN] = y
        else:
            out[b, OWN:SEQ] = y[::-1]
    return out
```

```python
import numpy as np
import concourse.bass as bass
import concourse.mybir as mybir
from concourse.bass_utils import run_bass_kernel_spmd

F32 = mybir.dt.float32
BF16 = mybir.dt.bfloat16
AF = mybir.ActivationFunctionType
ALU = mybir.AluOpType

D_MODEL = 1024
SEQ = 4096
BATCH = 4
D_FF = 2816
NFC = 22
OWN = 2048
HALO = 1024
TC = OWN + 8
TH = OWN + HALO
ALPHA = 4.0 ** 0.25
LN_EPS = 1e-5
EPS_P = LN_EPS / (ALPHA * ALPHA)
DIL = ((128, 1), (512, 4), (2048, 16))
BIGD = 1.0e5

ENGS = ("pe", "act", "dve", "pool", "sp")


class Buf:
    __slots__ = ("name", "wdep", "rdeps")

    def __init__(self, name):
        self.name = name
        self.wdep = None
        self.rdeps = {}


class Sched:
    def __init__(self, nc):
        self.nc = nc
        self.sem = {}
        self.cnt = {}
        for e in ENGS:
            self.sem[e] = nc.alloc_semaphore("s_" + e)
            self.cnt[e] = 0
        self.seen = {e: {} for e in ENGS}
        self.ops = {e: [] for e in ENGS}

    def new_dma_sem(self, key):
        self.sem[key] = self.nc.alloc_semaphore("d_" + key)
        self.cnt[key] = 0
        return key

    def _collect(self, reads, writes):
        w = {}
        for b in reads:
            d = b.wdep
            if d is not None and w.get(d[0], 0) < d[1]:
                w[d[0]] = d[1]
        for b in writes:
            d = b.wdep
            if d is not None and w.get(d[0], 0) < d[1]:
                w[d[0]] = d[1]
            for k, v in b.rdeps.items():
                if w.get(k, 0) < v:
                    w[k] = v
        return w

    def _need(self, e, w):
        need = []
        s = self.seen[e]
        for k, v in w.items():
            if s.get(k, 0) < v:
                need.append((k, v))
                s[k] = v
        return need

    def op(self, e, fn, reads=(), writes=()):
        need = self._need(e, self._collect(reads, writes))
        self.cnt[e] += 1
        v = self.cnt[e]
        self.ops[e].append((need, fn, (e, 1)))
        for b in reads:
            if b.rdeps.get(e, 0) < v:
                b.rdeps[e] = v
        for b in writes:
            b.wdep = (e, v)
            b.rdeps = {}
        return (e, v)

    def dma(self, q, fn, semkey, reads=(), writes=()):
        need = self._need(q, self._collect(reads, writes))
        self.cnt[semkey] += 16
        v = self.cnt[semkey]
        self.ops[q].append((need, fn, (semkey, 16)))
        for b in reads:
            if b.rdeps.get(semkey, 0) < v:
                b.rdeps[semkey] = v
        for b in writes:
            b.wdep = (semkey, v)
            b.rdeps = {}
        return (semkey, v)

    def wait_all(self, e, keys):
        w = {k: self.cnt[k] for k in keys if self.cnt[k] > 0}
        need = self._need(e, w)
        if need:
            self.ops[e].append((need, None, None))

    def barrier(self, engines=("pe", "act", "dve", "pool")):
        for e in engines:
            self.wait_all(e, list(engines))

    def emit(self, block):
        sched = self

        def make(e):
            def body(eng):
                for need, fn, inc in sched.ops[e]:
                    for k, v in need:
                        eng.wait_ge(sched.sem[k], v)
                    if fn is not None:
                        ins = fn(eng)
                        ins.then_inc(sched.sem[inc[0]], inc[1])
            return body
        block.tensor(make("pe"))
        block.scalar(make("act"))
        block.vector(make("dve"))
        block.gpsimd(make("pool"))
        block.sync(make("sp"))


def _halves(nt):
    return [(h0, min(512, nt - h0)) for h0 in range(0, nt, 512)]


def build_program(dbg=None):
    nc = bass.Bass("TRN2", target_bir_lowering=False)
    S = Sched(nc)

    def din(name, shape):
        return nc.dram_tensor(name, list(shape), F32, kind="ExternalInput").ap()

    xT_d = din("xT", (128, 8, TC))
    lnp_d = din("lnp", (128, 96))
    pflag_d = din("pflag", (128, 2))
    pinv_d = din("pinv", (128, 64))
    pscale_d = din("pscale", (128, 8))
    dtile_d = din("dtile", (128, 512))
    ident_d = din("ident", (128, 128))
    w_gu_d = din("w_gu", (88, 128, 2048))
    w_d_d = din("w_d", (32, 128, 2816))
    w_pin_d = din("w_pin", (128, 8192))
    w_pout_d = din("w_pout", (128, 8192))
    w_pgrp_d = din("w_pgrp", (128, 2048))
    w_qkv_d = din("w_qkv", (24, 128, 3072))
    w_ao_d = din("w_ao", (8, 128, 1024))
    if dbg is None:
        y_d = nc.dram_tensor("yT", [128, 8, OWN], F32, kind="ExternalOutput").ap()
    else:
        y_d = nc.dram_tensor("yT", [128, 8, TC], F32, kind="ExternalOutput").ap()

    base = (nc.sbuf_base + 31) // 32 * 32
    OFF_X = 0
    OFF_OS = 65792
    OFF_C = 98560
    OFF_R1 = OFF_C + 4096
    OFF_RING = OFF_R1 + 61440
    OFF_LN = OFF_RING + 23552
    OFF_SP = OFF_LN + 16384
    ARENA = OFF_SP + 4608
    arena = nc.alloc_sbuf_tensor("arena", [128, ARENA // 4], F32)
    abase = base
    assert nc.sbuf_base >= abase + ARENA

    def at(name, shape, dtype, off):
        return nc.alloc_sbuf_tensor_at(name, list(shape), dtype, offset=abase + off)

    xres = at("xres", (128, 8, TC), F32, OFF_X)
    os_t = at("os_t", (128, 8, 1024), F32, OFF_OS)
    lnp = at("lnp", (128, 96), F32, OFF_C)
    pflag = at("pflag", (128, 2), F32, OFF_C + 384)
    pinv = at("pinv", (128, 64), F32, OFF_C + 416)
    pscale = at("pscale", (128, 8), F32, OFF_C + 672)
    dtile = at("dtile", (128, 512), F32, OFF_C + 704)
    ones_bf = at("ones_bf", (128, 128), BF16, OFF_C + 2752)
    ident_bf = at("ident_bf", (128, 128), BF16, OFF_C + 3008)
    hsave = at("hsave", (128, 8, 8), F32, OFF_C + 3264)
    epst = at("epst", (128, 1), F32, OFF_C + 3520)
    negone = at("negone", (128, 1), F32, OFF_C + 3552)
    hbuf = at("hbuf", (128, NFC, 1024), BF16, OFF_R1)
    xb = at("xb", (128, 8, 1024), BF16, OFF_R1 + 45056)
    gu_slot = [at(f"gu{i}", (128, 2, 8, 128), BF16, OFF_RING + i * 4096) for i in range(3)]
    d_slot = [at(f"dw{i}", (128, NFC, 128), BF16, OFF_RING + 12288 + i * 5632) for i in range(2)]
    zb = at("zb", (128, 2, 1024), BF16, OFF_LN)
    zq = at("zq", (128, 2, 1024), BF16, OFF_LN + 4096)
    meant = at("meant", (128, 1024), F32, OFF_LN + 8192)
    rstdt = at("rstdt", (128, 1024), F32, OFF_LN + 12288)
    sgt = at("sgt", (128, 1024), F32, OFF_SP)
    p_win = at("p_win", (128, 8, 1024), BF16, OFF_R1)
    p_wout = at("p_wout", (128, 8, 1024), BF16, OFF_R1 + 16384)
    p_wgrp = at("p_wgrp", (128, 4, 2, 256), BF16, OFF_R1 + 32768)
    p_xb = at("p_xb", (128, 8, 512), BF16, OFF_R1 + 36864)
    p_yb = at("p_yb", (128, 8, 512), BF16, OFF_R1 + 45056)
    p_mx = at("p_mx", (128, 8, 512), BF16, OFF_R1 + 53248)
    p_u = at("p_u", (128, 8, 528), F32, OFF_RING)
    p_t = [at(f"p_t{i}", (128, 528), F32, OFF_RING + 16896 + i * 2112) for i in range(2)]
    p_wn = at("p_wn", (128, 512), F32, OFF_RING + 21120)
    x1b = at("x1b", (128, 8, TH), BF16, OFF_R1)
    qT = at("qT", (128, OWN), BF16, OFF_R1 + 49152)
    kT = at("kT", (128, TH), BF16, OFF_R1 + 53248)
    PTt = [at(f"PT{i}", (128, 512), BF16, OFF_RING + 20480 + i * 1024) for i in range(3)]
    wqkv2 = [at(f"wqkv{i}", (128, 3, 8, 128), BF16, OFF_RING + i * 6144) for i in range(2)]
    vT = at("vT", (128, TH), BF16, OFF_RING + 12288)
    wao = at("wao", (128, 1024), BF16, OFF_RING + 18432)
    vaug = at("vaug", (128, 32, 192), BF16, OFF_LN)
    o_hp = at("o_hp", (128, OWN), BF16, OFF_LN + 12288)
    tt = [at(f"tt{i}", (128, 512), F32, OFF_SP + i * 2048) for i in range(2)]

    PSLO = nc.alloc_psum_tensor("pslo", [128, 2048], F32)
    PSHI = nc.alloc_psum_tensor("pshi", [128, 2048], F32)
    PS = [PSLO[:, 0:1024], PSLO[:, 1024:2048], PSHI[:, 0:1024], PSHI[:, 1024:2048]]
    PSB = [Buf(f"ps{i}") for i in range(4)]

    for k in ["x0", "x1", "x2", "xc", "xs0", "xs1", "const", "gu0", "gu1", "gu2", "dw0", "dw1", "pw", "aw0", "aw1", "ao", "out"]:
        S.new_dma_sem(k)

    B_const = Buf("const")
    B_ident = Buf("ident")
    B_ones = Buf("ones")
    TILES = [(0, 1024), (1024, 1024), (2048, 8)]
    XR = [[Buf(f"xr{t}_{c}") for c in range(8)] for t in range(3)]
    XB = [Buf(f"xb{c}") for c in range(8)]
    HB = [Buf(f"hb{f}") for f in range(NFC)]
    GU = [Buf(f"gu{i}") for i in range(3)]
    DW = [Buf(f"dw{i}") for i in range(2)]
    B_sg = Buf("sg")
    B_zb = [Buf("zb0"), Buf("zb1")]
    B_zq = [Buf("zq0"), Buf("zq1")]
    B_mean = Buf("mean")
    B_rstd = Buf("rstd")
    ring = {"gu": 0, "dw": 0}

    for (dst, src) in [(lnp, lnp_d), (pflag, pflag_d), (pinv, pinv_d), (pscale, pscale_d), (dtile, dtile_d)]:
        S.dma("sp", (lambda dst, src: lambda e: e.dma_start(out=dst[:], in_=src))(dst, src), "const", writes=[B_const])
    B_const.wdep = ("const", S.cnt["const"])
    S.dma("pool", lambda e: e.dma_start(out=ident_bf[:], in_=ident_d), "pw", writes=[B_ident])
    S.op("dve", lambda e: e.memset(ones_bf[:], 1.0), writes=[B_ones])
    S.op("dve", lambda e: e.memset(negone[:], -1.0), writes=[B_const])
    S.op("dve", lambda e: e.memset(epst[:], EPS_P), writes=[B_const])
    B_const.wdep = ("const", S.cnt["const"])
    B_eps = Buf("eps")
    B_eps.wdep = ("dve", S.cnt["dve"])
    for t, (t0, nt) in enumerate(TILES):
        S.dma("sp", (lambda t0, nt: lambda e: e.dma_start(out=xres[:, :, t0:t0 + nt], in_=xT_d[:, :, t0:t0 + nt]))(t0, nt),
              f"x{t}", writes=XR[t])

    def ln_stats_chunk(c, t0, nt, xr_bufs):
        hv = _halves(nt)
        sl = c % 2
        S.op("act", (lambda c, sl: lambda e: e.activation(out=zb[:, sl, 0:nt], in_=xres[:, c, t0:t0 + nt], func=AF.Copy))(c, sl),
             reads=[xr_bufs[c]], writes=[B_zb[sl]])
        S.op("act", (lambda c, sl: lambda e: e.activation(out=zq[:, sl, 0:nt], in_=xres[:, c, t0:t0 + nt], func=AF.Square))(c, sl),
             reads=[xr_bufs[c]], writes=[B_zq[sl]])

        def mm(e, c=c, sl=sl):
            for (h0, hn) in hv:
                e.matmul(PS[0][:, h0:h0 + hn], lhsT=ones_bf[:, :], rhs=zb[:, sl, h0:h0 + hn], start=(c == 0), stop=(c == 7))
            for (h0, hn) in hv:
                ins = e.matmul(PS[1][:, h0:h0 + hn], lhsT=ones_bf[:, :], rhs=zq[:, sl, h0:h0 + hn], start=(c == 0), stop=(c == 7))
            return ins
        S.op("pe", mm, reads=[B_zb[sl], B_zq[sl], B_ones], writes=[PSB[0], PSB[1]])

    def ln_cols(l, s, t0, nt, xr_bufs):
        for c in range(8):
            ln_stats_chunk(c, t0, nt, xr_bufs)
        ln_finish(l, s, t0, nt, xr_bufs)

    def ln_finish_head(nt):
        S.op("dve", lambda e: e.tensor_scalar(out=meant[:, 0:nt], in0=PS[0][:, 0:nt], scalar1=1.0 / D_MODEL, scalar2=None, op0=ALU.mult),
             reads=[PSB[0]], writes=[B_mean])
        S.op("dve", lambda e: e.tensor_tensor(out=rstdt[:, 0:nt], in0=meant[:, 0:nt], in1=meant[:, 0:nt], op=ALU.mult),
             reads=[B_mean], writes=[B_rstd])
        S.op("dve", lambda e: e.scalar_tensor_tensor(out=rstdt[:, 0:nt], in0=PS[1][:, 0:nt], scalar=1.0 / D_MODEL, in1=rstdt[:, 0:nt],
                                                     op0=ALU.mult, op1=ALU.subtract),
             reads=[PSB[1], B_rstd], writes=[B_rstd])
        S.op("act", lambda e: e.activation(out=rstdt[:, 0:nt], in_=rstdt[:, 0:nt], func=AF.Ln, bias=epst[:, 0:1], scale=1.0),
             reads=[B_rstd, B_eps], writes=[B_rstd])
        S.op("act", lambda e: e.activation(out=rstdt[:, 0:nt], in_=rstdt[:, 0:nt], func=AF.Exp, scale=-0.5),
             reads=[B_rstd], writes=[B_rstd])

    def ln_finish_chunk(l, s, c, t0, nt, xr_bufs):
        gi = (l * 3 + s) * 16
        S.op("dve", lambda e: e.tensor_tensor(out=xres[:, c, t0:t0 + nt], in0=xres[:, c, t0:t0 + nt], in1=meant[:, 0:nt], op=ALU.subtract),
             reads=[xr_bufs[c], B_mean], writes=[xr_bufs[c]])
        S.op("dve", lambda e: e.tensor_tensor(out=xres[:, c, t0:t0 + nt], in0=xres[:, c, t0:t0 + nt], in1=rstdt[:, 0:nt], op=ALU.mult),
             reads=[xr_bufs[c], B_rstd], writes=[xr_bufs[c]])
        S.op("act", lambda e: e.activation(out=xres[:, c, t0:t0 + nt], in_=xres[:, c, t0:t0 + nt], func=AF.Identity,
                                           bias=lnp[:, gi + 8 + c:gi + 9 + c], scale=lnp[:, gi + c:gi + c + 1]),
             reads=[xr_bufs[c], B_const], writes=[xr_bufs[c]])

    def ln_finish(l, s, t0, nt, xr_bufs):
        ln_finish_head(nt)
        for c in range(8):
            ln_finish_chunk(l, s, c, t0, nt, xr_bufs)

    def ffn_cast(t):
        t0, nt = TILES[t]
        xr_bufs = XR[t]
        for c in range(8):
            eng = "dve" if c % 2 == 0 else "act"
            if eng == "dve":
                S.op("dve", (lambda c: lambda e: e.tensor_copy(out=xb[:, c, 0:nt], in_=xres[:, c, t0:t0 + nt]))(c), reads=[xr_bufs[c]], writes=[XB[c]])
            else:
                S.op("act", (lambda c: lambda e: e.activation(out=xb[:, c, 0:nt], in_=xres[:, c, t0:t0 + nt], func=AF.Copy))(c), reads=[xr_bufs[c]], writes=[XB[c]])

    def ffn_tile(l, fi, s, t, xr_bufs, do_cast=True, next_t=None, deferred=None, defer=False):
        t0, nt = TILES[t]
        hv = _halves(nt)
        if do_cast:
            ffn_cast(t)
        wbase = (l * 2 + fi) * NFC
        for fc in range(NFC):
            sl = ring["gu"] % 3
            ring["gu"] += 1
            S.dma("pool", (lambda sl, idx: lambda e: e.dma_start(out=gu_slot[sl][:], in_=w_gu_d[idx].rearrange("p (a k f) -> p a k f", a=2, k=8)))(sl, wbase + fc),
                  f"gu{sl}", writes=[GU[sl]])
            pg, pu = (2, 3) if fc % 2 == 0 else (0, 1)

            def mm(e, sl=sl, pg=pg, pu=pu):
                for (pp, a) in ((pg, 0), (pu, 1)):
                    for (h0, hn) in hv:
                        for kc in range(8):
                            ins = e.matmul(PS[pp][:, h0:h0 + hn], lhsT=gu_slot[sl][:, a, kc, :], rhs=xb[:, kc, h0:h0 + hn], start=(kc == 0), stop=(kc == 7))
                return ins
            S.op("pe", mm, reads=[GU[sl]] + XB, writes=[PSB[pg], PSB[pu]])
            S.op("act", (lambda pg: lambda e: e.activation(out=sgt[:, 0:nt], in_=PS[pg][:, 0:nt], func=AF.Silu))(pg), reads=[PSB[pg]], writes=[B_sg])
            S.op("dve", (lambda pu, fc: lambda e: e.tensor_tensor(out=hbuf[:, fc, 0:nt], in0=sgt[:, 0:nt], in1=PS[pu][:, 0:nt], op=ALU.mult))(pu, fc),
                 reads=[B_sg, PSB[pu]], writes=[HB[fc]])
            if deferred:
                deferred.pop(0)()
        while deferred:
            deferred.pop(0)()
        if next_t is not None:
            ffn_cast(next_t)
        dbase = (l * 2 + fi) * 8
        for dc in range(8):
            sl = ring["dw"] % 2
            ring["dw"] += 1
            S.dma("pool", (lambda sl, idx: lambda e: e.dma_start(out=d_slot[sl][:], in_=w_d_d[idx].rearrange("p (f d) -> p f d", f=NFC)))(sl, dbase + dc),
                  f"dw{sl}", writes=[DW[sl]])
            py = 2 + dc % 2

            def mm2(e, sl=sl, py=py):
                for (h0, hn) in hv:
                    for fc in range(NFC):
                        ins = e.matmul(PS[py][:, h0:h0 + hn], lhsT=d_slot[sl][:, fc, :], rhs=hbuf[:, fc, h0:h0 + hn], start=(fc == 0), stop=(fc == NFC - 1))
                return ins
            S.op("pe", mm2, reads=[DW[sl]] + HB, writes=[PSB[py]])
            S.op("dve", (lambda dc, py: lambda e: e.scalar_tensor_tensor(out=xres[:, dc, t0:t0 + nt], in0=PS[py][:, 0:nt], scalar=0.5 / ALPHA,
                                                                           in1=xres[:, dc, t0:t0 + nt], op0=ALU.mult, op1=ALU.add))(dc, py),
                 reads=[PSB[py], xr_bufs[dc]], writes=[xr_bufs[dc]])
            if dc >= 1:
                ln_stats_chunk(dc - 1, t0, nt, xr_bufs)
        ln_stats_chunk(7, t0, nt, xr_bufs)
        ln_finish_head(nt)
        pieces = [(lambda c: lambda: ln_finish_chunk(l, s, c, t0, nt, xr_bufs))(c) for c in range(8)]
        if defer:
            return pieces
        for p in pieces:
            p()
        return None

    def dump_and_finish():
        ncols = OWN if dbg is None else TC
        S.barrier()
        S.wait_all("sp", ["pe", "act", "dve"])
        for c in range(8):
            S.dma("sp", (lambda c: lambda e: e.dma_start(out=y_d[:, c, :], in_=xres[:, c, 0:ncols]))(c), "out",
                  reads=[b for t in range(3) for b in XR[t]])
        S.wait_all("sp", ["out"])
        S.wait_all("pool", ["gu0", "gu1", "gu2", "dw0", "dw1", "pw", "aw0", "aw1", "ao"])
        with nc.Block() as block:
            S.emit(block)
        return nc

    def ffn_jobs(jobs, dbg_after=None):
        pend = None
        for i, (l, fi, s_, t) in enumerate(jobs):
            nxt = jobs[i + 1][3] if i + 1 < len(jobs) else None
            pend = ffn_tile(l, fi, s_, t, XR[t], do_cast=(i == 0), next_t=nxt, deferred=(pend if i > 0 else None), defer=(nxt is not None))

    ffn_jobs([(0, 0, 0, t) for t in range(3)])
    if dbg == "l0ffn1":
        return dump_and_finish()

    S.barrier()
    ALLX = [b for t in range(3) for b in XR[t]]
    B_pw = Buf("pw")
    S.dma("pool", lambda e: e.dma_start(out=p_win[:], in_=w_pin_d.rearrange("p (k e) -> p k e", k=8)), "pw", writes=[B_pw])
    S.dma("pool", lambda e: e.dma_start(out=p_wout[:], in_=w_pout_d.rearrange("p (k e) -> p k e", k=8)), "pw", writes=[B_pw])
    S.dma("pool", lambda e: e.dma_start(out=p_wgrp[:], in_=w_pgrp_d.rearrange("p (g c e) -> p g c e", g=4, c=2)), "pw", writes=[B_pw])
    B_pw.wdep = ("pw", S.cnt["pw"])
    B_pxb = [Buf(f"pxb{c}") for c in range(8)]
    B_pu = [Buf(f"pu{c}") for c in range(8)]
    B_pmx = [Buf(f"pmx{c}") for c in range(8)]
    B_pyb = [Buf(f"pyb{c}") for c in range(8)]
    B_pt = [Buf("pt0"), Buf("pt1")]
    B_pwn = Buf("pwn")
    B_hs = Buf("hsave")
    S.op("dve", lambda e: e.memset(p_u[:, :, 0:8], 0.0), writes=B_pu)
    TP = 448
    ptiles = [(a, min(a + TP, OWN)) for a in range(0, OWN, TP)]
    XP = [Buf(f"xp{c}") for c in range(8)]
    def pool_tile(ti, a, b):
        n_o = b - a
        first = (ti == 0)
        ua = 0 if first else a - 8
        ub = b + 8
        n_u = ub - ua
        loff = 8 if first else 0
        for c in range(8):
            if first:
                S.op("act", (lambda c: lambda e: e.activation(out=p_xb[:, c, 0:n_u], in_=xres[:, c, ua:ub], func=AF.Copy))(c), reads=[XP[c]], writes=[B_pxb[c]])
            else:
                S.op("act", (lambda c: lambda e: e.activation(out=p_xb[:, c, 0:8], in_=hsave[:, c, :], func=AF.Copy))(c), reads=[B_hs], writes=[B_pxb[c]])
                S.op("act", (lambda c: lambda e: e.activation(out=p_xb[:, c, 8:n_u], in_=xres[:, c, a:ub], func=AF.Copy))(c), reads=[XP[c]], writes=[B_pxb[c]])
        S.op("dve", lambda e: e.tensor_copy(out=hsave[:, :, :], in_=xres[:, :, b - 8:b]), reads=XP + B_pxb, writes=[B_hs])
        for c in range(8):
            pp = c % 4

            def mmu(e, c=c, pp=pp):
                for kc in range(8):
                    ins = e.matmul(PS[pp][:, 0:n_u], lhsT=p_win[:, kc, c * 128:(c + 1) * 128], rhs=p_xb[:, kc, 0:n_u], start=(kc == 0), stop=(kc == 7))
                return ins
            S.op("pe", mmu, reads=[B_pw] + B_pxb, writes=[PSB[pp]])
            S.op("act", (lambda c, pp: lambda e: e.activation(out=p_u[:, c, loff:loff + n_u], in_=PS[pp][:, 0:n_u], func=AF.Copy))(c, pp),
                 reads=[PSB[pp]], writes=[B_pu[c]])
        n_loc = n_u + loff
        lo0 = 8
        for c in range(8):
            g = c // 2
            h = 1 << g
            w = 2 * h
            src = p_u[:, c, :]
            srcb = B_pu[c]
            ln = 1
            k = 0
            while ln < w:
                dst = p_t[k % 2]
                cnt = n_loc - 2 * ln + 1
                S.op("dve", (lambda src, dst, ln, cnt: lambda e: e.tensor_tensor(out=dst[:, 0:cnt], in0=src[:, 0:cnt], in1=src[:, ln:ln + cnt], op=ALU.add))(src, dst, ln, cnt),
                     reads=[srcb], writes=[B_pt[k % 2]])
                src = dst
                srcb = B_pt[k % 2]
                ln *= 2
                k += 1
            s0 = lo0 - h
            S.op("dve", (lambda src, s0: lambda e: e.tensor_scalar(out=p_wn[:, 0:n_o], in0=src[:, s0 + 1:s0 + 1 + n_o], scalar1=pflag[:, 1:2], scalar2=None, op0=ALU.mult))(src, s0),
                 reads=[srcb, B_const], writes=[B_pwn])
            S.op("dve", (lambda src, s0: lambda e: e.scalar_tensor_tensor(out=p_wn[:, 0:n_o], in0=src[:, s0:s0 + n_o], scalar=pflag[:, 0:1], in1=p_wn[:, 0:n_o],
                                                                            op0=ALU.mult, op1=ALU.add))(src, s0),
                 reads=[srcb, B_const, B_pwn], writes=[B_pwn])
            S.op("dve", (lambda c, w: lambda e: e.scalar_tensor_tensor(out=p_mx[:, c, 0:n_o], in0=p_wn[:, 0:n_o], scalar=1.0 / w, in1=p_u[:, c, lo0:lo0 + n_o],
                                                                         op0=ALU.mult, op1=ALU.subtract))(c, w),
                 reads=[B_pwn, B_pu[c]], writes=[B_pmx[c]])
            if first:
                S.op("dve", (lambda c: lambda e: e.tensor_tensor(out=p_wn[:, 0:8], in0=p_wn[:, 0:8], in1=pinv[:, c * 8:c * 8 + 8], op=ALU.mult))(c),
                     reads=[B_pwn, B_const], writes=[B_pwn])
                S.op("dve", (lambda c: lambda e: e.tensor_tensor(out=p_mx[:, c, 0:8], in0=p_wn[:, 0:8], in1=p_u[:, c, lo0:lo0 + 8], op=ALU.subtract))(c),
                     reads=[B_pwn, B_pu[c]], writes=[B_pmx[c]])
        for ec in range(8):
            g = ec // 2
            eh = ec % 2
            pp = ec % 4

            def mmg(e, g=g, eh=eh, pp=pp):
                for cc in range(2):
                    ins = e.matmul(PS[pp][:, 0:n_o], lhsT=p_wgrp[:, g, cc, eh * 128:(eh + 1) * 128], rhs=p_mx[:, 2 * g + cc, 0:n_o], start=(cc == 0), stop=(cc == 1))
                return ins
            S.op("pe", mmg, reads=[B_pw, B_pmx[2 * g], B_pmx[2 * g + 1]], writes=[PSB[pp]])
            S.op("act", (lambda ec, pp: lambda e: e.activation(out=p_yb[:, ec, 0:n_o], in_=PS[pp][:, 0:n_o], func=AF.Identity, scale=pscale[:, ec:ec + 1]))(ec, pp),
                 reads=[PSB[pp], B_const], writes=[B_pyb[ec]])
        if dbg == "pooldbg" and ti == 0:
            S.barrier()
            Bd = Buf("dbgout")
            S.dma("pool", lambda e: e.dma_start(out=y_d[:, :, 0:528], in_=p_u[:, :, :]), "out", reads=B_pu)
            S.dma("pool", lambda e: e.dma_start(out=y_d[:, :, 600:1048], in_=p_mx[:, :, 0:448]), "out", reads=B_pmx)
            S.dma("pool", lambda e: e.dma_start(out=y_d[:, :, 1100:1548], in_=p_yb[:, :, 0:448]), "out", reads=B_pyb)
            S.dma("pool", lambda e: e.dma_start(out=y_d[:, :, 1600:2056], in_=p_xb[:, :, 0:456]), "out", reads=B_pxb)
            S.dma("pool", lambda e: e.dma_start(out=y_d[:, :, 2056:3080], in_=p_wout[:, :, :]), "out", reads=[B_pw])
            S.wait_all("pool", ["out"])
            with nc.Block() as block:
                S.emit(block)
            return nc
        for dc in range(8):
            pp = dc % 4

            def mmo(e, dc=dc, pp=pp):
                for ec in range(8):
                    ins = e.matmul(PS[pp][:, 0:n_o], lhsT=p_wout[:, ec, dc * 128:(dc + 1) * 128], rhs=p_yb[:, ec, 0:n_o], start=(ec == 0), stop=(ec == 7))
                return ins
            S.op("pe", mmo, reads=[B_pw] + B_pyb, writes=[PSB[pp]])
            S.op("dve", (lambda dc, pp: lambda e: e.scalar_tensor_tensor(out=xres[:, dc, a:b], in0=PS[pp][:, 0:n_o], scalar=1.0 / ALPHA, in1=xres[:, dc, a:b],
                                                                           op0=ALU.mult, op1=ALU.add))(dc, pp),
                 reads=[PSB[pp], XP[dc], B_hs], writes=[XP[dc]])
        if dbg == "pooldbg2" and ti == 0:
            S.barrier()
            for c in range(8):
                S.dma("sp", (lambda c: lambda e: e.dma_start(out=y_d[:, c, 0:448], in_=xres[:, c, 0:448]))(c), "out", reads=XP)
            S.wait_all("sp", ["out"])
            with nc.Block() as block:
                S.emit(block)
            return nc
        ln_cols(0, 1, a, n_o, XP)
        if dbg is not None and dbg.startswith("pooldbg3:") and ti == int(dbg.split(":")[1]):
            S.barrier()
            for c in range(8):
                S.dma("sp", (lambda c: lambda e: e.dma_start(out=y_d[:, c, :], in_=xres[:, c, :]))(c), "out", reads=XP)
            S.wait_all("sp", ["out"])
            with nc.Block() as block:
                S.emit(block)
            return nc
        return None
    for ti, (a, b) in enumerate(ptiles):
        _r = pool_tile(ti, a, b)
        if _r is not None:
            return _r
    S.barrier()
    if dbg == "l0pool":
        return dump_and_finish()

    if dbg == "l0":
        ffn_jobs([(0, 1, 2, t) for t in range(2)])
        return dump_and_finish()
    ffn_jobs([(0, 1, 2, t) for t in range(2)] + [(1, 0, 0, t) for t in range(2)])
    S.barrier()
    if dbg == "l1ffn1":
        return dump_and_finish()

    B_x1b = [Buf(f"x1b{c}") for c in range(8)]
    for c in range(8):
        if c % 2 == 0:
            S.op("dve", (lambda c: lambda e: e.tensor_copy(out=x1b[:, c, 0:OWN], in_=xres[:, c, 0:OWN]))(c), reads=ALLX, writes=[B_x1b[c]])
        else:
            S.op("act", (lambda c: lambda e: e.activation(out=x1b[:, c, 0:OWN], in_=xres[:, c, 0:OWN], func=AF.Copy))(c), reads=ALLX, writes=[B_x1b[c]])
    xc_in = nc.dram_tensor("xc_in", [128, 8192], BF16)
    xc_out = nc.dram_tensor("xc_out", [256, 8192], BF16)
    B_inb, B_outb = Buf("xc_in"), Buf("xc_out")
    stg = [at(f"stg{i}", (128, 2, 1024), BF16, OFF_RING + i * 4096) for i in range(2)]
    xtmp = at("xtmp", (128, 1024), BF16, OFF_RING + 8192)
    B_stg = [Buf("stg0"), Buf("stg1")]
    B_xtmp = Buf("xtmp")
    S.dma("pool", lambda e: e.dma_start(out=xc_in.ap().rearrange("p (c n) -> p c n", c=8), in_=x1b[:, :, 1024:2048]), "xc",
          reads=B_x1b, writes=[B_inb])
    S.op("pool", lambda e: e.collective_compute("AllGather", ALU.bypass, replica_groups=[[0, 1], [2, 3], [4, 5], [6, 7]],
                                                ins=[xc_in.ap().opt()], outs=[xc_out.ap().opt()]),
         reads=[B_inb], writes=[B_outb])
    outv = xc_out.ap().rearrange("(r p) (c n) -> p r c n", r=2, c=8)

    def _rev(ap):
        pat = [list(p) for p in ap.ap]
        st, n = pat[-1]
        pat[-1] = [-st, n]
        return bass.AP(ap.tensor, ap.offset + st * (n - 1), pat)
    for c in range(8):
        sl = c % 2
        S.dma("sp", (lambda c, sl: lambda e: e.dma_start(out=stg[sl][:], in_=outv[:, :, c, :]))(c, sl), f"xs{sl}", reads=[B_outb], writes=[B_stg[sl]])
        S.op("dve", (lambda sl: lambda e: e.tensor_scalar(out=xtmp[:, :], in0=_rev(stg[sl][:, 0, :]), scalar1=pflag[:, 1:2], scalar2=None, op0=ALU.mult))(sl),
             reads=[B_stg[sl], B_const], writes=[B_xtmp])
        S.op("dve", (lambda c, sl: lambda e: e.scalar_tensor_tensor(out=x1b[:, c, OWN:TH], in0=_rev(stg[sl][:, 1, :]), scalar=pflag[:, 0:1], in1=xtmp[:, :],
                                                                     op0=ALU.mult, op1=ALU.add))(c, sl),
             reads=[B_stg[sl], B_xtmp, B_const], writes=[B_x1b[c]])
    S.barrier()
    B_vaug = Buf("vaug")
    S.op("dve", lambda e: e.memset(vaug[:, :, 64:128], 1.0), writes=[B_vaug])
    B_wq = [Buf("wqkv0"), Buf("wqkv1")]
    B_q = Buf("qT")
    B_k = Buf("kT")
    B_v = Buf("vT")
    B_pt2 = [Buf("PT0"), Buf("PT1"), Buf("PT2")]
    B_tt = [Buf("tt0"), Buf("tt1")]
    B_os = [[Buf(f"os{h}_{q}") for q in range(4)] for h in range(2)]
    B_rd = [Buf("rden0"), Buf("rden1")]
    B_ohp = Buf("ohp")
    B_wao = Buf("wao")
    XO = [[Buf(f"xo{dc}_{q}") for q in range(4)] for dc in range(8)]
    PSA = [Buf(f"psa{i}") for i in range(8)]
    psT = PSHI[:, :].bitcast(BF16)

    def osum(hd, qt):
        return os_t[:, hd * 4 + qt, 0:512]

    slopes = 2.0 ** (-8.0 * np.arange(1, 49) / 48.0)
    slopes = slopes.reshape(3, 16)
    rot = {"wk": 0, "tt": 0, "pt": 0, "wq": 0}

    def wk_slot():
        b = rot["wk"] % 4
        rot["wk"] += 1
        return b

    def attn_proj(hp, g, win, d):
        Lq = OWN // d
        Lk = Lq + 64
        ntok_k = Lk * d
        ws = rot["wq"] % 2
        rot["wq"] += 1
        wq = wqkv2[ws]
        S.dma("pool", (lambda idx, wq: lambda e: e.dma_start(out=wq[:], in_=w_qkv_d[idx].rearrange("p (m k f) -> p m k f", m=3, k=8)))(hp * 3 + g, wq),
              f"aw{ws}", writes=[B_wq[ws]])
        ev = 0
        for m, (dstT, dstB, ntok) in enumerate(((qT, B_q, OWN), (kT, B_k, ntok_k), (vT, B_v, ntok_k))):
            dview = dstT[:, 0:ntok].rearrange("p (r i) -> p r i", r=d)
            for j0 in range(0, ntok, 512):
                n = min(512, ntok - j0)
                bk = wk_slot()
                ph = bk * 512

                def mmp(e, m=m, j0=j0, n=n, ph=ph):
                    for kc in range(8):
                        ins = e.matmul(PSHI[:, ph:ph + n], lhsT=wq[:, m, kc, :], rhs=x1b[:, kc, j0:j0 + n], start=(kc == 0), stop=(kc == 7))
                    return ins
                S.op("pe", mmp, reads=[B_wq[ws]] + B_x1b, writes=[PSA[4 + bk]])
                i0 = j0 // d
                ni = n // d
                src = PSHI[:, ph:ph + n].rearrange("p (i r) -> p r i", r=d)
                if ev % 2 == 0:
                    S.op("act", (lambda dview, src, i0, ni: lambda e: e.activation(out=dview[:, :, i0:i0 + ni], in_=src, func=AF.Copy))(dview, src, i0, ni),
                         reads=[PSA[4 + bk]], writes=[dstB])
                else:
                    S.op("dve", (lambda dview, src, i0, ni: lambda e: e.tensor_copy(out=dview[:, :, i0:i0 + ni], in_=src))(dview, src, i0, ni),
                         reads=[PSA[4 + bk]], writes=[dstB])
                ev += 1
        chunks = []
        for r in range(d):
            for k0 in range(0, Lk, 128):
                chunks.append((r, k0, min(128, Lk - k0)))
        for c0 in range(0, len(chunks), 4):
            grp = chunks[c0:c0 + 4]
            ng = len(grp)
            bk = wk_slot()
            pbase = bk * 1024

            def tr(e, grp=grp, pbase=pbase):
                for q, (r, k0, nk) in enumerate(grp):
                    ins = e.transpose(psT[0:nk, pbase + q * 128:pbase + (q + 1) * 128], vT[:, r * Lk + k0:r * Lk + k0 + nk], ident_bf[:, :])
                return ins
            S.op("pe", tr, reads=[B_v, B_ident], writes=[PSA[4 + bk]])
            srcv = psT[:, pbase:pbase + ng * 128].rearrange("p (q f) -> p q f", f=128)
            if (c0 // 4) % 2 == 0:
                S.op("act", (lambda c0, ng, srcv: lambda e: e.activation(out=vaug[:, c0:c0 + ng, 0:64], in_=srcv[:, :, 0:64], func=AF.Copy))(c0, ng, srcv),
                     reads=[PSA[4 + bk]], writes=[B_vaug])
                S.op("act", (lambda c0, ng, srcv: lambda e: e.activation(out=vaug[:, c0:c0 + ng, 128:192], in_=srcv[:, :, 64:128], func=AF.Copy))(c0, ng, srcv),
                     reads=[PSA[4 + bk]], writes=[B_vaug])
            else:
                S.op("dve", (lambda c0, ng, srcv: lambda e: e.tensor_copy(out=vaug[:, c0:c0 + ng, 0:64], in_=srcv[:, :, 0:64]))(c0, ng, srcv),
                     reads=[PSA[4 + bk]], writes=[B_vaug])
                S.op("dve", (lambda c0, ng, srcv: lambda e: e.tensor_copy(out=vaug[:, c0:c0 + ng, 128:192], in_=srcv[:, :, 64:128]))(c0, ng, srcv),
                     reads=[PSA[4 + bk]], writes=[B_vaug])
        return {"Lq": Lq, "Lk": Lk, "chunks": chunks}

    def attn_core(hp, g, win, d, ctx):
        Lq, Lk, chunks = ctx["Lq"], ctx["Lk"], ctx["chunks"]
        items = []
        for hd in range(2):
            work = []
            for ci, (r, k0, nk) in enumerate(chunks):
                i0 = max(0, k0 - 64)
                i1 = min(Lq, k0 + nk + 64)
                if i1 > i0:
                    work.append((ci, r, k0, nk, i0, i1))
            npair = (len(work) + 1) // 2
            for pi in range(npair):
                items.append({"hd": hd, "pair": work[2 * pi:2 * pi + 2], "last": pi == npair - 1})
        started = {0: [False] * 4, 1: [False] * 4}
        accv = PSLO[:, :].rearrange("p (r i) -> p i r", r=d)

        def emit_scores(it):
            hd = it["hd"]
            pair = it["pair"]
            hg = 2 * hp + hd
            cneg = -float(slopes[g, hg]) * d * 8.0
            hrow = slice(64 * hd, 64 * hd + 64)
            bk = wk_slot()
            ph = bk * 512

            def mms(e, pair=pair, ph=ph, hrow=hrow):
                for q, (ci, r, k0, nk, i0, i1) in enumerate(pair):
                    doff = i0 - (k0 - 64)
                    nq = i1 - i0
                    ins = e.matmul(PSHI[0:nk, ph + q * 256 + doff:ph + q * 256 + doff + nq], lhsT=kT[hrow, r * Lk + k0:r * Lk + k0 + nk],
                                   rhs=qT[hrow, r * Lq + i0:r * Lq + i0 + nq], start=True, stop=True, skip_group_check=True)
                return ins
            S.op("pe", mms, reads=[B_k, B_q], writes=[PSA[4 + bk]])
            ts = rot["tt"] % 2
            rot["tt"] += 1
            ps_ = rot["pt"] % 3
            rot["pt"] += 1
            it["ps"] = ps_
            S.op("dve", (lambda ph, ts, cneg: lambda e: e.scalar_tensor_tensor(
                out=tt[ts][:, :], in0=dtile[:, :], scalar=cneg, in1=PSHI[:, ph:ph + 512], op0=ALU.mult, op1=ALU.add))(ph, ts, cneg),
                reads=[B_const, PSA[4 + bk]], writes=[B_tt[ts]])
            S.op("act", (lambda ts, ps_: lambda e: e.activation(out=PTt[ps_][:, :], in_=tt[ts][:, :], func=AF.Exp, scale=0.125))(ts, ps_),
                 reads=[B_tt[ts]], writes=[B_pt2[ps_]])

        def emit_pv(it):
            hd = it["hd"]
            pair = it["pair"]
            ps_ = it["ps"]
            vsel = (slice(0, 128) if hd == 0 else slice(64, 192))
            st = started[hd]
            plan = []
            wb = set()
            for q, (ci, r, k0, nk, i0, i1) in enumerate(pair):
                doff = i0 - (k0 - 64)
                pos = r * Lq + i0
                end = r * Lq + i1
                while pos < end:
                    bb = pos // 512
                    nx = min(end, (bb + 1) * 512)
                    plan.append((ci, nk, q * 256 + doff + (pos - (r * Lq + i0)), pos, nx - pos, not st[bb]))
                    st[bb] = True
                    wb.add(bb)
                    pos = nx

            def mmv(e, plan=plan, ps_=ps_, vsel=vsel):
                for (ci, nk, pcol, pos, n, stt) in plan:
                    ins = e.matmul(PSLO[:, pos:pos + n], lhsT=vaug[0:nk, ci, vsel], rhs=PTt[ps_][0:nk, pcol:pcol + n], start=stt, stop=True,
                                   skip_group_check=True)
                return ins
            S.op("pe", mmv, reads=[B_vaug, B_pt2[ps_]], writes=[PSA[bb] for bb in sorted(wb)])
            if it["last"]:
                for qt in range(4):
                    ia = 512 * qt // d
                    nn = 512 // d
                    srcp = accv[:, ia:ia + nn, :]
                    dst = osum(hd, qt).rearrange("p (i r) -> p i r", r=d)
                    rb = [PSA[qt]] if d == 1 else [PSA[0], PSA[1], PSA[2], PSA[3]]
                    if g == 0:
                        S.op("act", (lambda dst, srcp: lambda e: e.activation(out=dst, in_=srcp, func=AF.Copy))(dst, srcp),
                             reads=rb, writes=[B_os[hd][qt]])
                    else:
                        S.op("dve", (lambda dst, srcp: lambda e: e.tensor_tensor(out=dst, in0=dst, in1=srcp, op=ALU.add))(dst, srcp),
                             reads=rb + [B_os[hd][qt]], writes=[B_os[hd][qt]])

        LA = 2
        for idx in range(len(items) + LA):
            if idx < len(items):
                emit_scores(items[idx])
            if idx - LA >= 0:
                emit_pv(items[idx - LA])

    def attn_finish(hp):
        for hd in range(2):
            nrow = slice(0, 64) if hd == 0 else slice(64, 128)
            drow = slice(64, 128) if hd == 0 else slice(0, 64)
            osv = os_t[:, hd * 4:hd * 4 + 4, 0:512]
            rdv = os_t[:, hd * 4:hd * 4 + 4, 512:1024]
            ohv = o_hp[:, :].rearrange("p (q n) -> p q n", q=4)
            allos = [B_os[hd][q] for q in range(4)]
            S.op("act", (lambda osv, rdv, nrow, drow: lambda e: e.activation(out=rdv[nrow], in_=osv[drow], func=AF.Ln))(osv, rdv, nrow, drow),
                 reads=allos, writes=[B_rd[hd]])
            S.op("act", (lambda rdv, nrow: lambda e: e.activation(out=rdv[nrow], in_=rdv[nrow], func=AF.Exp, scale=-1.0))(rdv, nrow),
                 reads=[B_rd[hd]], writes=[B_rd[hd]])
            S.op("dve", (lambda osv, rdv, ohv, nrow: lambda e: e.tensor_tensor(out=ohv[nrow], in0=osv[nrow], in1=rdv[nrow], op=ALU.mult))(osv, rdv, ohv, nrow),
                 reads=allos + [B_rd[hd]], writes=[B_ohp])

    def attn_oproj(hp):
        S.dma("pool", (lambda hp: lambda e: e.dma_start(out=wao[:], in_=w_ao_d[hp]))(hp), "ao", writes=[B_wao])
        for dc in range(8):
            for qt in range(4):
                bk = wk_slot()
                ph = bk * 512

                def mmo2(e, dc=dc, qt=qt, ph=ph):
                    return e.matmul(PSHI[:, ph:ph + 512], lhsT=wao[:, dc * 128:(dc + 1) * 128], rhs=o_hp[:, qt * 512:(qt + 1) * 512], start=True, stop=True)
                S.op("pe", mmo2, reads=[B_wao, B_ohp], writes=[PSA[4 + bk]])
                S.op("dve", (lambda dc, qt, ph: lambda e: e.scalar_tensor_tensor(out=xres[:, dc, qt * 512:(qt + 1) * 512], in0=PSHI[:, ph:ph + 512], scalar=1.0 / ALPHA,
                                                                               in1=xres[:, dc, qt * 512:(qt + 1) * 512], op0=ALU.mult, op1=ALU.add))(dc, qt, ph),
                     reads=[PSA[4 + bk], XO[dc][qt]], writes=[XO[dc][qt]])
    ctx_next = attn_proj(0, 0, DIL[0][0], DIL[0][1])
    for hp in range(8):
        for g, (win, d) in enumerate(DIL):
            ctx = ctx_next
            attn_core(hp, g, win, d, ctx)
            if g < 2:
                ctx_next = attn_proj(hp, g + 1, DIL[g + 1][0], DIL[g + 1][1])
        attn_finish(hp)
        if hp < 7:
            ctx_next = attn_proj(hp + 1, 0, DIL[0][0], DIL[0][1])
        attn_oproj(hp)
    S.barrier()
    for t in range(2):
        t0, nt = TILES[t]
        ln_cols(1, 1, t0, nt, XR[t])
    S.barrier()
    if dbg == "l1attn":
        return dump_and_finish()
    ffn_jobs([(1, 1, 2, t) for t in range(2)])
    return dump_and_finish()


def _prep_shared(ffn1_w_gate, ffn1_w_up, ffn1_w_down, ffn2_w_gate, ffn2_w_up, ffn2_w_down,
                 ln_gain, ln_bias, pool_w_in, pool_w_group, pool_scale, pool_w_out, attn_w_qkv, attn_w_out):
    f = np.float32
    gates = [ffn1_w_gate, ffn2_w_gate]
    ups = [ffn1_w_up, ffn2_w_up]
    downs = [ffn1_w_down, ffn2_w_down]
    w_gu = np.empty((2, 2, NFC, 128, 2, 8, 128), f)
    w_d = np.empty((2, 2, 8, 128, NFC, 128), f)
    for l in range(2):
        for fi in range(2):
            g = np.asarray(gates[fi][l], f).reshape(8, 128, NFC, 128)
            u = np.asarray(ups[fi][l], f).reshape(8, 128, NFC, 128)
            w_gu[l, fi, :, :, 0] = g.transpose(2, 1, 0, 3)
            w_gu[l, fi, :, :, 1] = u.transpose(2, 1, 0, 3)
            dn = np.asarray(downs[fi][l], f).reshape(NFC, 128, 8, 128)
            w_d[l, fi] = dn.transpose(2, 1, 0, 3)
    lnp = np.empty((128, 2, 3, 2, 8), f)
    lnp[:, :, :, 0, :] = np.asarray(ln_gain, f).reshape(2, 3, 8, 128).transpose(3, 0, 1, 2)
    lnp[:, :, :, 1, :] = np.asarray(ln_bias, f).reshape(2, 3, 8, 128).transpose(3, 0, 1, 2)
    w_pin = np.asarray(pool_w_in[0], f).reshape(8, 128, 1024).transpose(1, 0, 2)
    w_pout = np.asarray(pool_w_out[0], f).reshape(8, 128, 1024).transpose(1, 0, 2)
    w_pgrp = np.asarray(pool_w_group[0], f).reshape(4, 2, 128, 256).transpose(2, 0, 1, 3)
    pscale = np.asarray(pool_scale[0], f).reshape(8, 128).T
    wq = np.asarray(attn_w_qkv[0], f).reshape(8, 128, 3, 3, 8, 128)
    w_qkv = wq.transpose(4, 2, 1, 3, 0, 5)
    w_ao = np.asarray(attn_w_out[0], f).reshape(8, 128, 1024)
    r = np.arange(128)[:, None]
    i = np.arange(256)[None, :]
    dd = np.abs(r - i + 64).astype(f)
    dtile = np.where(dd <= 64, dd, BIGD).astype(f)
    dtile = np.concatenate([dtile, dtile], axis=1)
    c = np.ascontiguousarray
    return {
        "lnp": c(lnp.reshape(128, 96)), "pscale": c(pscale), "dtile": c(dtile), "ident": np.eye(128, dtype=f),
        "w_gu": c(w_gu.reshape(88, 128, 2048)), "w_d": c(w_d.reshape(32, 128, 2816)),
        "w_pin": c(w_pin.reshape(128, 8192)), "w_pout": c(w_pout.reshape(128, 8192)), "w_pgrp": c(w_pgrp.reshape(128, 2048)),
        "w_qkv": c(w_qkv.reshape(24, 128, 3072)), "w_ao": c(w_ao),
    }


def _prep_core(x, core):
    f = np.float32
    b, half = core // 2, core % 2
    xs = np.asarray(x[b], f)
    if half == 0:
        loc = xs[0:TC]
    else:
        loc = xs[::-1][0:TC]
    xT = np.ascontiguousarray(loc.T.reshape(8, 128, TC).transpose(1, 0, 2))
    pflag = np.zeros((128, 2), f)
    pflag[:, half] = 1.0
    pinv = np.empty((128, 8, 8), f)
    for cidx in range(8):
        h = 1 << (cidx // 2)
        w = 2 * h
        for jo in range(8):
            if half == 0:
                cnt = h + min(h, jo)
            else:
                cnt = h + min(h, jo + 1)
            pinv[:, cidx, jo] = 1.0 / cnt
    return {"xT": xT, "pflag": pflag, "pinv": np.ascontiguousarray(pinv.reshape(128, 64))}


_NC_CACHE = {}


def _get_nc(dbg=None):
    if dbg not in _NC_CACHE:
        _NC_CACHE[dbg] = build_program(dbg)
    return _NC_CACHE[dbg]


def run_cores(inputs, dbg=None, cores=range(8), trace=False):
    x = np.asarray(inputs["x"], np.float32)
    shared = _prep_shared(**{k: v for k, v in inputs.items() if k != "x"})
    in_maps = []
    for core in cores:
        m = dict(shared)
        m.update(_prep_core(x, core))
        in_maps.append(m)
    nc = _get_nc(dbg)
    res = run_bass_kernel_spmd(nc, in_maps, core_ids=list(range(len(in_maps))), **({"trace": True} if trace else {}))
    return res


def kernel(**inputs):
    res = run_cores(inputs)
    out = np.empty((BATCH, SEQ, D_MODEL), np.float32)
    for core in range(8):
        yT = np.asarray(res.results[core]["yT"], np.float32)
        y = yT.transpose(2, 1, 0).reshape(OWN, D_MODEL)
        b, half = core // 2, core % 2
        if half == 0:
            out[b, 0:OWN] = y
        else:
            out[b, OWN:SEQ] = y[::-1]
    return out
```

```python
import numpy as np
import concourse.bass as bass
import concourse.mybir as mybir
from concourse.bass_utils import run_bass_kernel_spmd

F32 = mybir.dt.float32
BF16 = mybir.dt.bfloat16
AF = mybir.ActivationFunctionType
ALU = mybir.AluOpType

D_MODEL = 1024
SEQ = 4096
BATCH = 4
D_FF = 2816
NFC = 22
OWN = 2048
HALO = 1024
TC = OWN + 8
TH = OWN + HALO
ALPHA = 4.0 ** 0.25
LN_EPS = 1e-5
EPS_P = LN_EPS / (ALPHA * ALPHA)
DIL = ((128, 1), (512, 4), (2048, 16))
BIGD = 1.0e5

ENGS = ("pe", "act", "dve", "pool", "sp")


class Buf:
    __slots__ = ("name", "wdep", "rdeps")

    def __init__(self, name):
        self.name = name
        self.wdep = None
        self.rdeps = {}


class Sched:
    def __init__(self, nc):
        self.nc = nc
        self.sem = {}
        self.cnt = {}
        for e in ENGS:
            self.sem[e] = nc.alloc_semaphore("s_" + e)
            self.cnt[e] = 0
        self.seen = {e: {} for e in ENGS}
        self.ops = {e: [] for e in ENGS}

    def new_dma_sem(self, key):
        self.sem[key] = self.nc.alloc_semaphore("d_" + key)
        self.cnt[key] = 0
        return key

    def _collect(self, reads, writes):
        w = {}
        for b in reads:
            d = b.wdep
            if d is not None and w.get(d[0], 0) < d[1]:
                w[d[0]] = d[1]
        for b in writes:
            d = b.wdep
            if d is not None and w.get(d[0], 0) < d[1]:
                w[d[0]] = d[1]
            for k, v in b.rdeps.items():
                if w.get(k, 0) < v:
                    w[k] = v
        return w

    def _need(self, e, w):
        need = []
        s = self.seen[e]
        for k, v in w.items():
            if s.get(k, 0) < v:
                need.append((k, v))
                s[k] = v
        return need

    def op(self, e, fn, reads=(), writes=()):
        need = self._need(e, self._collect(reads, writes))
        self.cnt[e] += 1
        v = self.cnt[e]
        self.ops[e].append((need, fn, (e, 1)))
        for b in reads:
            if b.rdeps.get(e, 0) < v:
                b.rdeps[e] = v
        for b in writes:
            b.wdep = (e, v)
            b.rdeps = {}
        return (e, v)

    def dma(self, q, fn, semkey, reads=(), writes=()):
        need = self._need(q, self._collect(reads, writes))
        self.cnt[semkey] += 16
        v = self.cnt[semkey]
        self.ops[q].append((need, fn, (semkey, 16)))
        for b in reads:
            if b.rdeps.get(semkey, 0) < v:
                b.rdeps[semkey] = v
        for b in writes:
            b.wdep = (semkey, v)
            b.rdeps = {}
        return (semkey, v)

    def wait_all(self, e, keys):
        w = {k: self.cnt[k] for k in keys if self.cnt[k] > 0}
        need = self._need(e, w)
        if need:
            self.ops[e].append((need, None, None))

    def barrier(self, engines=("pe", "act", "dve", "pool")):
        for e in engines:
            self.wait_all(e, list(engines))

    def emit(self, block):
        sched = self

        def make(e):
            def body(eng):
                for need, fn, inc in sched.ops[e]:
                    for k, v in need:
                        eng.wait_ge(sched.sem[k], v)
                    if fn is not None:
                        ins = fn(eng)
                        ins.then_inc(sched.sem[inc[0]], inc[1])
            return body
        block.tensor(make("pe"))
        block.scalar(make("act"))
        block.vector(make("dve"))
        block.gpsimd(make("pool"))
        block.sync(make("sp"))


def _halves(nt):
    return [(h0, min(512, nt - h0)) for h0 in range(0, nt, 512)]


def build_program(dbg=None):
    nc = bass.Bass("TRN2", target_bir_lowering=False)
    S = Sched(nc)

    def din(name, shape):
        return nc.dram_tensor(name, list(shape), F32, kind="ExternalInput").ap()

    xT_d = din("xT", (128, 8, TC))
    lnp_d = din("lnp", (128, 96))
    pflag_d = din("pflag", (128, 2))
    pinv_d = din("pinv", (128, 64))
    pscale_d = din("pscale", (128, 8))
    dtile_d = din("dtile", (128, 512))
    ident_d = din("ident", (128, 128))
    w_gu_d = din("w_gu", (88, 128, 2048))
    w_d_d = din("w_d", (32, 128, 2816))
    w_pin_d = din("w_pin", (128, 8192))
    w_pout_d = din("w_pout", (128, 8192))
    w_pgrp_d = din("w_pgrp", (128, 2048))
    w_qkv_d = din("w_qkv", (24, 128, 3072))
    w_ao_d = din("w_ao", (8, 128, 1024))
    if dbg is None:
        y_d = nc.dram_tensor("yT", [128, 8, OWN], F32, kind="ExternalOutput").ap()
    else:
        y_d = nc.dram_tensor("yT", [128, 8, TC], F32, kind="ExternalOutput").ap()

    base = (nc.sbuf_base + 31) // 32 * 32
    OFF_X = 0
    OFF_OS = 65792
    OFF_C = 98560
    OFF_R1 = OFF_C + 4096
    OFF_RING = OFF_R1 + 61440
    OFF_LN = OFF_RING + 23552
    OFF_SP = OFF_LN + 16384
    ARENA = OFF_SP + 4608
    arena = nc.alloc_sbuf_tensor("arena", [128, ARENA // 4], F32)
    abase = base
    assert nc.sbuf_base >= abase + ARENA

    def at(name, shape, dtype, off):
        return nc.alloc_sbuf_tensor_at(name, list(shape), dtype, offset=abase + off)

    xres = at("xres", (128, 8, TC), F32, OFF_X)
    os_t = at("os_t", (128, 8, 1024), F32, OFF_OS)
    lnp = at("lnp", (128, 96), F32, OFF_C)
    pflag = at("pflag", (128, 2), F32, OFF_C + 384)
    pinv = at("pinv", (128, 64), F32, OFF_C + 416)
    pscale = at("pscale", (128, 8), F32, OFF_C + 672)
    dtile = at("dtile", (128, 512), F32, OFF_C + 704)
    ones_bf = at("ones_bf", (128, 128), BF16, OFF_C + 2752)
    ident_bf = at("ident_bf", (128, 128), BF16, OFF_C + 3008)
    hsave = at("hsave", (128, 8, 8), F32, OFF_C + 3264)
    epst = at("epst", (128, 1), F32, OFF_C + 3520)
    negone = at("negone", (128, 1), F32, OFF_C + 3552)
    hbuf = at("hbuf", (128, NFC, 1024), BF16, OFF_R1)
    xb = at("xb", (128, 8, 1024), BF16, OFF_R1 + 45056)
    gu_slot = [at(f"gu{i}", (128, 2, 8, 128), BF16, OFF_RING + i * 4096) for i in range(3)]
    d_slot = [at(f"dw{i}", (128, NFC, 128), BF16, OFF_RING + 12288 + i * 5632) for i in range(2)]
    zb = at("zb", (128, 2, 1024), BF16, OFF_LN)
    zq = at("zq", (128, 2, 1024), BF16, OFF_LN + 4096)
    meant = at("meant", (128, 1024), F32, OFF_LN + 8192)
    rstdt = at("rstdt", (128, 1024), F32, OFF_LN + 12288)
    sgt = at("sgt", (128, 1024), F32, OFF_SP)
    p_win = at("p_win", (128, 8, 1024), BF16, OFF_R1)
    p_wout = at("p_wout", (128, 8, 1024), BF16, OFF_R1 + 16384)
    p_wgrp = at("p_wgrp", (128, 4, 2, 256), BF16, OFF_R1 + 32768)
    p_xb = at("p_xb", (128, 8, 512), BF16, OFF_R1 + 36864)
    p_yb = at("p_yb", (128, 8, 512), BF16, OFF_R1 + 45056)
    p_mx = at("p_mx", (128, 8, 512), BF16, OFF_R1 + 53248)
    p_u = at("p_u", (128, 8, 528), F32, OFF_RING)
    p_t = [at(f"p_t{i}", (128, 528), F32, OFF_RING + 16896 + i * 2112) for i in range(2)]
    p_wn = at("p_wn", (128, 512), F32, OFF_RING + 21120)
    x1b = at("x1b", (128, 8, TH), BF16, OFF_R1)
    qT = at("qT", (128, OWN), BF16, OFF_R1 + 49152)
    kT = at("kT", (128, TH), BF16, OFF_R1 + 53248)
    PTt = [at(f"PT{i}", (128, 512), BF16, OFF_RING + 20480 + i * 1024) for i in range(3)]
    wqkv2 = [at(f"wqkv{i}", (128, 3, 8, 128), BF16, OFF_RING + i * 6144) for i in range(2)]
    vT = at("vT", (128, TH), BF16, OFF_RING + 12288)
    wao = at("wao", (128, 1024), BF16, OFF_RING + 18432)
    vaug = at("vaug", (128, 32, 192), BF16, OFF_LN)
    o_hp = at("o_hp", (128, OWN), BF16, OFF_LN + 12288)
    tt = [at(f"tt{i}", (128, 512), F32, OFF_SP + i * 2048) for i in range(2)]

    PSLO = nc.alloc_psum_tensor("pslo", [128, 2048], F32)
    PSHI = nc.alloc_psum_tensor("pshi", [128, 2048], F32)
    PS = [PSLO[:, 0:1024], PSLO[:, 1024:2048], PSHI[:, 0:1024], PSHI[:, 1024:2048]]
    PSB = [Buf(f"ps{i}") for i in range(4)]

    for k in ["x0", "x1", "x2", "xc", "xs0", "xs1", "const", "gu0", "gu1", "gu2", "dw0", "dw1", "pw", "aw0", "aw1", "ao", "out"]:
        S.new_dma_sem(k)

    B_const = Buf("const")
    B_ident = Buf("ident")
    B_ones = Buf("ones")
    TILES = [(0, 1024), (1024, 1024), (2048, 8)]
    XR = [[Buf(f"xr{t}_{c}") for c in range(8)] for t in range(3)]
    XB = [Buf(f"xb{c}") for c in range(8)]
    HB = [Buf(f"hb{f}") for f in range(NFC)]
    GU = [Buf(f"gu{i}") for i in range(3)]
    DW = [Buf(f"dw{i}") for i in range(2)]
    B_sg = Buf("sg")
    B_zb = [Buf("zb0"), Buf("zb1")]
    B_zq = [Buf("zq0"), Buf("zq1")]
    B_mean = Buf("mean")
    B_rstd = Buf("rstd")
    ring = {"gu": 0, "dw": 0}

    for (dst, src) in [(lnp, lnp_d), (pflag, pflag_d), (pinv, pinv_d), (pscale, pscale_d), (dtile, dtile_d)]:
        S.dma("sp", (lambda dst, src: lambda e: e.dma_start(out=dst[:], in_=src))(dst, src), "const", writes=[B_const])
    B_const.wdep = ("const", S.cnt["const"])
    S.dma("pool", lambda e: e.dma_start(out=ident_bf[:], in_=ident_d), "pw", writes=[B_ident])
    S.op("dve", lambda e: e.memset(ones_bf[:], 1.0), writes=[B_ones])
    S.op("dve", lambda e: e.memset(negone[:], -1.0), writes=[B_const])
    S.op("dve", lambda e: e.memset(epst[:], EPS_P), writes=[B_const])
    B_const.wdep = ("const", S.cnt["const"])
    B_eps = Buf("eps")
    B_eps.wdep = ("dve", S.cnt["dve"])
    for t, (t0, nt) in enumerate(TILES):
        S.dma("sp", (lambda t0, nt: lambda e: e.dma_start(out=xres[:, :, t0:t0 + nt], in_=xT_d[:, :, t0:t0 + nt]))(t0, nt),
              f"x{t}", writes=XR[t])

    def ln_stats_chunk(c, t0, nt, xr_bufs):
        hv = _halves(nt)
        sl = c % 2
        S.op("act", (lambda c, sl: lambda e: e.activation(out=zb[:, sl, 0:nt], in_=xres[:, c, t0:t0 + nt], func=AF.Copy))(c, sl),
             reads=[xr_bufs[c]], writes=[B_zb[sl]])
        S.op("act", (lambda c, sl: lambda e: e.activation(out=zq[:, sl, 0:nt], in_=xres[:, c, t0:t0 + nt], func=AF.Square))(c, sl),
             reads=[xr_bufs[c]], writes=[B_zq[sl]])

        def mm(e, c=c, sl=sl):
            for (h0, hn) in hv:
                e.matmul(PS[0][:, h0:h0 + hn], lhsT=ones_bf[:, :], rhs=zb[:, sl, h0:h0 + hn], start=(c == 0), stop=(c == 7))
            for (h0, hn) in hv:
                ins = e.matmul(PS[1][:, h0:h0 + hn], lhsT=ones_bf[:, :], rhs=zq[:, sl, h0:h0 + hn], start=(c == 0), stop=(c == 7))
            return ins
        S.op("pe", mm, reads=[B_zb[sl], B_zq[sl], B_ones], writes=[PSB[0], PSB[1]])

    def ln_cols(l, s, t0, nt, xr_bufs):
        for c in range(8):
            ln_stats_chunk(c, t0, nt, xr_bufs)
        ln_finish(l, s, t0, nt, xr_bufs)

    def ln_finish_head(nt):
        S.op("dve", lambda e: e.tensor_scalar(out=meant[:, 0:nt], in0=PS[0][:, 0:nt], scalar1=1.0 / D_MODEL, scalar2=None, op0=ALU.mult),
             reads=[PSB[0]], writes=[B_mean])
        S.op("dve", lambda e: e.tensor_tensor(out=rstdt[:, 0:nt], in0=meant[:, 0:nt], in1=meant[:, 0:nt], op=ALU.mult),
             reads=[B_mean], writes=[B_rstd])
        S.op("dve", lambda e: e.scalar_tensor_tensor(out=rstdt[:, 0:nt], in0=PS[1][:, 0:nt], scalar=1.0 / D_MODEL, in1=rstdt[:, 0:nt],
                                                     op0=ALU.mult, op1=ALU.subtract),
             reads=[PSB[1], B_rstd], writes=[B_rstd])
        S.op("act", lambda e: e.activation(out=rstdt[:, 0:nt], in_=rstdt[:, 0:nt], func=AF.Ln, bias=epst[:, 0:1], scale=1.0),
             reads=[B_rstd, B_eps], writes=[B_rstd])
        S.op("act", lambda e: e.activation(out=rstdt[:, 0:nt], in_=rstdt[:, 0:nt], func=AF.Exp, scale=-0.5),
             reads=[B_rstd], writes=[B_rstd])

    def ln_finish_chunk(l, s, c, t0, nt, xr_bufs):
        gi = (l * 3 + s) * 16
        S.op("dve", lambda e: e.tensor_tensor(out=xres[:, c, t0:t0 + nt], in0=xres[:, c, t0:t0 + nt], in1=meant[:, 0:nt], op=ALU.subtract),
             reads=[xr_bufs[c], B_mean], writes=[xr_bufs[c]])
        S.op("dve", lambda e: e.tensor_tensor(out=xres[:, c, t0:t0 + nt], in0=xres[:, c, t0:t0 + nt], in1=rstdt[:, 0:nt], op=ALU.mult),
             reads=[xr_bufs[c], B_rstd], writes=[xr_bufs[c]])
        S.op("act", lambda e: e.activation(out=xres[:, c, t0:t0 + nt], in_=xres[:, c, t0:t0 + nt], func=AF.Identity,
                                           bias=lnp[:, gi + 8 + c:gi + 9 + c], scale=lnp[:, gi + c:gi + c + 1]),
             reads=[xr_bufs[c], B_const], writes=[xr_bufs[c]])

    def ln_finish(l, s, t0, nt, xr_bufs):
        ln_finish_head(nt)
        for c in range(8):
            ln_finish_chunk(l, s, c, t0, nt, xr_bufs)

    def ffn_cast(t):
        t0, nt = TILES[t]
        xr_bufs = XR[t]
        for c in range(8):
            eng = "dve" if c % 2 == 0 else "act"
            if eng == "dve":
                S.op("dve", (lambda c: lambda e: e.tensor_copy(out=xb[:, c, 0:nt], in_=xres[:, c, t0:t0 + nt]))(c), reads=[xr_bufs[c]], writes=[XB[c]])
            else:
                S.op("act", (lambda c: lambda e: e.activation(out=xb[:, c, 0:nt], in_=xres[:, c, t0:t0 + nt], func=AF.Copy))(c), reads=[xr_bufs[c]], writes=[XB[c]])

    def ffn_tile(l, fi, s, t, xr_bufs, do_cast=True, next_t=None, deferred=None, defer=False):
        t0, nt = TILES[t]
        hv = _halves(nt)
        if do_cast:
            ffn_cast(t)
        wbase = (l * 2 + fi) * NFC
        for fc in range(NFC):
            sl = ring["gu"] % 3
            ring["gu"] += 1
            S.dma("pool", (lambda sl, idx: lambda e: e.dma_start(out=gu_slot[sl][:], in_=w_gu_d[idx].rearrange("p (a k f) -> p a k f", a=2, k=8)))(sl, wbase + fc),
                  f"gu{sl}", writes=[GU[sl]])
            pg, pu = (2, 3) if fc % 2 == 0 else (0, 1)

            def mm(e, sl=sl, pg=pg, pu=pu):
                for (pp, a) in ((pg, 0), (pu, 1)):
                    for (h0, hn) in hv:
                        for kc in range(8):
                            ins = e.matmul(PS[pp][:, h0:h0 + hn], lhsT=gu_slot[sl][:, a, kc, :], rhs=xb[:, kc, h0:h0 + hn], start=(kc == 0), stop=(kc == 7))
                return ins
            S.op("pe", mm, reads=[GU[sl]] + XB, writes=[PSB[pg], PSB[pu]])
            S.op("act", (lambda pg: lambda e: e.activation(out=sgt[:, 0:nt], in_=PS[pg][:, 0:nt], func=AF.Silu))(pg), reads=[PSB[pg]], writes=[B_sg])
            S.op("dve", (lambda pu, fc: lambda e: e.tensor_tensor(out=hbuf[:, fc, 0:nt], in0=sgt[:, 0:nt], in1=PS[pu][:, 0:nt], op=ALU.mult))(pu, fc),
                 reads=[B_sg, PSB[pu]], writes=[HB[fc]])
            if deferred:
                deferred.pop(0)()
        while deferred:
            deferred.pop(0)()
        if next_t is not None:
            ffn_cast(next_t)
        dbase = (l * 2 + fi) * 8
        for dc in range(8):
            sl = ring["dw"] % 2
            ring["dw"] += 1
            S.dma("pool", (lambda sl, idx: lambda e: e.dma_start(out=d_slot[sl][:], in_=w_d_d[idx].rearrange("p (f d) -> p f d", f=NFC)))(sl, dbase + dc),
                  f"dw{sl}", writes=[DW[sl]])
            py = 2 + dc % 2

            def mm2(e, sl=sl, py=py):
                for (h0, hn) in hv:
                    for fc in range(NFC):
                        ins = e.matmul(PS[py][:, h0:h0 + hn], lhsT=d_slot[sl][:, fc, :], rhs=hbuf[:, fc, h0:h0 + hn], start=(fc == 0), stop=(fc == NFC - 1))
                return ins
            S.op("pe", mm2, reads=[DW[sl]] + HB, writes=[PSB[py]])
            S.op("dve", (lambda dc, py: lambda e: e.scalar_tensor_tensor(out=xres[:, dc, t0:t0 + nt], in0=PS[py][:, 0:nt], scalar=0.5 / ALPHA,
                                                                           in1=xres[:, dc, t0:t0 + nt], op0=ALU.mult, op1=ALU.add))(dc, py),
                 reads=[PSB[py], xr_bufs[dc]], writes=[xr_bufs[dc]])
            if dc >= 1:
                ln_stats_chunk(dc - 1, t0, nt, xr_bufs)
        ln_stats_chunk(7, t0, nt, xr_bufs)
        ln_finish_head(nt)
        pieces = [(lambda c: lambda: ln_finish_chunk(l, s, c, t0, nt, xr_bufs))(c) for c in range(8)]
        if defer:
            return pieces
        for p in pieces:
            p()
        return None

    def dump_and_finish():
        ncols = OWN if dbg is None else TC
        S.barrier()
        S.wait_all("sp", ["pe", "act", "dve"])
        for c in range(8):
            S.dma("sp", (lambda c: lambda e: e.dma_start(out=y_d[:, c, :], in_=xres[:, c, 0:ncols]))(c), "out",
                  reads=[b for t in range(3) for b in XR[t]])
        S.wait_all("sp", ["out"])
        S.wait_all("pool", ["gu0", "gu1", "gu2", "dw0", "dw1", "pw", "aw0", "aw1", "ao"])
        with nc.Block() as block:
            S.emit(block)
        return nc

    def ffn_jobs(jobs, extra_after=None):
        pend = None
        for i, (l, fi, s_, t) in enumerate(jobs):
            nxt = jobs[i + 1][3] if i + 1 < len(jobs) else None
            pend = ffn_tile(l, fi, s_, t, XR[t], do_cast=(i == 0), next_t=nxt, deferred=(pend if i > 0 else None), defer=(nxt is not None))
            if extra_after and i in extra_after:
                assert pend is not None
                pend = pend + list(extra_after[i])

    ffn_jobs([(0, 0, 0, t) for t in range(3)])
    if dbg == "l0ffn1":
        return dump_and_finish()

    S.barrier()
    ALLX = [b for t in range(3) for b in XR[t]]
    B_pw = Buf("pw")
    S.dma("pool", lambda e: e.dma_start(out=p_win[:], in_=w_pin_d.rearrange("p (k e) -> p k e", k=8)), "pw", writes=[B_pw])
    S.dma("pool", lambda e: e.dma_start(out=p_wout[:], in_=w_pout_d.rearrange("p (k e) -> p k e", k=8)), "pw", writes=[B_pw])
    S.dma("pool", lambda e: e.dma_start(out=p_wgrp[:], in_=w_pgrp_d.rearrange("p (g c e) -> p g c e", g=4, c=2)), "pw", writes=[B_pw])
    B_pw.wdep = ("pw", S.cnt["pw"])
    B_pxb = [Buf(f"pxb{c}") for c in range(8)]
    B_pu = [Buf(f"pu{c}") for c in range(8)]
    B_pmx = [Buf(f"pmx{c}") for c in range(8)]
    B_pyb = [Buf(f"pyb{c}") for c in range(8)]
    B_pt = [Buf("pt0"), Buf("pt1")]
    B_pwn = Buf("pwn")
    B_hs = Buf("hsave")
    S.op("dve", lambda e: e.memset(p_u[:, :, 0:8], 0.0), writes=B_pu)
    TP = 448
    ptiles = [(a, min(a + TP, OWN)) for a in range(0, OWN, TP)]
    XP = [Buf(f"xp{c}") for c in range(8)]
    def pool_tile(ti, a, b):
        n_o = b - a
        first = (ti == 0)
        ua = 0 if first else a - 8
        ub = b + 8
        n_u = ub - ua
        loff = 8 if first else 0
        for c in range(8):
            if first:
                S.op("act", (lambda c: lambda e: e.activation(out=p_xb[:, c, 0:n_u], in_=xres[:, c, ua:ub], func=AF.Copy))(c), reads=[XP[c]], writes=[B_pxb[c]])
            else:
                S.op("act", (lambda c: lambda e: e.activation(out=p_xb[:, c, 0:8], in_=hsave[:, c, :], func=AF.Copy))(c), reads=[B_hs], writes=[B_pxb[c]])
                S.op("act", (lambda c: lambda e: e.activation(out=p_xb[:, c, 8:n_u], in_=xres[:, c, a:ub], func=AF.Copy))(c), reads=[XP[c]], writes=[B_pxb[c]])
        S.op("dve", lambda e: e.tensor_copy(out=hsave[:, :, :], in_=xres[:, :, b - 8:b]), reads=XP + B_pxb, writes=[B_hs])
        for c in range(8):
            pp = c % 4

            def mmu(e, c=c, pp=pp):
                for kc in range(8):
                    ins = e.matmul(PS[pp][:, 0:n_u], lhsT=p_win[:, kc, c * 128:(c + 1) * 128], rhs=p_xb[:, kc, 0:n_u], start=(kc == 0), stop=(kc == 7))
                return ins
            S.op("pe", mmu, reads=[B_pw] + B_pxb, writes=[PSB[pp]])
            S.op("act", (lambda c, pp: lambda e: e.activation(out=p_u[:, c, loff:loff + n_u], in_=PS[pp][:, 0:n_u], func=AF.Copy))(c, pp),
                 reads=[PSB[pp]], writes=[B_pu[c]])
        n_loc = n_u + loff
        lo0 = 8
        for c in range(8):
            g = c // 2
            h = 1 << g
            w = 2 * h
            src = p_u[:, c, :]
            srcb = B_pu[c]
            ln = 1
            k = 0
            while ln < w:
                dst = p_t[k % 2]
                cnt = n_loc - 2 * ln + 1
                S.op("dve", (lambda src, dst, ln, cnt: lambda e: e.tensor_tensor(out=dst[:, 0:cnt], in0=src[:, 0:cnt], in1=src[:, ln:ln + cnt], op=ALU.add))(src, dst, ln, cnt),
                     reads=[srcb], writes=[B_pt[k % 2]])
                src = dst
                srcb = B_pt[k % 2]
                ln *= 2
                k += 1
            s0 = lo0 - h
            S.op("dve", (lambda src, s0: lambda e: e.tensor_scalar(out=p_wn[:, 0:n_o], in0=src[:, s0 + 1:s0 + 1 + n_o], scalar1=pflag[:, 1:2], scalar2=None, op0=ALU.mult))(src, s0),
                 reads=[srcb, B_const], writes=[B_pwn])
            S.op("dve", (lambda src, s0: lambda e: e.scalar_tensor_tensor(out=p_wn[:, 0:n_o], in0=src[:, s0:s0 + n_o], scalar=pflag[:, 0:1], in1=p_wn[:, 0:n_o],
                                                                            op0=ALU.mult, op1=ALU.add))(src, s0),
                 reads=[srcb, B_const, B_pwn], writes=[B_pwn])
            S.op("dve", (lambda c, w: lambda e: e.scalar_tensor_tensor(out=p_mx[:, c, 0:n_o], in0=p_wn[:, 0:n_o], scalar=1.0 / w, in1=p_u[:, c, lo0:lo0 + n_o],
                                                                         op0=ALU.mult, op1=ALU.subtract))(c, w),
                 reads=[B_pwn, B_pu[c]], writes=[B_pmx[c]])
            if first:
                S.op("dve", (lambda c: lambda e: e.tensor_tensor(out=p_wn[:, 0:8], in0=p_wn[:, 0:8], in1=pinv[:, c * 8:c * 8 + 8], op=ALU.mult))(c),
                     reads=[B_pwn, B_const], writes=[B_pwn])
                S.op("dve", (lambda c: lambda e: e.tensor_tensor(out=p_mx[:, c, 0:8], in0=p_wn[:, 0:8], in1=p_u[:, c, lo0:lo0 + 8], op=ALU.subtract))(c),
                     reads=[B_pwn, B_pu[c]], writes=[B_pmx[c]])
        for ec in range(8):
            g = ec // 2
            eh = ec % 2
            pp = ec % 4

            def mmg(e, g=g, eh=eh, pp=pp):
                for cc in range(2):
                    ins = e.matmul(PS[pp][:, 0:n_o], lhsT=p_wgrp[:, g, cc, eh * 128:(eh + 1) * 128], rhs=p_mx[:, 2 * g + cc, 0:n_o], start=(cc == 0), stop=(cc == 1))
                return ins
            S.op("pe", mmg, reads=[B_pw, B_pmx[2 * g], B_pmx[2 * g + 1]], writes=[PSB[pp]])
            S.op("act", (lambda ec, pp: lambda e: e.activation(out=p_yb[:, ec, 0:n_o], in_=PS[pp][:, 0:n_o], func=AF.Identity, scale=pscale[:, ec:ec + 1]))(ec, pp),
                 reads=[PSB[pp], B_const], writes=[B_pyb[ec]])
        if dbg == "pooldbg" and ti == 0:
            S.barrier()
            Bd = Buf("dbgout")
            S.dma("pool", lambda e: e.dma_start(out=y_d[:, :, 0:528], in_=p_u[:, :, :]), "out", reads=B_pu)
            S.dma("pool", lambda e: e.dma_start(out=y_d[:, :, 600:1048], in_=p_mx[:, :, 0:448]), "out", reads=B_pmx)
            S.dma("pool", lambda e: e.dma_start(out=y_d[:, :, 1100:1548], in_=p_yb[:, :, 0:448]), "out", reads=B_pyb)
            S.dma("pool", lambda e: e.dma_start(out=y_d[:, :, 1600:2056], in_=p_xb[:, :, 0:456]), "out", reads=B_pxb)
            S.dma("pool", lambda e: e.dma_start(out=y_d[:, :, 2056:3080], in_=p_wout[:, :, :]), "out", reads=[B_pw])
            S.wait_all("pool", ["out"])
            with nc.Block() as block:
                S.emit(block)
            return nc
        for dc in range(8):
            pp = dc % 4

            def mmo(e, dc=dc, pp=pp):
                for ec in range(8):
                    ins = e.matmul(PS[pp][:, 0:n_o], lhsT=p_wout[:, ec, dc * 128:(dc + 1) * 128], rhs=p_yb[:, ec, 0:n_o], start=(ec == 0), stop=(ec == 7))
                return ins
            S.op("pe", mmo, reads=[B_pw] + B_pyb, writes=[PSB[pp]])
            S.op("dve", (lambda dc, pp: lambda e: e.scalar_tensor_tensor(out=xres[:, dc, a:b], in0=PS[pp][:, 0:n_o], scalar=1.0 / ALPHA, in1=xres[:, dc, a:b],
                                                                           op0=ALU.mult, op1=ALU.add))(dc, pp),
                 reads=[PSB[pp], XP[dc], B_hs], writes=[XP[dc]])
        if dbg == "pooldbg2" and ti == 0:
            S.barrier()
            for c in range(8):
                S.dma("sp", (lambda c: lambda e: e.dma_start(out=y_d[:, c, 0:448], in_=xres[:, c, 0:448]))(c), "out", reads=XP)
            S.wait_all("sp", ["out"])
            with nc.Block() as block:
                S.emit(block)
            return nc
        ln_cols(0, 1, a, n_o, XP)
        if dbg is not None and dbg.startswith("pooldbg3:") and ti == int(dbg.split(":")[1]):
            S.barrier()
            for c in range(8):
                S.dma("sp", (lambda c: lambda e: e.dma_start(out=y_d[:, c, :], in_=xres[:, c, :]))(c), "out", reads=XP)
            S.wait_all("sp", ["out"])
            with nc.Block() as block:
                S.emit(block)
            return nc
        return None
    for ti, (a, b) in enumerate(ptiles):
        _r = pool_tile(ti, a, b)
        if _r is not None:
            return _r
    S.barrier()
    if dbg == "l0pool":
        return dump_and_finish()

    if dbg == "l0":
        ffn_jobs([(0, 1, 2, t) for t in range(2)])
        return dump_and_finish()
    xc_in = nc.dram_tensor("xc_in", [128, 8192], BF16)
    xc_out = nc.dram_tensor("xc_out", [256, 8192], BF16)
    B_inb, B_outb = Buf("xc_in"), Buf("xc_out")

    def start_exchange():
        S.dma("pool", lambda e: e.dma_start(out=xc_in.ap().rearrange("p (c n) -> p c n", c=8), in_=xres[:, :, 1024:2048]), "xc",
              reads=XR[1], writes=[B_inb])
        S.op("pool", lambda e: e.collective_compute("AllGather", ALU.bypass, replica_groups=[[0, 1], [2, 3], [4, 5], [6, 7]],
                                                    ins=[xc_in.ap().opt()], outs=[xc_out.ap().opt()]),
             reads=[B_inb], writes=[B_outb])
    ffn_jobs([(0, 1, 2, 1), (0, 1, 2, 0), (1, 0, 0, 1), (1, 0, 0, 0)], extra_after={2: [start_exchange]})
    S.barrier()
    if dbg == "l1ffn1":
        return dump_and_finish()

    B_x1b = [Buf(f"x1b{c}") for c in range(8)]
    for c in range(8):
        if c % 2 == 0:
            S.op("dve", (lambda c: lambda e: e.tensor_copy(out=x1b[:, c, 0:OWN], in_=xres[:, c, 0:OWN]))(c), reads=ALLX, writes=[B_x1b[c]])
        else:
            S.op("act", (lambda c: lambda e: e.activation(out=x1b[:, c, 0:OWN], in_=xres[:, c, 0:OWN], func=AF.Copy))(c), reads=ALLX, writes=[B_x1b[c]])
    S.wait_all("sp", ["pe", "act", "dve"])
    stg = [at(f"stg{i}", (128, 2, 1024), BF16, OFF_RING + i * 4096) for i in range(2)]
    xtmp = at("xtmp", (128, 1024), BF16, OFF_RING + 8192)
    B_stg = [Buf("stg0"), Buf("stg1")]
    B_xtmp = Buf("xtmp")
    outv = xc_out.ap().rearrange("(r p) (c n) -> p r c n", r=2, c=8)

    def _rev(ap):
        pat = [list(p) for p in ap.ap]
        st, n = pat[-1]
        pat[-1] = [-st, n]
        return bass.AP(ap.tensor, ap.offset + st * (n - 1), pat)
    for c in range(8):
        sl = c % 2
        S.dma("sp", (lambda c, sl: lambda e: e.dma_start(out=stg[sl][:], in_=outv[:, :, c, :]))(c, sl), f"xs{sl}", reads=[B_outb], writes=[B_stg[sl]])
        S.op("dve", (lambda sl: lambda e: e.tensor_scalar(out=xtmp[:, :], in0=_rev(stg[sl][:, 0, :]), scalar1=pflag[:, 1:2], scalar2=None, op0=ALU.mult))(sl),
             reads=[B_stg[sl], B_const], writes=[B_xtmp])
        S.op("dve", (lambda c, sl: lambda e: e.scalar_tensor_tensor(out=x1b[:, c, OWN:TH], in0=_rev(stg[sl][:, 1, :]), scalar=pflag[:, 0:1], in1=xtmp[:, :],
                                                                     op0=ALU.mult, op1=ALU.add))(c, sl),
             reads=[B_stg[sl], B_xtmp, B_const], writes=[B_x1b[c]])
    S.barrier()
    B_vaug = Buf("vaug")
    S.op("dve", lambda e: e.memset(vaug[:, :, 64:128], 1.0), writes=[B_vaug])
    B_wq = [Buf("wqkv0"), Buf("wqkv1")]
    B_q = Buf("qT")
    B_k = Buf("kT")
    B_v = Buf("vT")
    B_pt2 = [Buf("PT0"), Buf("PT1"), Buf("PT2")]
    B_tt = [Buf("tt0"), Buf("tt1")]
    B_os = [[Buf(f"os{h}_{q}") for q in range(4)] for h in range(2)]
    B_rd = [Buf("rden0"), Buf("rden1")]
    B_ohp = Buf("ohp")
    B_wao = Buf("wao")
    XO = [[Buf(f"xo{dc}_{q}") for q in range(4)] for dc in range(8)]
    PSA = [Buf(f"psa{i}") for i in range(8)]
    psT = PSHI[:, :].bitcast(BF16)

    def osum(hd, qt):
        return os_t[:, hd * 4 + qt, 0:512]

    slopes = 2.0 ** (-8.0 * np.arange(1, 49) / 48.0)
    slopes = slopes.reshape(3, 16)
    rot = {"wk": 0, "tt": 0, "pt": 0, "wq": 0}

    def wk_slot():
        b = rot["wk"] % 4
        rot["wk"] += 1
        return b

    def attn_proj(hp, g, win, d):
        Lq = OWN // d
        Lk = Lq + 64
        ntok_k = Lk * d
        ws = rot["wq"] % 2
        rot["wq"] += 1
        wq = wqkv2[ws]
        S.dma("pool", (lambda idx, wq: lambda e: e.dma_start(out=wq[:], in_=w_qkv_d[idx].rearrange("p (m k f) -> p m k f", m=3, k=8)))(hp * 3 + g, wq),
              f"aw{ws}", writes=[B_wq[ws]])
        ev = 0
        for m, (dstT, dstB, ntok) in enumerate(((qT, B_q, OWN), (kT, B_k, ntok_k), (vT, B_v, ntok_k))):
            dview = dstT[:, 0:ntok].rearrange("p (r i) -> p r i", r=d)
            for j0 in range(0, ntok, 512):
                n = min(512, ntok - j0)
                bk = wk_slot()
                ph = bk * 512

                def mmp(e, m=m, j0=j0, n=n, ph=ph):
                    for kc in range(8):
                        ins = e.matmul(PSHI[:, ph:ph + n], lhsT=wq[:, m, kc, :], rhs=x1b[:, kc, j0:j0 + n], start=(kc == 0), stop=(kc == 7))
                    return ins
                S.op("pe", mmp, reads=[B_wq[ws]] + B_x1b, writes=[PSA[4 + bk]])
                i0 = j0 // d
                ni = n // d
                src = PSHI[:, ph:ph + n].rearrange("p (i r) -> p r i", r=d)
                if ev % 2 == 0:
                    S.op("act", (lambda dview, src, i0, ni: lambda e: e.activation(out=dview[:, :, i0:i0 + ni], in_=src, func=AF.Copy))(dview, src, i0, ni),
                         reads=[PSA[4 + bk]], writes=[dstB])
                else:
                    S.op("dve", (lambda dview, src, i0, ni: lambda e: e.tensor_copy(out=dview[:, :, i0:i0 + ni], in_=src))(dview, src, i0, ni),
                         reads=[PSA[4 + bk]], writes=[dstB])
                ev += 1
        chunks = []
        for r in range(d):
            for k0 in range(0, Lk, 128):
                chunks.append((r, k0, min(128, Lk - k0)))
        for c0 in range(0, len(chunks), 4):
            grp = chunks[c0:c0 + 4]
            ng = len(grp)
            bk = wk_slot()
            pbase = bk * 1024

            def tr(e, grp=grp, pbase=pbase):
                for q, (r, k0, nk) in enumerate(grp):
                    ins = e.transpose(psT[0:nk, pbase + q * 128:pbase + (q + 1) * 128], vT[:, r * Lk + k0:r * Lk + k0 + nk], ident_bf[:, :])
                return ins
            S.op("pe", tr, reads=[B_v, B_ident], writes=[PSA[4 + bk]])
            srcv = psT[:, pbase:pbase + ng * 128].rearrange("p (q f) -> p q f", f=128)
            if (c0 // 4) % 2 == 0:
                S.op("act", (lambda c0, ng, srcv: lambda e: e.activation(out=vaug[:, c0:c0 + ng, 0:64], in_=srcv[:, :, 0:64], func=AF.Copy))(c0, ng, srcv),
                     reads=[PSA[4 + bk]], writes=[B_vaug])
                S.op("act", (lambda c0, ng, srcv: lambda e: e.activation(out=vaug[:, c0:c0 + ng, 128:192], in_=srcv[:, :, 64:128], func=AF.Copy))(c0, ng, srcv),
                     reads=[PSA[4 + bk]], writes=[B_vaug])
            else:
                S.op("dve", (lambda c0, ng, srcv: lambda e: e.tensor_copy(out=vaug[:, c0:c0 + ng, 0:64], in_=srcv[:, :, 0:64]))(c0, ng, srcv),
                     reads=[PSA[4 + bk]], writes=[B_vaug])
                S.op("dve", (lambda c0, ng, srcv: lambda e: e.tensor_copy(out=vaug[:, c0:c0 + ng, 128:192], in_=srcv[:, :, 64:128]))(c0, ng, srcv),
                     reads=[PSA[4 + bk]], writes=[B_vaug])
        return {"Lq": Lq, "Lk": Lk, "chunks": chunks}

    def attn_core(hp, g, win, d, ctx):
        Lq, Lk, chunks = ctx["Lq"], ctx["Lk"], ctx["chunks"]
        items = []
        for hd in range(2):
            work = []
            for ci, (r, k0, nk) in enumerate(chunks):
                i0 = max(0, k0 - 64)
                i1 = min(Lq, k0 + nk + 64)
                if i1 > i0:
                    work.append((ci, r, k0, nk, i0, i1))
            npair = (len(work) + 1) // 2
            for pi in range(npair):
                items.append({"hd": hd, "pair": work[2 * pi:2 * pi + 2], "last": pi == npair - 1})
        started = {0: [False] * 4, 1: [False] * 4}
        accv = PSLO[:, :].rearrange("p (r i) -> p i r", r=d)

        def emit_scores(it):
            hd = it["hd"]
            pair = it["pair"]
            hg = 2 * hp + hd
            cneg = -float(slopes[g, hg]) * d * 8.0
            hrow = slice(64 * hd, 64 * hd + 64)
            bk = wk_slot()
            ph = bk * 512

            def mms(e, pair=pair, ph=ph, hrow=hrow):
                for q, (ci, r, k0, nk, i0, i1) in enumerate(pair):
                    doff = i0 - (k0 - 64)
                    nq = i1 - i0
                    ins = e.matmul(PSHI[0:nk, ph + q * 256 + doff:ph + q * 256 + doff + nq], lhsT=kT[hrow, r * Lk + k0:r * Lk + k0 + nk],
                                   rhs=qT[hrow, r * Lq + i0:r * Lq + i0 + nq], start=True, stop=True, skip_group_check=True)
                return ins
            S.op("pe", mms, reads=[B_k, B_q], writes=[PSA[4 + bk]])
            ts = rot["tt"] % 2
            rot["tt"] += 1
            ps_ = rot["pt"] % 3
            rot["pt"] += 1
            it["ps"] = ps_
            S.op("dve", (lambda ph, ts, cneg: lambda e: e.scalar_tensor_tensor(
                out=tt[ts][:, :], in0=dtile[:, :], scalar=cneg, in1=PSHI[:, ph:ph + 512], op0=ALU.mult, op1=ALU.add))(ph, ts, cneg),
                reads=[B_const, PSA[4 + bk]], writes=[B_tt[ts]])
            S.op("act", (lambda ts, ps_: lambda e: e.activation(out=PTt[ps_][:, :], in_=tt[ts][:, :], func=AF.Exp, scale=0.125))(ts, ps_),
                 reads=[B_tt[ts]], writes=[B_pt2[ps_]])

        def emit_pv(it):
            hd = it["hd"]
            pair = it["pair"]
            ps_ = it["ps"]
            vsel = (slice(0, 128) if hd == 0 else slice(64, 192))
            st = started[hd]
            plan = []
            wb = set()
            for q, (ci, r, k0, nk, i0, i1) in enumerate(pair):
                doff = i0 - (k0 - 64)
                pos = r * Lq + i0
                end = r * Lq + i1
                while pos < end:
                    bb = pos // 512
                    nx = min(end, (bb + 1) * 512)
                    plan.append((ci, nk, q * 256 + doff + (pos - (r * Lq + i0)), pos, nx - pos, not st[bb]))
                    st[bb] = True
                    wb.add(bb)
                    pos = nx

            def mmv(e, plan=plan, ps_=ps_, vsel=vsel):
                for (ci, nk, pcol, pos, n, stt) in plan:
                    ins = e.matmul(PSLO[:, pos:pos + n], lhsT=vaug[0:nk, ci, vsel], rhs=PTt[ps_][0:nk, pcol:pcol + n], start=stt, stop=True,
                                   skip_group_check=True)
                return ins
            S.op("pe", mmv, reads=[B_vaug, B_pt2[ps_]], writes=[PSA[bb] for bb in sorted(wb)])
            if it["last"]:
                for qt in range(4):
                    ia = 512 * qt // d
                    nn = 512 // d
                    srcp = accv[:, ia:ia + nn, :]
                    dst = osum(hd, qt).rearrange("p (i r) -> p i r", r=d)
                    rb = [PSA[qt]] if d == 1 else [PSA[0], PSA[1], PSA[2], PSA[3]]
                    if g == 0:
                        S.op("act", (lambda dst, srcp: lambda e: e.activation(out=dst, in_=srcp, func=AF.Copy))(dst, srcp),
                             reads=rb, writes=[B_os[hd][qt]])
                    else:
                        S.op("dve", (lambda dst, srcp: lambda e: e.tensor_tensor(out=dst, in0=dst, in1=srcp, op=ALU.add))(dst, srcp),
                             reads=rb + [B_os[hd][qt]], writes=[B_os[hd][qt]])

        LA = 2
        for idx in range(len(items) + LA):
            if idx < len(items):
                emit_scores(items[idx])
            if idx - LA >= 0:
                emit_pv(items[idx - LA])

    def attn_finish(hp):
        for hd in range(2):
            nrow = slice(0, 64) if hd == 0 else slice(64, 128)
            drow = slice(64, 128) if hd == 0 else slice(0, 64)
            osv = os_t[:, hd * 4:hd * 4 + 4, 0:512]
            rdv = os_t[:, hd * 4:hd * 4 + 4, 512:1024]
            ohv = o_hp[:, :].rearrange("p (q n) -> p q n", q=4)
            allos = [B_os[hd][q] for q in range(4)]
            S.op("act", (lambda osv, rdv, nrow, drow: lambda e: e.activation(out=rdv[nrow], in_=osv[drow], func=AF.Ln))(osv, rdv, nrow, drow),
                 reads=allos, writes=[B_rd[hd]])
            S.op("act", (lambda rdv, nrow: lambda e: e.activation(out=rdv[nrow], in_=rdv[nrow], func=AF.Exp, scale=-1.0))(rdv, nrow),
                 reads=[B_rd[hd]], writes=[B_rd[hd]])
            S.op("dve", (lambda osv, rdv, ohv, nrow: lambda e: e.tensor_tensor(out=ohv[nrow], in0=osv[nrow], in1=rdv[nrow], op=ALU.mult))(osv, rdv, ohv, nrow),
                 reads=allos + [B_rd[hd]], writes=[B_ohp])

    def attn_oproj(hp):
        S.dma("pool", (lambda hp: lambda e: e.dma_start(out=wao[:], in_=w_ao_d[hp]))(hp), "ao", writes=[B_wao])
        for dc in range(8):
            for qt in range(4):
                bk = wk_slot()
                ph = bk * 512

                def mmo2(e, dc=dc, qt=qt, ph=ph):
                    return e.matmul(PSHI[:, ph:ph + 512], lhsT=wao[:, dc * 128:(dc + 1) * 128], rhs=o_hp[:, qt * 512:(qt + 1) * 512], start=True, stop=True)
                S.op("pe", mmo2, reads=[B_wao, B_ohp], writes=[PSA[4 + bk]])
                S.op("dve", (lambda dc, qt, ph: lambda e: e.scalar_tensor_tensor(out=xres[:, dc, qt * 512:(qt + 1) * 512], in0=PSHI[:, ph:ph + 512], scalar=1.0 / ALPHA,
                                                                               in1=xres[:, dc, qt * 512:(qt + 1) * 512], op0=ALU.mult, op1=ALU.add))(dc, qt, ph),
                     reads=[PSA[4 + bk], XO[dc][qt]], writes=[XO[dc][qt]])
    ctx_next = attn_proj(0, 0, DIL[0][0], DIL[0][1])
    for hp in range(8):
        for g, (win, d) in enumerate(DIL):
            ctx = ctx_next
            attn_core(hp, g, win, d, ctx)
            if g < 2:
                ctx_next = attn_proj(hp, g + 1, DIL[g + 1][0], DIL[g + 1][1])
        attn_finish(hp)
        if hp < 7:
            ctx_next = attn_proj(hp + 1, 0, DIL[0][0], DIL[0][1])
        attn_oproj(hp)
    S.barrier()
    for t in range(2):
        t0, nt = TILES[t]
        ln_cols(1, 1, t0, nt, XR[t])
    S.barrier()
    if dbg == "l1attn":
        return dump_and_finish()
    ffn_jobs([(1, 1, 2, t) for t in range(2)])
    return dump_and_finish()


def _prep_shared(ffn1_w_gate, ffn1_w_up, ffn1_w_down, ffn2_w_gate, ffn2_w_up, ffn2_w_down,
                 ln_gain, ln_bias, pool_w_in, pool_w_group, pool_scale, pool_w_out, attn_w_qkv, attn_w_out):
    f = np.float32
    gates = [ffn1_w_gate, ffn2_w_gate]
    ups = [ffn1_w_up, ffn2_w_up]
    downs = [ffn1_w_down, ffn2_w_down]
    w_gu = np.empty((2, 2, NFC, 128, 2, 8, 128), f)
    w_d = np.empty((2, 2, 8, 128, NFC, 128), f)
    for l in range(2):
        for fi in range(2):
            g = np.asarray(gates[fi][l], f).reshape(8, 128, NFC, 128)
            u = np.asarray(ups[fi][l], f).reshape(8, 128, NFC, 128)
            w_gu[l, fi, :, :, 0] = g.transpose(2, 1, 0, 3)
            w_gu[l, fi, :, :, 1] = u.transpose(2, 1, 0, 3)
            dn = np.asarray(downs[fi][l], f).reshape(NFC, 128, 8, 128)
            w_d[l, fi] = dn.transpose(2, 1, 0, 3)
    lnp = np.empty((128, 2, 3, 2, 8), f)
    lnp[:, :, :, 0, :] = np.asarray(ln_gain, f).reshape(2, 3, 8, 128).transpose(3, 0, 1, 2)
    lnp[:, :, :, 1, :] = np.asarray(ln_bias, f).reshape(2, 3, 8, 128).transpose(3, 0, 1, 2)
    w_pin = np.asarray(pool_w_in[0], f).reshape(8, 128, 1024).transpose(1, 0, 2)
    w_pout = np.asarray(pool_w_out[0], f).reshape(8, 128, 1024).transpose(1, 0, 2)
    w_pgrp = np.asarray(pool_w_group[0], f).reshape(4, 2, 128, 256).transpose(2, 0, 1, 3)
    pscale = np.asarray(pool_scale[0], f).reshape(8, 128).T
    wq = np.asarray(attn_w_qkv[0], f).reshape(8, 128, 3, 3, 8, 128)
    w_qkv = wq.transpose(4, 2, 1, 3, 0, 5)
    w_ao = np.asarray(attn_w_out[0], f).reshape(8, 128, 1024)
    r = np.arange(128)[:, None]
    i = np.arange(256)[None, :]
    dd = np.abs(r - i + 64).astype(f)
    dtile = np.where(dd <= 64, dd, BIGD).astype(f)
    dtile = np.concatenate([dtile, dtile], axis=1)
    c = np.ascontiguousarray
    return {
        "lnp": c(lnp.reshape(128, 96)), "pscale": c(pscale), "dtile": c(dtile), "ident": np.eye(128, dtype=f),
        "w_gu": c(w_gu.reshape(88, 128, 2048)), "w_d": c(w_d.reshape(32, 128, 2816)),
        "w_pin": c(w_pin.reshape(128, 8192)), "w_pout": c(w_pout.reshape(128, 8192)), "w_pgrp": c(w_pgrp.reshape(128, 2048)),
        "w_qkv": c(w_qkv.reshape(24, 128, 3072)), "w_ao": c(w_ao),
    }


def _prep_core(x, core):
    f = np.float32
    b, half = core // 2, core % 2
    xs = np.asarray(x[b], f)
    if half == 0:
        loc = xs[0:TC]
    else:
        loc = xs[::-1][0:TC]
    xT = np.ascontiguousarray(loc.T.reshape(8, 128, TC).transpose(1, 0, 2))
    pflag = np.zeros((128, 2), f)
    pflag[:, half] = 1.0
    pinv = np.empty((128, 8, 8), f)
    for cidx in range(8):
        h = 1 << (cidx // 2)
        w = 2 * h
        for jo in range(8):
            if half == 0:
                cnt = h + min(h, jo)
            else:
                cnt = h + min(h, jo + 1)
            pinv[:, cidx, jo] = 1.0 / cnt
    return {"xT": xT, "pflag": pflag, "pinv": np.ascontiguousarray(pinv.reshape(128, 64))}


_NC_CACHE = {}


def _get_nc(dbg=None):
    if dbg not in _NC_CACHE:
        _NC_CACHE[dbg] = build_program(dbg)
    return _NC_CACHE[dbg]


def run_cores(inputs, dbg=None, cores=range(8), trace=False):
    x = np.asarray(inputs["x"], np.float32)
    shared = _prep_shared(**{k: v for k, v in inputs.items() if k != "x"})
    in_maps = []
    for core in cores:
        m = dict(shared)
        m.update(_prep_core(x, core))
        in_maps.append(m)
    nc = _get_nc(dbg)
    res = run_bass_kernel_spmd(nc, in_maps, core_ids=list(range(len(in_maps))), **({"trace": True} if trace else {}))
    return res


def kernel(**inputs):
    res = run_cores(inputs)
    out = np.empty((BATCH, SEQ, D_MODEL), np.float32)
    for core in range(8):
        yT = np.asarray(res.results[core]["yT"], np.float32)
        y = yT.transpose(2, 1, 0).reshape(OWN, D_MODEL)
        b, half = core // 2, core % 2
        if half == 0:
            out[b, 0:OWN] = y
        else:
            out[b, OWN:SEQ] = y[::-1]
    return out
```

```python
import numpy as np
import concourse.bass as bass
import concourse.mybir as mybir
from concourse.bass_utils import run_bass_kernel_spmd

F32 = mybir.dt.float32
BF16 = mybir.dt.bfloat16
AF = mybir.ActivationFunctionType
ALU = mybir.AluOpType

D_MODEL = 1024
SEQ = 4096
BATCH = 4
D_FF = 2816
NFC = 22
OWN = 2048
HALO = 1024
TC = OWN + 8
TH = OWN + HALO
ALPHA = 4.0 ** 0.25
LN_EPS = 1e-5
EPS_P = LN_EPS / (ALPHA * ALPHA)
DIL = ((128, 1), (512, 4), (2048, 16))
BIGD = 1.0e5

ENGS = ("pe", "act", "dve", "pool", "sp")


class Buf:
    __slots__ = ("name", "wdep", "rdeps")

    def __init__(self, name):
        self.name = name
        self.wdep = None
        self.rdeps = {}


class Sched:
    def __init__(self, nc):
        self.nc = nc
        self.sem = {}
        self.cnt = {}
        for e in ENGS:
            self.sem[e] = nc.alloc_semaphore("s_" + e)
            self.cnt[e] = 0
        self.seen = {e: {} for e in ENGS}
        self.ops = {e: [] for e in ENGS}

    def new_dma_sem(self, key):
        self.sem[key] = self.nc.alloc_semaphore("d_" + key)
        self.cnt[key] = 0
        return key

    def _collect(self, reads, writes):
        w = {}
        for b in reads:
            d = b.wdep
            if d is not None and w.get(d[0], 0) < d[1]:
                w[d[0]] = d[1]
        for b in writes:
            d = b.wdep
            if d is not None and w.get(d[0], 0) < d[1]:
                w[d[0]] = d[1]
            for k, v in b.rdeps.items():
                if w.get(k, 0) < v:
                    w[k] = v
        return w

    def _need(self, e, w):
        need = []
        s = self.seen[e]
        for k, v in w.items():
            if s.get(k, 0) < v:
                need.append((k, v))
                s[k] = v
        return need

    def op(self, e, fn, reads=(), writes=()):
        need = self._need(e, self._collect(reads, writes))
        self.cnt[e] += 1
        v = self.cnt[e]
        self.ops[e].append((need, fn, (e, 1)))
        for b in reads:
            if b.rdeps.get(e, 0) < v:
                b.rdeps[e] = v
        for b in writes:
            b.wdep = (e, v)
            b.rdeps = {}
        return (e, v)

    def dma(self, q, fn, semkey, reads=(), writes=()):
        need = self._need(q, self._collect(reads, writes))
        self.cnt[semkey] += 16
        v = self.cnt[semkey]
        self.ops[q].append((need, fn, (semkey, 16)))
        for b in reads:
            if b.rdeps.get(semkey, 0) < v:
                b.rdeps[semkey] = v
        for b in writes:
            b.wdep = (semkey, v)
            b.rdeps = {}
        return (semkey, v)

    def wait_all(self, e, keys):
        w = {k: self.cnt[k] for k in keys if self.cnt[k] > 0}
        need = self._need(e, w)
        if need:
            self.ops[e].append((need, None, None))

    def barrier(self, engines=("pe", "act", "dve", "pool")):
        for e in engines:
            self.wait_all(e, list(engines))

    def emit(self, block):
        sched = self

        def make(e):
            def body(eng):
                for need, fn, inc in sched.ops[e]:
                    for k, v in need:
                        eng.wait_ge(sched.sem[k], v)
                    if fn is not None:
                        ins = fn(eng)
                        ins.then_inc(sched.sem[inc[0]], inc[1])
            return body
        block.tensor(make("pe"))
        block.scalar(make("act"))
        block.vector(make("dve"))
        block.gpsimd(make("pool"))
        block.sync(make("sp"))


def _halves(nt):
    return [(h0, min(512, nt - h0)) for h0 in range(0, nt, 512)]


def build_program(dbg=None):
    nc = bass.Bass("TRN2", target_bir_lowering=False)
    S = Sched(nc)

    def din(name, shape):
        return nc.dram_tensor(name, list(shape), F32, kind="ExternalInput").ap()

    def _rev(ap):
        pat = [list(p) for p in ap.ap]
        st, n = pat[-1]
        pat[-1] = [-st, n]
        return bass.AP(ap.tensor, ap.offset + st * (n - 1), pat)

    xT_d = din("xT", (128, 8, TC))
    lnp_d = din("lnp", (128, 96))
    pflag_d = din("pflag", (128, 2))
    pinv_d = din("pinv", (128, 64))
    pscale_d = din("pscale", (128, 8))
    dtile_d = din("dtile", (128, 512))
    ident_d = din("ident", (128, 128))
    w_gu_d = din("w_gu", (88, 128, 2048))
    w_d_d = din("w_d", (32, 128, 2816))
    w_pin_d = din("w_pin", (128, 8192))
    w_pout_d = din("w_pout", (128, 8192))
    w_pgrp_d = din("w_pgrp", (128, 2048))
    w_qkv_d = din("w_qkv", (24, 128, 3072))
    w_ao_d = din("w_ao", (8, 128, 1024))
    if dbg is None:
        y_d = nc.dram_tensor("yT", [128, 8, OWN], F32, kind="ExternalOutput").ap()
    else:
        y_d = nc.dram_tensor("yT", [128, 8, TC], F32, kind="ExternalOutput").ap()

    base = (nc.sbuf_base + 31) // 32 * 32
    OFF_X = 0
    OFF_OS = 65792
    OFF_C = 98560
    OFF_R1 = OFF_C + 4096
    OFF_RING = OFF_R1 + 61440
    OFF_LN = OFF_RING + 23552
    OFF_SP = OFF_LN + 16384
    ARENA = OFF_SP + 4608
    arena = nc.alloc_sbuf_tensor("arena", [128, ARENA // 4], F32)
    abase = base
    assert nc.sbuf_base >= abase + ARENA

    def at(name, shape, dtype, off):
        return nc.alloc_sbuf_tensor_at(name, list(shape), dtype, offset=abase + off)

    xres = at("xres", (128, 8, TC), F32, OFF_X)
    os_t = at("os_t", (128, 8, 1024), F32, OFF_OS)
    lnp = at("lnp", (128, 96), F32, OFF_C)
    pflag = at("pflag", (128, 2), F32, OFF_C + 384)
    pinv = at("pinv", (128, 64), F32, OFF_C + 416)
    pscale = at("pscale", (128, 8), F32, OFF_C + 672)
    dtile = at("dtile", (128, 512), F32, OFF_C + 704)
    ones_bf = at("ones_bf", (128, 128), BF16, OFF_C + 2752)
    ident_bf = at("ident_bf", (128, 128), BF16, OFF_C + 3008)
    hsave = at("hsave", (128, 8, 8), F32, OFF_C + 3264)
    epst = at("epst", (128, 1), F32, OFF_C + 3520)
    negone = at("negone", (128, 1), F32, OFF_C + 3552)
    hbuf = at("hbuf", (128, NFC, 1024), BF16, OFF_R1)
    xb = at("xb", (128, 8, 1024), BF16, OFF_R1 + 45056)
    gu_slot = [at(f"gu{i}", (128, 2, 8, 128), BF16, OFF_RING + i * 4096) for i in range(3)]
    d_slot = [at(f"dw{i}", (128, NFC, 128), BF16, OFF_RING + 12288 + i * 5632) for i in range(2)]
    zb = at("zb", (128, 2, 1024), BF16, OFF_LN)
    zq = at("zq", (128, 2, 1024), BF16, OFF_LN + 4096)
    meant = at("meant", (128, 1024), F32, OFF_LN + 8192)
    rstdt = at("rstdt", (128, 1024), F32, OFF_LN + 12288)
    sgt = at("sgt", (128, 1024), F32, OFF_SP)
    p_win = at("p_win", (128, 8, 1024), BF16, OFF_OS)
    p_wout = at("p_wout", (128, 8, 1024), BF16, OFF_OS + 16384)
    p_wgrp = at("p_wgrp", (128, 4, 2, 256), BF16, OFF_R1 + 32768)
    p_xb = at("p_xb", (128, 8, 512), BF16, OFF_R1 + 36864)
    p_yb = at("p_yb", (128, 8, 512), BF16, OFF_R1 + 45056)
    p_mx = at("p_mx", (128, 8, 512), BF16, OFF_R1 + 53248)
    p_u = at("p_u", (128, 8, 528), F32, OFF_RING)
    p_t = [at(f"p_t{i}", (128, 528), F32, OFF_RING + 16896 + i * 2112) for i in range(2)]
    p_wn = at("p_wn", (128, 512), F32, OFF_RING + 21120)
    x1b = at("x1b", (128, 8, TH), BF16, OFF_R1)
    qT = at("qT", (128, OWN), BF16, OFF_R1 + 49152)
    kT = at("kT", (128, TH), BF16, OFF_R1 + 53248)
    PTt = [at(f"PT{i}", (128, 512), BF16, OFF_RING + 20480 + i * 1024) for i in range(3)]
    wqkv2 = [at(f"wqkv{i}", (128, 3, 8, 128), BF16, OFF_RING + i * 6144) for i in range(2)]
    vT = at("vT", (128, TH), BF16, OFF_RING + 12288)
    wao = at("wao", (128, 1024), BF16, OFF_RING + 18432)
    vaug = at("vaug", (128, 32, 192), BF16, OFF_LN)
    o_hp = at("o_hp", (128, OWN), BF16, OFF_LN + 12288)
    tt = [at(f"tt{i}", (128, 512), F32, OFF_SP + i * 2048) for i in range(2)]

    PSLO = nc.alloc_psum_tensor("pslo", [128, 2048], F32)
    PSHI = nc.alloc_psum_tensor("pshi", [128, 2048], F32)
    PS = [PSLO[:, 0:1024], PSLO[:, 1024:2048], PSHI[:, 0:1024], PSHI[:, 1024:2048]]
    PSB = [Buf(f"ps{i}") for i in range(4)]

    for k in ["x0", "x1", "xc", "xe", "xs0", "xs1", "const", "gu0", "gu1", "gu2", "dw0", "dw1", "pw", "aw0", "aw1", "ao", "out"]:
        S.new_dma_sem(k)

    B_const = Buf("const")
    B_ident = Buf("ident")
    B_ones = Buf("ones")
    TILES = [(0, 1024), (1024, 1024)]
    XR = [[Buf(f"xr{t}_{c}") for c in range(8)] for t in range(2)]
    XB = [Buf(f"xb{c}") for c in range(8)]
    HB = [Buf(f"hb{f}") for f in range(NFC)]
    GU = [Buf(f"gu{i}") for i in range(3)]
    DW = [Buf(f"dw{i}") for i in range(2)]
    B_sg = Buf("sg")
    B_zb = [Buf("zb0"), Buf("zb1")]
    B_zq = [Buf("zq0"), Buf("zq1")]
    B_mean = Buf("mean")
    B_rstd = Buf("rstd")
    ring = {"gu": 0, "dw": 0}

    for (dst, src) in [(lnp, lnp_d), (pflag, pflag_d), (pinv, pinv_d), (pscale, pscale_d), (dtile, dtile_d)]:
        S.dma("sp", (lambda dst, src: lambda e: e.dma_start(out=dst[:], in_=src))(dst, src), "const", writes=[B_const])
    B_const.wdep = ("const", S.cnt["const"])
    S.dma("pool", lambda e: e.dma_start(out=ident_bf[:], in_=ident_d), "pw", writes=[B_ident])
    B_pw = Buf("pw")
    B_pwg = Buf("pwg")
    S.op("dve", lambda e: e.memset(ones_bf[:], 1.0), writes=[B_ones])
    S.op("dve", lambda e: e.memset(negone[:], -1.0), writes=[B_const])
    S.op("dve", lambda e: e.memset(epst[:], EPS_P), writes=[B_const])
    B_const.wdep = ("const", S.cnt["const"])
    B_eps = Buf("eps")
    B_eps.wdep = ("dve", S.cnt["dve"])
    for t, (t0, nt) in enumerate(TILES):
        S.dma("sp", (lambda t0, nt: lambda e: e.dma_start(out=xres[:, :, t0:t0 + nt], in_=xT_d[:, :, t0:t0 + nt]))(t0, nt),
              f"x{t}", writes=XR[t])

    def ln_stats_chunk(c, t0, nt, xr_bufs):
        hv = _halves(nt)
        sl = c % 2
        S.op("act", (lambda c, sl: lambda e: e.activation(out=zb[:, sl, 0:nt], in_=xres[:, c, t0:t0 + nt], func=AF.Copy))(c, sl),
             reads=[xr_bufs[c]], writes=[B_zb[sl]])
        S.op("act", (lambda c, sl: lambda e: e.activation(out=zq[:, sl, 0:nt], in_=xres[:, c, t0:t0 + nt], func=AF.Square))(c, sl),
             reads=[xr_bufs[c]], writes=[B_zq[sl]])

        def mm(e, c=c, sl=sl):
            for (h0, hn) in hv:
                e.matmul(PS[0][:, h0:h0 + hn], lhsT=ones_bf[:, :], rhs=zb[:, sl, h0:h0 + hn], start=(c == 0), stop=(c == 7))
            for (h0, hn) in hv:
                ins = e.matmul(PS[1][:, h0:h0 + hn], lhsT=ones_bf[:, :], rhs=zq[:, sl, h0:h0 + hn], start=(c == 0), stop=(c == 7))
            return ins
        S.op("pe", mm, reads=[B_zb[sl], B_zq[sl], B_ones], writes=[PSB[0], PSB[1]])

    def ln_cols(l, s, t0, nt, xr_bufs):
        for c in range(8):
            ln_stats_chunk(c, t0, nt, xr_bufs)
        ln_finish(l, s, t0, nt, xr_bufs)

    def ln_finish_head(nt):
        S.op("dve", lambda e: e.tensor_scalar(out=meant[:, 0:nt], in0=PS[0][:, 0:nt], scalar1=1.0 / D_MODEL, scalar2=None, op0=ALU.mult),
             reads=[PSB[0]], writes=[B_mean])
        S.op("dve", lambda e: e.tensor_tensor(out=rstdt[:, 0:nt], in0=meant[:, 0:nt], in1=meant[:, 0:nt], op=ALU.mult),
             reads=[B_mean], writes=[B_rstd])
        S.op("dve", lambda e: e.scalar_tensor_tensor(out=rstdt[:, 0:nt], in0=PS[1][:, 0:nt], scalar=1.0 / D_MODEL, in1=rstdt[:, 0:nt],
                                                     op0=ALU.mult, op1=ALU.subtract),
             reads=[PSB[1], B_rstd], writes=[B_rstd])
        S.op("act", lambda e: e.activation(out=rstdt[:, 0:nt], in_=rstdt[:, 0:nt], func=AF.Ln, bias=epst[:, 0:1], scale=1.0),
             reads=[B_rstd, B_eps], writes=[B_rstd])
        S.op("act", lambda e: e.activation(out=rstdt[:, 0:nt], in_=rstdt[:, 0:nt], func=AF.Exp, scale=-0.5),
             reads=[B_rstd], writes=[B_rstd])

    def ln_finish_chunk(l, s, c, t0, nt, xr_bufs):
        gi = (l * 3 + s) * 16
        S.op("dve", lambda e: e.tensor_tensor(out=xres[:, c, t0:t0 + nt], in0=xres[:, c, t0:t0 + nt], in1=meant[:, 0:nt], op=ALU.subtract),
             reads=[xr_bufs[c], B_mean], writes=[xr_bufs[c]])
        S.op("dve", lambda e: e.tensor_tensor(out=xres[:, c, t0:t0 + nt], in0=xres[:, c, t0:t0 + nt], in1=rstdt[:, 0:nt], op=ALU.mult),
             reads=[xr_bufs[c], B_rstd], writes=[xr_bufs[c]])
        S.op("act", lambda e: e.activation(out=xres[:, c, t0:t0 + nt], in_=xres[:, c, t0:t0 + nt], func=AF.Identity,
                                           bias=lnp[:, gi + 8 + c:gi + 9 + c], scale=lnp[:, gi + c:gi + c + 1]),
             reads=[xr_bufs[c], B_const], writes=[xr_bufs[c]])

    def ln_finish(l, s, t0, nt, xr_bufs):
        ln_finish_head(nt)
        for c in range(8):
            ln_finish_chunk(l, s, c, t0, nt, xr_bufs)

    def ffn_cast(t):
        t0, nt = TILES[t]
        xr_bufs = XR[t]
        for c in range(8):
            eng = "dve" if c % 2 == 0 else "act"
            if eng == "dve":
                S.op("dve", (lambda c: lambda e: e.tensor_copy(out=xb[:, c, 0:nt], in_=xres[:, c, t0:t0 + nt]))(c), reads=[xr_bufs[c]], writes=[XB[c]])
            else:
                S.op("act", (lambda c: lambda e: e.activation(out=xb[:, c, 0:nt], in_=xres[:, c, t0:t0 + nt], func=AF.Copy))(c), reads=[xr_bufs[c]], writes=[XB[c]])

    def ffn_tile(l, fi, s, t, xr_bufs, do_cast=True, next_t=None, deferred=None, defer=False):
        t0, nt = TILES[t]
        hv = _halves(nt)
        if do_cast:
            ffn_cast(t)
        wbase = (l * 2 + fi) * NFC
        for fc in range(NFC):
            sl = ring["gu"] % 3
            ring["gu"] += 1
            S.dma("pool", (lambda sl, idx: lambda e: e.dma_start(out=gu_slot[sl][:], in_=w_gu_d[idx].rearrange("p (a k f) -> p a k f", a=2, k=8)))(sl, wbase + fc),
                  f"gu{sl}", writes=[GU[sl]])
            pg, pu = (2, 3) if fc % 2 == 0 else (0, 1)

            def mm(e, sl=sl, pg=pg, pu=pu):
                for (pp, a) in ((pg, 0), (pu, 1)):
                    for (h0, hn) in hv:
                        for kc in range(8):
                            ins = e.matmul(PS[pp][:, h0:h0 + hn], lhsT=gu_slot[sl][:, a, kc, :], rhs=xb[:, kc, h0:h0 + hn], start=(kc == 0), stop=(kc == 7))
                return ins
            S.op("pe", mm, reads=[GU[sl]] + XB, writes=[PSB[pg], PSB[pu]])
            S.op("act", (lambda pg: lambda e: e.activation(out=sgt[:, 0:nt], in_=PS[pg][:, 0:nt], func=AF.Silu))(pg), reads=[PSB[pg]], writes=[B_sg])
            S.op("dve", (lambda pu, fc: lambda e: e.tensor_tensor(out=hbuf[:, fc, 0:nt], in0=sgt[:, 0:nt], in1=PS[pu][:, 0:nt], op=ALU.mult))(pu, fc),
                 reads=[B_sg, PSB[pu]], writes=[HB[fc]])
            if deferred:
                deferred.pop(0)()
        while deferred:
            deferred.pop(0)()
        if next_t is not None:
            ffn_cast(next_t)
        dbase = (l * 2 + fi) * 8
        for dc in range(8):
            sl = ring["dw"] % 2
            ring["dw"] += 1
            S.dma("pool", (lambda sl, idx: lambda e: e.dma_start(out=d_slot[sl][:], in_=w_d_d[idx].rearrange("p (f d) -> p f d", f=NFC)))(sl, dbase + dc),
                  f"dw{sl}", writes=[DW[sl]])
            py = 2 + dc % 2

            def mm2(e, sl=sl, py=py):
                for (h0, hn) in hv:
                    for fc in range(NFC):
                        ins = e.matmul(PS[py][:, h0:h0 + hn], lhsT=d_slot[sl][:, fc, :], rhs=hbuf[:, fc, h0:h0 + hn], start=(fc == 0), stop=(fc == NFC - 1))
                return ins
            S.op("pe", mm2, reads=[DW[sl]] + HB, writes=[PSB[py]])
            S.op("dve", (lambda dc, py: lambda e: e.scalar_tensor_tensor(out=xres[:, dc, t0:t0 + nt], in0=PS[py][:, 0:nt], scalar=0.5 / ALPHA,
                                                                           in1=xres[:, dc, t0:t0 + nt], op0=ALU.mult, op1=ALU.add))(dc, py),
                 reads=[PSB[py], xr_bufs[dc]], writes=[xr_bufs[dc]])
            if dc >= 1:
                ln_stats_chunk(dc - 1, t0, nt, xr_bufs)
        ln_stats_chunk(7, t0, nt, xr_bufs)
        ln_finish_head(nt)
        pieces = [(lambda c: lambda: ln_finish_chunk(l, s, c, t0, nt, xr_bufs))(c) for c in range(8)]
        if defer:
            return pieces
        for p in pieces:
            p()
        return None

    def dump_and_finish():
        ncols = OWN if dbg is None else TC
        S.barrier()
        S.wait_all("sp", ["pe", "act", "dve"])
        for c in range(8):
            S.dma("sp", (lambda c: lambda e: e.dma_start(out=y_d[:, c, :], in_=xres[:, c, 0:ncols]))(c), "out",
                  reads=[b for t in range(2) for b in XR[t]])
        S.wait_all("sp", ["out"])
        S.wait_all("pool", ["gu0", "gu1", "gu2", "dw0", "dw1", "pw", "aw0", "aw1", "ao"])
        with nc.Block() as block:
            S.emit(block)
        return nc

    def ffn_jobs(jobs, extra_after=None):
        pend = None
        for i, (l, fi, s_, t) in enumerate(jobs):
            nxt = jobs[i + 1][3] if i + 1 < len(jobs) else None
            pend = ffn_tile(l, fi, s_, t, XR[t], do_cast=(i == 0), next_t=nxt, deferred=(pend if i > 0 else None), defer=(nxt is not None))
            if extra_after and i in extra_after:
                assert pend is not None
                pend = pend + list(extra_after[i])

    xe_in = nc.dram_tensor("xe_in", [128, 64], F32)
    xe_out = nc.dram_tensor("xe_out", [256, 64], F32)
    B_ein, B_eout, B_estg = Buf("xe_in"), Buf("xe_out"), Buf("estg")
    estg = at("estg", (128, 2, 8, 8), F32, OFF_C + 3584)

    def start_exchange0():
        S.dma("pool", lambda e: e.dma_start(out=xe_in.ap().rearrange("p (c n) -> p c n", c=8), in_=xres[:, :, OWN - 8:OWN]), "xe",
              reads=XR[1], writes=[B_ein])
        S.op("pool", lambda e: e.collective_compute("AllGather", ALU.bypass, replica_groups=[[0, 1], [2, 3], [4, 5], [6, 7]],
                                                    ins=[xe_in.ap().opt()], outs=[xe_out.ap().opt()]),
             reads=[B_ein], writes=[B_eout])
    def prefetch_pool_weights():
        S.dma("pool", lambda e: e.dma_start(out=p_win[:], in_=w_pin_d.rearrange("p (k e) -> p k e", k=8)), "pw", writes=[B_pw])
        S.dma("pool", lambda e: e.dma_start(out=p_wout[:], in_=w_pout_d.rearrange("p (k e) -> p k e", k=8)), "pw", writes=[B_pw])
    ffn_jobs([(0, 0, 0, 1), (0, 0, 0, 0)], extra_after={0: [start_exchange0, prefetch_pool_weights]})
    if dbg == "l0ffn1":
        return dump_and_finish()

    S.barrier()
    ALLX = [b for t in range(2) for b in XR[t]]
    S.dma("pool", lambda e: e.dma_start(out=p_wgrp[:], in_=w_pgrp_d.rearrange("p (g c e) -> p g c e", g=4, c=2)), "pw", writes=[B_pw])
    B_pw.wdep = ("pw", S.cnt["pw"])
    B_pxb = [Buf(f"pxb{c}") for c in range(8)]
    B_pu = [Buf(f"pu{c}") for c in range(8)]
    B_pmx = [Buf(f"pmx{c}") for c in range(8)]
    B_pyb = [Buf(f"pyb{c}") for c in range(8)]
    B_pt = [Buf("pt0"), Buf("pt1")]
    B_pwn = Buf("pwn")
    B_hs = Buf("hsave")
    S.op("dve", lambda e: e.memset(p_u[:, :, 0:8], 0.0), writes=B_pu)
    S.dma("sp", lambda e: e.dma_start(out=estg[:], in_=xe_out.ap().rearrange("(r p) (c n) -> p r c n", r=2, c=8)), "xs0",
          reads=[B_eout], writes=[B_estg])
    S.op("dve", lambda e: e.tensor_scalar(out=hsave[:, :, :], in0=_rev(estg[:, 0, :, :]), scalar1=pflag[:, 1:2], scalar2=None, op0=ALU.mult),
         reads=[B_estg, B_const], writes=[B_hs])
    TP = 448
    ptiles = [(a, min(a + TP, OWN)) for a in range(0, OWN, TP)]
    XPt = [[Buf(f"xp{t}_{c}") for c in range(8)] for t in range(len(ptiles) + 1)]
    S.op("dve", lambda e: e.scalar_tensor_tensor(out=xres[:, :, OWN:OWN + 8], in0=_rev(estg[:, 1, :, :]), scalar=pflag[:, 0:1], in1=hsave[:, :, :],
                                                 op0=ALU.mult, op1=ALU.add),
         reads=[B_estg, B_hs, B_const], writes=XPt[len(ptiles)])
    def pool_tile(ti, a, b):
        XP = XPt[ti]
        XN = XPt[ti + 1]
        n_o = b - a
        first = (ti == 0)
        ua = 0 if first else a - 8
        ub = b + 8
        n_u = ub - ua
        loff = 8 if first else 0
        for c in range(8):
            if first:
                S.op("act", (lambda c: lambda e: e.activation(out=p_xb[:, c, 0:n_u], in_=xres[:, c, ua:ub], func=AF.Copy))(c), reads=[XP[c], XN[c]], writes=[B_pxb[c]])
            else:
                S.op("act", (lambda c: lambda e: e.activation(out=p_xb[:, c, 0:8], in_=hsave[:, c, :], func=AF.Copy))(c), reads=[B_hs], writes=[B_pxb[c]])
                S.op("act", (lambda c: lambda e: e.activation(out=p_xb[:, c, 8:n_u], in_=xres[:, c, a:ub], func=AF.Copy))(c), reads=[XP[c], XN[c]], writes=[B_pxb[c]])
        S.op("dve", lambda e: e.tensor_copy(out=hsave[:, :, :], in_=xres[:, :, b - 8:b]), reads=XP + B_pxb, writes=[B_hs])
        for c in range(8):
            pp = c % 4

            def mmu(e, c=c, pp=pp):
                for kc in range(8):
                    ins = e.matmul(PS[pp][:, 0:n_u], lhsT=p_win[:, kc, c * 128:(c + 1) * 128], rhs=p_xb[:, kc, 0:n_u], start=(kc == 0), stop=(kc == 7))
                return ins
            S.op("pe", mmu, reads=[B_pw] + B_pxb, writes=[PSB[pp]])
            S.op("act", (lambda c, pp: lambda e: e.activation(out=p_u[:, c, loff:loff + n_u], in_=PS[pp][:, 0:n_u], func=AF.Copy))(c, pp),
                 reads=[PSB[pp]], writes=[B_pu[c]])
        n_loc = n_u + loff
        lo0 = 8
        for c in range(8):
            g = c // 2
            h = 1 << g
            w = 2 * h
            src = p_u[:, c, :]
            srcb = B_pu[c]
            ln = 1
            k = 0
            while ln < w:
                dst = p_t[k % 2]
                cnt = n_loc - 2 * ln + 1
                S.op("dve", (lambda src, dst, ln, cnt: lambda e: e.tensor_tensor(out=dst[:, 0:cnt], in0=src[:, 0:cnt], in1=src[:, ln:ln + cnt], op=ALU.add))(src, dst, ln, cnt),
                     reads=[srcb], writes=[B_pt[k % 2]])
                src = dst
                srcb = B_pt[k % 2]
                ln *= 2
                k += 1
            s0 = lo0 - h
            S.op("dve", (lambda src, s0: lambda e: e.tensor_scalar(out=p_wn[:, 0:n_o], in0=src[:, s0 + 1:s0 + 1 + n_o], scalar1=pflag[:, 1:2], scalar2=None, op0=ALU.mult))(src, s0),
                 reads=[srcb, B_const], writes=[B_pwn])
            S.op("dve", (lambda src, s0: lambda e: e.scalar_tensor_tensor(out=p_wn[:, 0:n_o], in0=src[:, s0:s0 + n_o], scalar=pflag[:, 0:1], in1=p_wn[:, 0:n_o],
                                                                            op0=ALU.mult, op1=ALU.add))(src, s0),
                 reads=[srcb, B_const, B_pwn], writes=[B_pwn])
            S.op("dve", (lambda c, w: lambda e: e.scalar_tensor_tensor(out=p_mx[:, c, 0:n_o], in0=p_wn[:, 0:n_o], scalar=1.0 / w, in1=p_u[:, c, lo0:lo0 + n_o],
                                                                         op0=ALU.mult, op1=ALU.subtract))(c, w),
                 reads=[B_pwn, B_pu[c]], writes=[B_pmx[c]])
            if first:
                S.op("dve", (lambda c: lambda e: e.tensor_tensor(out=p_wn[:, 0:8], in0=p_wn[:, 0:8], in1=pinv[:, c * 8:c * 8 + 8], op=ALU.mult))(c),
                     reads=[B_pwn, B_const], writes=[B_pwn])
                S.op("dve", (lambda c: lambda e: e.tensor_tensor(out=p_mx[:, c, 0:8], in0=p_wn[:, 0:8], in1=p_u[:, c, lo0:lo0 + 8], op=ALU.subtract))(c),
                     reads=[B_pwn, B_pu[c]], writes=[B_pmx[c]])
        for ec in range(8):
            g = ec // 2
            eh = ec % 2
            pp = ec % 4

            def mmg(e, g=g, eh=eh, pp=pp):
                for cc in range(2):
                    ins = e.matmul(PS[pp][:, 0:n_o], lhsT=p_wgrp[:, g, cc, eh * 128:(eh + 1) * 128], rhs=p_mx[:, 2 * g + cc, 0:n_o], start=(cc == 0), stop=(cc == 1))
                return ins
            S.op("pe", mmg, reads=[B_pw, B_pmx[2 * g], B_pmx[2 * g + 1]], writes=[PSB[pp]])
            S.op("act", (lambda ec, pp: lambda e: e.activation(out=p_yb[:, ec, 0:n_o], in_=PS[pp][:, 0:n_o], func=AF.Identity, scale=pscale[:, ec:ec + 1]))(ec, pp),
                 reads=[PSB[pp], B_const], writes=[B_pyb[ec]])
        for dc in range(8):
            pp = dc % 4

            def mmo(e, dc=dc, pp=pp):
                for ec in range(8):
                    ins = e.matmul(PS[pp][:, 0:n_o], lhsT=p_wout[:, ec, dc * 128:(dc + 1) * 128], rhs=p_yb[:, ec, 0:n_o], start=(ec == 0), stop=(ec == 7))
                return ins
            S.op("pe", mmo, reads=[B_pw] + B_pyb, writes=[PSB[pp]])
            S.op("dve", (lambda dc, pp: lambda e: e.scalar_tensor_tensor(out=xres[:, dc, a:b], in0=PS[pp][:, 0:n_o], scalar=1.0 / ALPHA, in1=xres[:, dc, a:b],
                                                                           op0=ALU.mult, op1=ALU.add))(dc, pp),
                 reads=[PSB[pp], XP[dc], B_hs], writes=[XP[dc]])
        return None
    for ti, (a, b) in enumerate(ptiles):
        pool_tile(ti, a, b)
        if ti > 0:
            pa, pb = ptiles[ti - 1]
            ln_cols(0, 1, pa, pb - pa, XPt[ti - 1])
    pa, pb = ptiles[-1]
    ln_cols(0, 1, pa, pb - pa, XPt[len(ptiles) - 1])
    S.barrier()
    if dbg == "l0pool":
        return dump_and_finish()

    if dbg == "l0":
        ffn_jobs([(0, 1, 2, t) for t in range(2)])
        return dump_and_finish()
    xc_in = nc.dram_tensor("xc_in", [128, 8192], BF16)
    xc_out = nc.dram_tensor("xc_out", [256, 8192], BF16)
    B_inb, B_outb = Buf("xc_in"), Buf("xc_out")

    def start_exchange():
        S.dma("pool", lambda e: e.dma_start(out=xc_in.ap().rearrange("p (c n) -> p c n", c=8), in_=xres[:, :, 1024:2048]), "xc",
              reads=XR[1], writes=[B_inb])
        S.op("pool", lambda e: e.collective_compute("AllGather", ALU.bypass, replica_groups=[[0, 1], [2, 3], [4, 5], [6, 7]],
                                                    ins=[xc_in.ap().opt()], outs=[xc_out.ap().opt()]),
             reads=[B_inb], writes=[B_outb])
    ffn_jobs([(0, 1, 2, 1), (0, 1, 2, 0), (1, 0, 0, 1), (1, 0, 0, 0)], extra_after={2: [start_exchange]})
    S.barrier()
    if dbg == "l1ffn1":
        return dump_and_finish()

    B_x1b = [Buf(f"x1b{c}") for c in range(8)]
    for c in range(8):
        if c % 2 == 0:
            S.op("dve", (lambda c: lambda e: e.tensor_copy(out=x1b[:, c, 0:OWN], in_=xres[:, c, 0:OWN]))(c), reads=ALLX, writes=[B_x1b[c]])
        else:
            S.op("act", (lambda c: lambda e: e.activation(out=x1b[:, c, 0:OWN], in_=xres[:, c, 0:OWN], func=AF.Copy))(c), reads=ALLX, writes=[B_x1b[c]])
    S.wait_all("sp", ["pe", "act", "dve"])
    stg = [at(f"stg{i}", (128, 2, 1024), BF16, OFF_RING + i * 4096) for i in range(2)]
    xtmp = at("xtmp", (128, 1024), BF16, OFF_RING + 8192)
    B_stg = [Buf("stg0"), Buf("stg1")]
    B_xtmp = Buf("xtmp")
    outv = xc_out.ap().rearrange("(r p) (c n) -> p r c n", r=2, c=8)

    for c in range(8):
        sl = c % 2
        S.dma("sp", (lambda c, sl: lambda e: e.dma_start(out=stg[sl][:], in_=outv[:, :, c, :]))(c, sl), f"xs{sl}", reads=[B_outb], writes=[B_stg[sl]])
        S.op("dve", (lambda sl: lambda e: e.tensor_scalar(out=xtmp[:, :], in0=_rev(stg[sl][:, 0, :]), scalar1=pflag[:, 1:2], scalar2=None, op0=ALU.mult))(sl),
             reads=[B_stg[sl], B_const], writes=[B_xtmp])
        S.op("dve", (lambda c, sl: lambda e: e.scalar_tensor_tensor(out=x1b[:, c, OWN:TH], in0=_rev(stg[sl][:, 1, :]), scalar=pflag[:, 0:1], in1=xtmp[:, :],
                                                                     op0=ALU.mult, op1=ALU.add))(c, sl),
             reads=[B_stg[sl], B_xtmp, B_const], writes=[B_x1b[c]])
    S.barrier()
    B_vaug = Buf("vaug")
    S.op("dve", lambda e: e.memset(vaug[:, :, 64:128], 1.0), writes=[B_vaug])
    B_wq = [Buf("wqkv0"), Buf("wqkv1")]
    B_q = Buf("qT")
    B_k = Buf("kT")
    B_v = Buf("vT")
    B_pt2 = [Buf("PT0"), Buf("PT1"), Buf("PT2")]
    B_tt = [Buf("tt0"), Buf("tt1")]
    B_os = [[Buf(f"os{h}_{q}") for q in range(4)] for h in range(2)]
    B_rd = [Buf("rden0"), Buf("rden1")]
    B_ohp = Buf("ohp")
    B_wao = Buf("wao")
    XO = [[Buf(f"xo{dc}_{q}") for q in range(4)] for dc in range(8)]
    PSA = [Buf(f"psa{i}") for i in range(8)]
    psT = PSHI[:, :].bitcast(BF16)

    def osum(hd, qt):
        return os_t[:, hd * 4 + qt, 0:512]

    slopes = 2.0 ** (-8.0 * np.arange(1, 49) / 48.0)
    slopes = slopes.reshape(3, 16)
    rot = {"wk": 0, "tt": 0, "pt": 0, "wq": 0}

    def wk_slot():
        b = rot["wk"] % 4
        rot["wk"] += 1
        return b

    def attn_proj(hp, g, win, d):
        Lq = OWN // d
        Lk = Lq + 64
        ntok_k = Lk * d
        ws = rot["wq"] % 2
        rot["wq"] += 1
        wq = wqkv2[ws]
        S.dma("pool", (lambda idx, wq: lambda e: e.dma_start(out=wq[:], in_=w_qkv_d[idx].rearrange("p (m k f) -> p m k f", m=3, k=8)))(hp * 3 + g, wq),
              f"aw{ws}", writes=[B_wq[ws]])
        ev = 0
        for m, (dstT, dstB, ntok) in enumerate(((qT, B_q, OWN), (kT, B_k, ntok_k), (vT, B_v, ntok_k))):
            dview = dstT[:, 0:ntok].rearrange("p (r i) -> p r i", r=d)
            for j0 in range(0, ntok, 512):
                n = min(512, ntok - j0)
                bk = wk_slot()
                ph = bk * 512

                def mmp(e, m=m, j0=j0, n=n, ph=ph):
                    for kc in range(8):
                        ins = e.matmul(PSHI[:, ph:ph + n], lhsT=wq[:, m, kc, :], rhs=x1b[:, kc, j0:j0 + n], start=(kc == 0), stop=(kc == 7))
                    return ins
                S.op("pe", mmp, reads=[B_wq[ws]] + B_x1b, writes=[PSA[4 + bk]])
                i0 = j0 // d
                ni = n // d
                src = PSHI[:, ph:ph + n].rearrange("p (i r) -> p r i", r=d)
                if ev % 2 == 0:
                    S.op("act", (lambda dview, src, i0, ni: lambda e: e.activation(out=dview[:, :, i0:i0 + ni], in_=src, func=AF.Copy))(dview, src, i0, ni),
                         reads=[PSA[4 + bk]], writes=[dstB])
                else:
                    S.op("dve", (lambda dview, src, i0, ni: lambda e: e.tensor_copy(out=dview[:, :, i0:i0 + ni], in_=src))(dview, src, i0, ni),
                         reads=[PSA[4 + bk]], writes=[dstB])
                ev += 1
        chunks = []
        for r in range(d):
            for k0 in range(0, Lk, 128):
                chunks.append((r, k0, min(128, Lk - k0)))
        for c0 in range(0, len(chunks), 4):
            grp = chunks[c0:c0 + 4]
            ng = len(grp)
            bk = wk_slot()
            pbase = bk * 1024

            def tr(e, grp=grp, pbase=pbase):
                for q, (r, k0, nk) in enumerate(grp):
                    ins = e.transpose(psT[0:nk, pbase + q * 128:pbase + (q + 1) * 128], vT[:, r * Lk + k0:r * Lk + k0 + nk], ident_bf[:, :])
                return ins
            S.op("pe", tr, reads=[B_v, B_ident], writes=[PSA[4 + bk]])
            srcv = psT[:, pbase:pbase + ng * 128].rearrange("p (q f) -> p q f", f=128)
            if (c0 // 4) % 2 == 0:
                S.op("act", (lambda c0, ng, srcv: lambda e: e.activation(out=vaug[:, c0:c0 + ng, 0:64], in_=srcv[:, :, 0:64], func=AF.Copy))(c0, ng, srcv),
                     reads=[PSA[4 + bk]], writes=[B_vaug])
                S.op("act", (lambda c0, ng, srcv: lambda e: e.activation(out=vaug[:, c0:c0 + ng, 128:192], in_=srcv[:, :, 64:128], func=AF.Copy))(c0, ng, srcv),
                     reads=[PSA[4 + bk]], writes=[B_vaug])
            else:
                S.op("dve", (lambda c0, ng, srcv: lambda e: e.tensor_copy(out=vaug[:, c0:c0 + ng, 0:64], in_=srcv[:, :, 0:64]))(c0, ng, srcv),
                     reads=[PSA[4 + bk]], writes=[B_vaug])
                S.op("dve", (lambda c0, ng, srcv: lambda e: e.tensor_copy(out=vaug[:, c0:c0 + ng, 128:192], in_=srcv[:, :, 64:128]))(c0, ng, srcv),
                     reads=[PSA[4 + bk]], writes=[B_vaug])
        return {"Lq": Lq, "Lk": Lk, "chunks": chunks}

    def attn_core(hp, g, win, d, ctx):
        Lq, Lk, chunks = ctx["Lq"], ctx["Lk"], ctx["chunks"]
        items = []
        for hd in range(2):
            work = []
            for ci, (r, k0, nk) in enumerate(chunks):
                i0 = max(0, k0 - 64)
                i1 = min(Lq, k0 + nk + 64)
                if i1 > i0:
                    work.append((ci, r, k0, nk, i0, i1))
            npair = (len(work) + 1) // 2
            for pi in range(npair):
                items.append({"hd": hd, "pair": work[2 * pi:2 * pi + 2], "last": pi == npair - 1})
        started = {0: [False] * 4, 1: [False] * 4}
        accv = PSLO[:, :].rearrange("p (r i) -> p i r", r=d)

        def emit_scores(it):
            hd = it["hd"]
            pair = it["pair"]
            hg = 2 * hp + hd
            cneg = -float(slopes[g, hg]) * d * 8.0
            hrow = slice(64 * hd, 64 * hd + 64)
            bk = wk_slot()
            ph = bk * 512

            def mms(e, pair=pair, ph=ph, hrow=hrow):
                for q, (ci, r, k0, nk, i0, i1) in enumerate(pair):
                    doff = i0 - (k0 - 64)
                    nq = i1 - i0
                    ins = e.matmul(PSHI[0:nk, ph + q * 256 + doff:ph + q * 256 + doff + nq], lhsT=kT[hrow, r * Lk + k0:r * Lk + k0 + nk],
                                   rhs=qT[hrow, r * Lq + i0:r * Lq + i0 + nq], start=True, stop=True, skip_group_check=True)
                return ins
            S.op("pe", mms, reads=[B_k, B_q], writes=[PSA[4 + bk]])
            ts = rot["tt"] % 2
            rot["tt"] += 1
            ps_ = rot["pt"] % 3
            rot["pt"] += 1
            it["ps"] = ps_
            S.op("dve", (lambda ph, ts, cneg: lambda e: e.scalar_tensor_tensor(
                out=tt[ts][:, :], in0=dtile[:, :], scalar=cneg, in1=PSHI[:, ph:ph + 512], op0=ALU.mult, op1=ALU.add))(ph, ts, cneg),
                reads=[B_const, PSA[4 + bk]], writes=[B_tt[ts]])
            S.op("act", (lambda ts, ps_: lambda e: e.activation(out=PTt[ps_][:, :], in_=tt[ts][:, :], func=AF.Exp, scale=0.125))(ts, ps_),
                 reads=[B_tt[ts]], writes=[B_pt2[ps_]])

        def emit_pv(it):
            hd = it["hd"]
            pair = it["pair"]
            ps_ = it["ps"]
            vsel = (slice(0, 128) if hd == 0 else slice(64, 192))
            st = started[hd]
            plan = []
            wb = set()
            for q, (ci, r, k0, nk, i0, i1) in enumerate(pair):
                doff = i0 - (k0 - 64)
                pos = r * Lq + i0
                end = r * Lq + i1
                while pos < end:
                    bb = pos // 512
                    nx = min(end, (bb + 1) * 512)
                    plan.append((ci, nk, q * 256 + doff + (pos - (r * Lq + i0)), pos, nx - pos, not st[bb]))
                    st[bb] = True
                    wb.add(bb)
                    pos = nx

            def mmv(e, plan=plan, ps_=ps_, vsel=vsel):
                for (ci, nk, pcol, pos, n, stt) in plan:
                    ins = e.matmul(PSLO[:, pos:pos + n], lhsT=vaug[0:nk, ci, vsel], rhs=PTt[ps_][0:nk, pcol:pcol + n], start=stt, stop=True,
                                   skip_group_check=True)
                return ins
            S.op("pe", mmv, reads=[B_vaug, B_pt2[ps_]], writes=[PSA[bb] for bb in sorted(wb)])
            if it["last"]:
                for qt in range(4):
                    ia = 512 * qt // d
                    nn = 512 // d
                    srcp = accv[:, ia:ia + nn, :]
                    dst = osum(hd, qt).rearrange("p (i r) -> p i r", r=d)
                    rb = [PSA[qt]] if d == 1 else [PSA[0], PSA[1], PSA[2], PSA[3]]
                    if g == 0:
                        S.op("act", (lambda dst, srcp: lambda e: e.activation(out=dst, in_=srcp, func=AF.Copy))(dst, srcp),
                             reads=rb, writes=[B_os[hd][qt]])
                    else:
                        S.op("dve", (lambda dst, srcp: lambda e: e.tensor_tensor(out=dst, in0=dst, in1=srcp, op=ALU.add))(dst, srcp),
                             reads=rb + [B_os[hd][qt]], writes=[B_os[hd][qt]])

        LA = 2
        for idx in range(len(items) + LA):
            if idx < len(items):
                emit_scores(items[idx])
            if idx - LA >= 0:
                emit_pv(items[idx - LA])

    def attn_finish(hp):
        for hd in range(2):
            nrow = slice(0, 64) if hd == 0 else slice(64, 128)
            drow = slice(64, 128) if hd == 0 else slice(0, 64)
            osv = os_t[:, hd * 4:hd * 4 + 4, 0:512]
            rdv = os_t[:, hd * 4:hd * 4 + 4, 512:1024]
            ohv = o_hp[:, :].rearrange("p (q n) -> p q n", q=4)
            allos = [B_os[hd][q] for q in range(4)]
            S.op("act", (lambda osv, rdv, nrow, drow: lambda e: e.activation(out=rdv[nrow], in_=osv[drow], func=AF.Ln))(osv, rdv, nrow, drow),
                 reads=allos, writes=[B_rd[hd]])
            S.op("act", (lambda rdv, nrow: lambda e: e.activation(out=rdv[nrow], in_=rdv[nrow], func=AF.Exp, scale=-1.0))(rdv, nrow),
                 reads=[B_rd[hd]], writes=[B_rd[hd]])
            S.op("dve", (lambda osv, rdv, ohv, nrow: lambda e: e.tensor_tensor(out=ohv[nrow], in0=osv[nrow], in1=rdv[nrow], op=ALU.mult))(osv, rdv, ohv, nrow),
                 reads=allos + [B_rd[hd]], writes=[B_ohp])

    def attn_oproj(hp):
        S.dma("pool", (lambda hp: lambda e: e.dma_start(out=wao[:], in_=w_ao_d[hp]))(hp), "ao", writes=[B_wao])
        for dc in range(8):
            for qt in range(4):
                bk = wk_slot()
                ph = bk * 512

                def mmo2(e, dc=dc, qt=qt, ph=ph):
                    return e.matmul(PSHI[:, ph:ph + 512], lhsT=wao[:, dc * 128:(dc + 1) * 128], rhs=o_hp[:, qt * 512:(qt + 1) * 512], start=True, stop=True)
                S.op("pe", mmo2, reads=[B_wao, B_ohp], writes=[PSA[4 + bk]])
                S.op("dve", (lambda dc, qt, ph: lambda e: e.scalar_tensor_tensor(out=xres[:, dc, qt * 512:(qt + 1) * 512], in0=PSHI[:, ph:ph + 512], scalar=1.0 / ALPHA,
                                                                               in1=xres[:, dc, qt * 512:(qt + 1) * 512], op0=ALU.mult, op1=ALU.add))(dc, qt, ph),
                     reads=[PSA[4 + bk], XO[dc][qt]], writes=[XO[dc][qt]])
    ctx_next = attn_proj(0, 0, DIL[0][0], DIL[0][1])
    for hp in range(8):
        for g, (win, d) in enumerate(DIL):
            ctx = ctx_next
            attn_core(hp, g, win, d, ctx)
            if g < 2:
                ctx_next = attn_proj(hp, g + 1, DIL[g + 1][0], DIL[g + 1][1])
        attn_finish(hp)
        if hp < 7:
            ctx_next = attn_proj(hp + 1, 0, DIL[0][0], DIL[0][1])
        attn_oproj(hp)
    S.barrier()
    for t in range(2):
        t0, nt = TILES[t]
        ln_cols(1, 1, t0, nt, XR[t])
    S.barrier()
    if dbg == "l1attn":
        return dump_and_finish()
    ffn_jobs([(1, 1, 2, t) for t in range(2)])
    return dump_and_finish()


def _prep_shared(ffn1_w_gate, ffn1_w_up, ffn1_w_down, ffn2_w_gate, ffn2_w_up, ffn2_w_down,
                 ln_gain, ln_bias, pool_w_in, pool_w_group, pool_scale, pool_w_out, attn_w_qkv, attn_w_out):
    f = np.float32
    gates = [ffn1_w_gate, ffn2_w_gate]
    ups = [ffn1_w_up, ffn2_w_up]
    downs = [ffn1_w_down, ffn2_w_down]
    w_gu = np.empty((2, 2, NFC, 128, 2, 8, 128), f)
    w_d = np.empty((2, 2, 8, 128, NFC, 128), f)
    for l in range(2):
        for fi in range(2):
            g = np.asarray(gates[fi][l], f).reshape(8, 128, NFC, 128)
            u = np.asarray(ups[fi][l], f).reshape(8, 128, NFC, 128)
            w_gu[l, fi, :, :, 0] = g.transpose(2, 1, 0, 3)
            w_gu[l, fi, :, :, 1] = u.transpose(2, 1, 0, 3)
            dn = np.asarray(downs[fi][l], f).reshape(NFC, 128, 8, 128)
            w_d[l, fi] = dn.transpose(2, 1, 0, 3)
    lnp = np.empty((128, 2, 3, 2, 8), f)
    lnp[:, :, :, 0, :] = np.asarray(ln_gain, f).reshape(2, 3, 8, 128).transpose(3, 0, 1, 2)
    lnp[:, :, :, 1, :] = np.asarray(ln_bias, f).reshape(2, 3, 8, 128).transpose(3, 0, 1, 2)
    w_pin = np.asarray(pool_w_in[0], f).reshape(8, 128, 1024).transpose(1, 0, 2)
    w_pout = np.asarray(pool_w_out[0], f).reshape(8, 128, 1024).transpose(1, 0, 2)
    w_pgrp = np.asarray(pool_w_group[0], f).reshape(4, 2, 128, 256).transpose(2, 0, 1, 3)
    pscale = np.asarray(pool_scale[0], f).reshape(8, 128).T
    wq = np.asarray(attn_w_qkv[0], f).reshape(8, 128, 3, 3, 8, 128)
    w_qkv = wq.transpose(4, 2, 1, 3, 0, 5)
    w_ao = np.asarray(attn_w_out[0], f).reshape(8, 128, 1024)
    r = np.arange(128)[:, None]
    i = np.arange(256)[None, :]
    dd = np.abs(r - i + 64).astype(f)
    dtile = np.where(dd <= 64, dd, BIGD).astype(f)
    dtile = np.concatenate([dtile, dtile], axis=1)
    c = np.ascontiguousarray
    return {
        "lnp": c(lnp.reshape(128, 96)), "pscale": c(pscale), "dtile": c(dtile), "ident": np.eye(128, dtype=f),
        "w_gu": c(w_gu.reshape(88, 128, 2048)), "w_d": c(w_d.reshape(32, 128, 2816)),
        "w_pin": c(w_pin.reshape(128, 8192)), "w_pout": c(w_pout.reshape(128, 8192)), "w_pgrp": c(w_pgrp.reshape(128, 2048)),
        "w_qkv": c(w_qkv.reshape(24, 128, 3072)), "w_ao": c(w_ao),
    }


def _prep_core(x, core):
    f = np.float32
    b, half = core // 2, core % 2
    xs = np.asarray(x[b], f)
    if half == 0:
        loc = xs[0:TC]
    else:
        loc = xs[::-1][0:TC]
    xT = np.ascontiguousarray(loc.T.reshape(8, 128, TC).transpose(1, 0, 2))
    pflag = np.zeros((128, 2), f)
    pflag[:, half] = 1.0
    pinv = np.empty((128, 8, 8), f)
    for cidx in range(8):
        h = 1 << (cidx // 2)
        w = 2 * h
        for jo in range(8):
            if half == 0:
                cnt = h + min(h, jo)
            else:
                cnt = h + min(h, jo + 1)
            pinv[:, cidx, jo] = 1.0 / cnt
    return {"xT": xT, "pflag": pflag, "pinv": np.ascontiguousarray(pinv.reshape(128, 64))}


_NC_CACHE = {}


def _get_nc(dbg=None):
    if dbg not in _NC_CACHE:
        _NC_CACHE[dbg] = build_program(dbg)
    return _NC_CACHE[dbg]


def run_cores(inputs, dbg=None, cores=range(8), trace=False):
    x = np.asarray(inputs["x"], np.float32)
    shared = _prep_shared(**{k: v for k, v in inputs.items() if k != "x"})
    in_maps = []
    for core in cores:
        m = dict(shared)
        m.update(_prep_core(x, core))
        in_maps.append(m)
    nc = _get_nc(dbg)
    res = run_bass_kernel_spmd(nc, in_maps, core_ids=list(range(len(in_maps))), **({"trace": True} if trace else {}))
    return res


def kernel(**inputs):
    res = run_cores(inputs)
    out = np.empty((BATCH, SEQ, D_MODEL), np.float32)
    for core in range(8):
        yT = np.asarray(res.results[core]["yT"], np.float32)
        y = yT.transpose(2, 1, 0).reshape(OWN, D_MODEL)
        b, half = core // 2, core % 2
        if half == 0:
            out[b, 0:OWN] = y
        else:
            out[b, OWN:SEQ] = y[::-1]
    return out
```

```python
import numpy as np
import concourse.bass as bass
import concourse.mybir as mybir
from concourse.bass_utils import run_bass_kernel_spmd

F32 = mybir.dt.float32
BF16 = mybir.dt.bfloat16
AF = mybir.ActivationFunctionType
ALU = mybir.AluOpType

D_MODEL = 1024
SEQ = 4096
BATCH = 4
D_FF = 2816
NFC = 22
OWN = 2048
HALO = 1024
TC = OWN + 8
TH = OWN + HALO
ALPHA = 4.0 ** 0.25
LN_EPS = 1e-5
EPS_P = LN_EPS / (ALPHA * ALPHA)
DIL = ((128, 1), (512, 4), (2048, 16))
BIGD = 1.0e5

ENGS = ("pe", "act", "dve", "pool", "sp")


class Buf:
    __slots__ = ("name", "wdep", "rdeps")

    def __init__(self, name):
        self.name = name
        self.wdep = None
        self.rdeps = {}


class Sched:
    def __init__(self, nc):
        self.nc = nc
        self.sem = {}
        self.cnt = {}
        for e in ENGS:
            self.sem[e] = nc.alloc_semaphore("s_" + e)
            self.cnt[e] = 0
        self.seen = {e: {} for e in ENGS}
        self.ops = {e: [] for e in ENGS}

    def new_dma_sem(self, key):
        self.sem[key] = self.nc.alloc_semaphore("d_" + key)
        self.cnt[key] = 0
        return key

    def _collect(self, reads, writes):
        w = {}
        for b in reads:
            d = b.wdep
            if d is not None and w.get(d[0], 0) < d[1]:
                w[d[0]] = d[1]
        for b in writes:
            d = b.wdep
            if d is not None and w.get(d[0], 0) < d[1]:
                w[d[0]] = d[1]
            for k, v in b.rdeps.items():
                if w.get(k, 0) < v:
                    w[k] = v
        return w

    def _need(self, e, w):
        need = []
        s = self.seen[e]
        for k, v in w.items():
            if s.get(k, 0) < v:
                need.append((k, v))
                s[k] = v
        return need

    def op(self, e, fn, reads=(), writes=()):
        need = self._need(e, self._collect(reads, writes))
        self.cnt[e] += 1
        v = self.cnt[e]
        self.ops[e].append((need, fn, (e, 1)))
        for b in reads:
            if b.rdeps.get(e, 0) < v:
                b.rdeps[e] = v
        for b in writes:
            b.wdep = (e, v)
            b.rdeps = {}
        return (e, v)

    def dma(self, q, fn, semkey, reads=(), writes=()):
        need = self._need(q, self._collect(reads, writes))
        self.cnt[semkey] += 16
        v = self.cnt[semkey]
        self.ops[q].append((need, fn, (semkey, 16)))
        for b in reads:
            if b.rdeps.get(semkey, 0) < v:
                b.rdeps[semkey] = v
        for b in writes:
            b.wdep = (semkey, v)
            b.rdeps = {}
        return (semkey, v)

    def wait_all(self, e, keys):
        w = {k: self.cnt[k] for k in keys if self.cnt[k] > 0}
        need = self._need(e, w)
        if need:
            self.ops[e].append((need, None, None))

    def barrier(self, engines=("pe", "act", "dve", "pool")):
        for e in engines:
            self.wait_all(e, list(engines))

    def emit(self, block):
        sched = self

        def make(e):
            def body(eng):
                for need, fn, inc in sched.ops[e]:
                    for k, v in need:
                        eng.wait_ge(sched.sem[k], v)
                    if fn is not None:
                        ins = fn(eng)
                        ins.then_inc(sched.sem[inc[0]], inc[1])
            return body
        block.tensor(make("pe"))
        block.scalar(make("act"))
        block.vector(make("dve"))
        block.gpsimd(make("pool"))
        block.sync(make("sp"))


def _halves(nt):
    return [(h0, min(512, nt - h0)) for h0 in range(0, nt, 512)]


def build_program(dbg=None):
    nc = bass.Bass("TRN2", target_bir_lowering=False)
    S = Sched(nc)

    def din(name, shape):
        return nc.dram_tensor(name, list(shape), F32, kind="ExternalInput").ap()

    def _rev(ap):
        pat = [list(p) for p in ap.ap]
        st, n = pat[-1]
        pat[-1] = [-st, n]
        return bass.AP(ap.tensor, ap.offset + st * (n - 1), pat)

    xT_d = din("xT", (128, 8, TC))
    lnp_d = din("lnp", (128, 96))
    pflag_d = din("pflag", (128, 2))
    pinv_d = din("pinv", (128, 64))
    pscale_d = din("pscale", (128, 8))
    dtile_d = din("dtile", (128, 512))
    ident_d = din("ident", (128, 128))
    w_gu_d = din("w_gu", (88, 128, 2048))
    w_d_d = din("w_d", (32, 128, 2816))
    w_pin_d = din("w_pin", (128, 8192))
    w_pout_d = din("w_pout", (128, 8192))
    w_pgrp_d = din("w_pgrp", (128, 2048))
    w_qkv_d = din("w_qkv", (24, 128, 3072))
    w_ao_d = din("w_ao", (8, 128, 1024))
    if dbg is None:
        y_d = nc.dram_tensor("yT", [128, 8, OWN], F32, kind="ExternalOutput").ap()
    else:
        y_d = nc.dram_tensor("yT", [128, 8, TC], F32, kind="ExternalOutput").ap()

    base = (nc.sbuf_base + 31) // 32 * 32
    OFF_X = 0
    OFF_OS = 65792
    OFF_C = 98560
    OFF_R1 = OFF_C + 4096
    OFF_RING = OFF_R1 + 61440
    OFF_LN = OFF_RING + 23552
    OFF_SP = OFF_LN + 16384
    ARENA = OFF_SP + 4608
    arena = nc.alloc_sbuf_tensor("arena", [128, ARENA // 4], F32)
    abase = base
    assert nc.sbuf_base >= abase + ARENA

    def at(name, shape, dtype, off):
        return nc.alloc_sbuf_tensor_at(name, list(shape), dtype, offset=abase + off)

    xres = at("xres", (128, 8, TC), F32, OFF_X)
    os_t = at("os_t", (128, 8, 1024), F32, OFF_OS)
    lnp = at("lnp", (128, 96), F32, OFF_C)
    pflag = at("pflag", (128, 2), F32, OFF_C + 384)
    pinv = at("pinv", (128, 64), F32, OFF_C + 416)
    pscale = at("pscale", (128, 8), F32, OFF_C + 672)
    dtile = at("dtile", (128, 512), F32, OFF_C + 704)
    ones_bf = at("ones_bf", (128, 128), BF16, OFF_C + 2752)
    ident_bf = at("ident_bf", (128, 128), BF16, OFF_C + 3008)
    hsave = at("hsave", (128, 8, 8), F32, OFF_C + 3264)
    epst = at("epst", (128, 1), F32, OFF_C + 3520)
    negone = at("negone", (128, 1), F32, OFF_C + 3552)
    hbuf = at("hbuf", (128, NFC, 1024), BF16, OFF_R1)
    xb = at("xb", (128, 8, 1024), BF16, OFF_R1 + 45056)
    gu_slot = [at(f"gu{i}", (128, 2, 8, 128), BF16, OFF_RING + i * 4096) for i in range(3)]
    d_slot = [at(f"dw{i}", (128, NFC, 128), BF16, OFF_RING + 12288 + i * 5632) for i in range(2)]
    zb = at("zb", (128, 2, 1024), BF16, OFF_LN)
    zq = at("zq", (128, 2, 1024), BF16, OFF_LN + 4096)
    meant = at("meant", (128, 1024), F32, OFF_LN + 8192)
    rstdt = at("rstdt", (128, 1024), F32, OFF_LN + 12288)
    sgt = at("sgt", (128, 1024), F32, OFF_SP)
    p_win = at("p_win", (128, 8, 1024), BF16, OFF_OS)
    p_wout = at("p_wout", (128, 8, 1024), BF16, OFF_OS + 16384)
    p_wgrp = at("p_wgrp", (128, 4, 2, 256), BF16, OFF_R1 + 32768)
    p_xb = at("p_xb", (128, 8, 512), BF16, OFF_R1 + 36864)
    p_yb = at("p_yb", (128, 8, 512), BF16, OFF_R1 + 45056)
    p_mx = at("p_mx", (128, 8, 512), BF16, OFF_R1 + 53248)
    p_u = at("p_u", (128, 8, 528), F32, OFF_RING)
    p_t = [at(f"p_t{i}", (128, 528), F32, OFF_RING + 16896 + i * 2112) for i in range(2)]
    p_wn = at("p_wn", (128, 512), F32, OFF_RING + 21120)
    x1b = at("x1b", (128, 8, TH), BF16, OFF_R1)
    qT = at("qT", (128, OWN), BF16, OFF_R1 + 49152)
    kT = at("kT", (128, TH), BF16, OFF_R1 + 53248)
    PTt = [at(f"PT{i}", (128, 512), BF16, OFF_RING + 20480 + i * 1024) for i in range(3)]
    wqkv2 = [at(f"wqkv{i}", (128, 3, 8, 128), BF16, OFF_RING + i * 6144) for i in range(2)]
    vT = at("vT", (128, TH), BF16, OFF_RING + 12288)
    wao = at("wao", (128, 1024), BF16, OFF_RING + 18432)
    vaug = at("vaug", (128, 32, 192), BF16, OFF_LN)
    o_hp = at("o_hp", (128, OWN), BF16, OFF_LN + 12288)
    tt = [at(f"tt{i}", (128, 512), F32, OFF_SP + i * 2048) for i in range(2)]

    PSLO = nc.alloc_psum_tensor("pslo", [128, 2048], F32)
    PSHI = nc.alloc_psum_tensor("pshi", [128, 2048], F32)
    PS = [PSLO[:, 0:1024], PSLO[:, 1024:2048], PSHI[:, 0:1024], PSHI[:, 1024:2048]]
    PSB = [Buf(f"ps{i}") for i in range(4)]

    for k in ["x0", "x1", "xc", "xe", "xs0", "xs1", "const", "gu0", "gu1", "gu2", "dw0", "dw1", "pw", "aw0", "aw1", "ao", "out"]:
        S.new_dma_sem(k)

    B_const = Buf("const")
    B_ident = Buf("ident")
    B_ones = Buf("ones")
    TILES = [(0, 1024), (1024, 1024)]
    XR = [[Buf(f"xr{t}_{c}") for c in range(8)] for t in range(2)]
    XB = [Buf(f"xb{c}") for c in range(8)]
    HB = [Buf(f"hb{f}") for f in range(NFC)]
    GU = [Buf(f"gu{i}") for i in range(3)]
    DW = [Buf(f"dw{i}") for i in range(2)]
    B_sg = Buf("sg")
    B_zb = [Buf("zb0"), Buf("zb1")]
    B_zq = [Buf("zq0"), Buf("zq1")]
    B_mean = Buf("mean")
    B_rstd = Buf("rstd")
    ring = {"gu": 0, "dw": 0}

    def load_x(t):
        t0, nt = TILES[t]
        S.dma("sp", lambda e: e.dma_start(out=xres[:, :, t0:t0 + nt], in_=xT_d[:, :, t0:t0 + nt]), f"x{t}", writes=XR[t])
    load_x(1)
    for (dst, src) in [(lnp, lnp_d), (pflag, pflag_d), (pinv, pinv_d), (pscale, pscale_d), (dtile, dtile_d)]:
        S.dma("sp", (lambda dst, src: lambda e: e.dma_start(out=dst[:], in_=src))(dst, src), "const", writes=[B_const])
    B_const.wdep = ("const", S.cnt["const"])
    load_x(0)
    S.dma("pool", lambda e: e.dma_start(out=ident_bf[:], in_=ident_d), "pw", writes=[B_ident])
    B_pw = Buf("pw")
    B_pwg = Buf("pwg")
    S.op("dve", lambda e: e.memset(ones_bf[:], 1.0), writes=[B_ones])
    S.op("dve", lambda e: e.memset(negone[:], -1.0), writes=[B_const])
    S.op("dve", lambda e: e.memset(epst[:], EPS_P), writes=[B_const])
    B_const.wdep = ("const", S.cnt["const"])
    B_eps = Buf("eps")
    B_eps.wdep = ("dve", S.cnt["dve"])

    def ln_stats_chunk(c, t0, nt, xr_bufs):
        hv = _halves(nt)
        sl = c % 2
        S.op("act", (lambda c, sl: lambda e: e.activation(out=zb[:, sl, 0:nt], in_=xres[:, c, t0:t0 + nt], func=AF.Copy))(c, sl),
             reads=[xr_bufs[c]], writes=[B_zb[sl]])
        S.op("act", (lambda c, sl: lambda e: e.activation(out=zq[:, sl, 0:nt], in_=xres[:, c, t0:t0 + nt], func=AF.Square))(c, sl),
             reads=[xr_bufs[c]], writes=[B_zq[sl]])

        def mm(e, c=c, sl=sl):
            for (h0, hn) in hv:
                e.matmul(PS[0][:, h0:h0 + hn], lhsT=ones_bf[:, :], rhs=zb[:, sl, h0:h0 + hn], start=(c == 0), stop=(c == 7))
            for (h0, hn) in hv:
                ins = e.matmul(PS[1][:, h0:h0 + hn], lhsT=ones_bf[:, :], rhs=zq[:, sl, h0:h0 + hn], start=(c == 0), stop=(c == 7))
            return ins
        S.op("pe", mm, reads=[B_zb[sl], B_zq[sl], B_ones], writes=[PSB[0], PSB[1]])

    def ln_cols(l, s, t0, nt, xr_bufs):
        for c in range(8):
            ln_stats_chunk(c, t0, nt, xr_bufs)
        ln_finish(l, s, t0, nt, xr_bufs)

    def ln_finish_head(nt):
        S.op("dve", lambda e: e.tensor_scalar(out=meant[:, 0:nt], in0=PS[0][:, 0:nt], scalar1=1.0 / D_MODEL, scalar2=None, op0=ALU.mult),
             reads=[PSB[0]], writes=[B_mean])
        S.op("dve", lambda e: e.tensor_tensor(out=rstdt[:, 0:nt], in0=meant[:, 0:nt], in1=meant[:, 0:nt], op=ALU.mult),
             reads=[B_mean], writes=[B_rstd])
        S.op("dve", lambda e: e.scalar_tensor_tensor(out=rstdt[:, 0:nt], in0=PS[1][:, 0:nt], scalar=1.0 / D_MODEL, in1=rstdt[:, 0:nt],
                                                     op0=ALU.mult, op1=ALU.subtract),
             reads=[PSB[1], B_rstd], writes=[B_rstd])
        S.op("act", lambda e: e.activation(out=rstdt[:, 0:nt], in_=rstdt[:, 0:nt], func=AF.Ln, bias=epst[:, 0:1], scale=1.0),
             reads=[B_rstd, B_eps], writes=[B_rstd])
        S.op("act", lambda e: e.activation(out=rstdt[:, 0:nt], in_=rstdt[:, 0:nt], func=AF.Exp, scale=-0.5),
             reads=[B_rstd], writes=[B_rstd])

    def ln_finish_chunk(l, s, c, t0, nt, xr_bufs):
        gi = (l * 3 + s) * 16
        S.op("dve", lambda e: e.tensor_tensor(out=xres[:, c, t0:t0 + nt], in0=xres[:, c, t0:t0 + nt], in1=meant[:, 0:nt], op=ALU.subtract),
             reads=[xr_bufs[c], B_mean], writes=[xr_bufs[c]])
        S.op("dve", lambda e: e.tensor_tensor(out=xres[:, c, t0:t0 + nt], in0=xres[:, c, t0:t0 + nt], in1=rstdt[:, 0:nt], op=ALU.mult),
             reads=[xr_bufs[c], B_rstd], writes=[xr_bufs[c]])
        S.op("act", lambda e: e.activation(out=xres[:, c, t0:t0 + nt], in_=xres[:, c, t0:t0 + nt], func=AF.Identity,
                                           bias=lnp[:, gi + 8 + c:gi + 9 + c], scale=lnp[:, gi + c:gi + c + 1]),
             reads=[xr_bufs[c], B_const], writes=[xr_bufs[c]])

    def ln_finish(l, s, t0, nt, xr_bufs):
        ln_finish_head(nt)
        for c in range(8):
            ln_finish_chunk(l, s, c, t0, nt, xr_bufs)

    def ffn_cast(t):
        t0, nt = TILES[t]
        xr_bufs = XR[t]
        for c in range(8):
            eng = "dve" if c % 2 == 0 else "act"
            if eng == "dve":
                S.op("dve", (lambda c: lambda e: e.tensor_copy(out=xb[:, c, 0:nt], in_=xres[:, c, t0:t0 + nt]))(c), reads=[xr_bufs[c]], writes=[XB[c]])
            else:
                S.op("act", (lambda c: lambda e: e.activation(out=xb[:, c, 0:nt], in_=xres[:, c, t0:t0 + nt], func=AF.Copy))(c), reads=[xr_bufs[c]], writes=[XB[c]])

    def ffn_tile(l, fi, s, t, xr_bufs, do_cast=True, next_t=None, deferred=None, defer=False):
        t0, nt = TILES[t]
        hv = _halves(nt)
        if do_cast:
            ffn_cast(t)
        wbase = (l * 2 + fi) * NFC
        for fc in range(NFC):
            sl = ring["gu"] % 3
            ring["gu"] += 1
            S.dma("pool", (lambda sl, idx: lambda e: e.dma_start(out=gu_slot[sl][:], in_=w_gu_d[idx].rearrange("p (a k f) -> p a k f", a=2, k=8)))(sl, wbase + fc),
                  f"gu{sl}", writes=[GU[sl]])
            pg, pu = (2, 3) if fc % 2 == 0 else (0, 1)

            def mm(e, sl=sl, pg=pg, pu=pu):
                for (pp, a) in ((pg, 0), (pu, 1)):
                    for (h0, hn) in hv:
                        for kc in range(8):
                            ins = e.matmul(PS[pp][:, h0:h0 + hn], lhsT=gu_slot[sl][:, a, kc, :], rhs=xb[:, kc, h0:h0 + hn], start=(kc == 0), stop=(kc == 7))
                return ins
            S.op("pe", mm, reads=[GU[sl]] + XB, writes=[PSB[pg], PSB[pu]])
            S.op("act", (lambda pg: lambda e: e.activation(out=sgt[:, 0:nt], in_=PS[pg][:, 0:nt], func=AF.Silu))(pg), reads=[PSB[pg]], writes=[B_sg])
            S.op("dve", (lambda pu, fc: lambda e: e.tensor_tensor(out=hbuf[:, fc, 0:nt], in0=sgt[:, 0:nt], in1=PS[pu][:, 0:nt], op=ALU.mult))(pu, fc),
                 reads=[B_sg, PSB[pu]], writes=[HB[fc]])
            if deferred:
                deferred.pop(0)()
        while deferred:
            deferred.pop(0)()
        if next_t is not None:
            ffn_cast(next_t)
        dbase = (l * 2 + fi) * 8
        for dc in range(8):
            sl = ring["dw"] % 2
            ring["dw"] += 1
            S.dma("pool", (lambda sl, idx: lambda e: e.dma_start(out=d_slot[sl][:], in_=w_d_d[idx].rearrange("p (f d) -> p f d", f=NFC)))(sl, dbase + dc),
                  f"dw{sl}", writes=[DW[sl]])
            py = 2 + dc % 2

            def mm2(e, sl=sl, py=py):
                for (h0, hn) in hv:
                    for fc in range(NFC):
                        ins = e.matmul(PS[py][:, h0:h0 + hn], lhsT=d_slot[sl][:, fc, :], rhs=hbuf[:, fc, h0:h0 + hn], start=(fc == 0), stop=(fc == NFC - 1))
                return ins
            S.op("pe", mm2, reads=[DW[sl]] + HB, writes=[PSB[py]])
            S.op("dve", (lambda dc, py: lambda e: e.scalar_tensor_tensor(out=xres[:, dc, t0:t0 + nt], in0=PS[py][:, 0:nt], scalar=0.5 / ALPHA,
                                                                           in1=xres[:, dc, t0:t0 + nt], op0=ALU.mult, op1=ALU.add))(dc, py),
                 reads=[PSB[py], xr_bufs[dc]], writes=[xr_bufs[dc]])
            if dc >= 1:
                ln_stats_chunk(dc - 1, t0, nt, xr_bufs)
        ln_stats_chunk(7, t0, nt, xr_bufs)
        ln_finish_head(nt)
        pieces = [(lambda c: lambda: ln_finish_chunk(l, s, c, t0, nt, xr_bufs))(c) for c in range(8)]
        if defer:
            return pieces
        for p in pieces:
            p()
        return None

    def dump_and_finish():
        ncols = OWN if dbg is None else TC
        S.barrier()
        S.wait_all("sp", ["pe", "act", "dve"])
        for c in range(8):
            S.dma("sp", (lambda c: lambda e: e.dma_start(out=y_d[:, c, :], in_=xres[:, c, 0:ncols]))(c), "out",
                  reads=[b for t in range(2) for b in XR[t]])
        S.wait_all("sp", ["out"])
        S.wait_all("pool", ["gu0", "gu1", "gu2", "dw0", "dw1", "pw", "aw0", "aw1", "ao"])
        with nc.Block() as block:
            S.emit(block)
        return nc

    def ffn_jobs(jobs, extra_after=None, first_pend=None, first_cast_done=False):
        pend = first_pend
        for i, (l, fi, s_, t) in enumerate(jobs):
            nxt = jobs[i + 1][3] if i + 1 < len(jobs) else None
            pend = ffn_tile(l, fi, s_, t, XR[t], do_cast=(i == 0 and not first_cast_done), next_t=nxt, deferred=pend, defer=(nxt is not None))
            if extra_after and i in extra_after:
                assert pend is not None
                pend = pend + list(extra_after[i])

    xe_in = nc.dram_tensor("xe_in", [128, 64], F32)
    xe_out = nc.dram_tensor("xe_out", [256, 64], F32)
    B_ein, B_eout, B_estg = Buf("xe_in"), Buf("xe_out"), Buf("estg")
    estg = at("estg", (128, 2, 8, 8), F32, OFF_C + 3584)

    def start_exchange0():
        S.dma("pool", lambda e: e.dma_start(out=xe_in.ap().rearrange("p (c n) -> p c n", c=8), in_=xres[:, :, OWN - 8:OWN]), "xe",
              reads=XR[1], writes=[B_ein])
        S.op("pool", lambda e: e.collective_compute("AllGather", ALU.bypass, replica_groups=[[0, 1], [2, 3], [4, 5], [6, 7]],
                                                    ins=[xe_in.ap().opt()], outs=[xe_out.ap().opt()]),
             reads=[B_ein], writes=[B_eout])
    def prefetch_pool_weights():
        S.dma("pool", lambda e: e.dma_start(out=p_win[:], in_=w_pin_d.rearrange("p (k e) -> p k e", k=8)), "pw", writes=[B_pw])
        S.dma("pool", lambda e: e.dma_start(out=p_wout[:], in_=w_pout_d.rearrange("p (k e) -> p k e", k=8)), "pw", writes=[B_pw])
    ffn_jobs([(0, 0, 0, 1), (0, 0, 0, 0)], extra_after={0: [start_exchange0, prefetch_pool_weights]})
    if dbg == "l0ffn1":
        return dump_and_finish()

    S.barrier()
    ALLX = [b for t in range(2) for b in XR[t]]
    S.dma("pool", lambda e: e.dma_start(out=p_wgrp[:], in_=w_pgrp_d.rearrange("p (g c e) -> p g c e", g=4, c=2)), "pw", writes=[B_pw])
    B_pw.wdep = ("pw", S.cnt["pw"])
    B_pxb = [Buf(f"pxb{c}") for c in range(8)]
    B_pu = [Buf(f"pu{c}") for c in range(8)]
    B_pmx = [Buf(f"pmx{c}") for c in range(8)]
    B_pyb = [Buf(f"pyb{c}") for c in range(8)]
    B_pt = [Buf("pt0"), Buf("pt1")]
    B_pwn = Buf("pwn")
    B_hs = Buf("hsave")
    S.op("dve", lambda e: e.memset(p_u[:, :, 0:8], 0.0), writes=B_pu)
    S.dma("sp", lambda e: e.dma_start(out=estg[:], in_=xe_out.ap().rearrange("(r p) (c n) -> p r c n", r=2, c=8)), "xs0",
          reads=[B_eout], writes=[B_estg])
    S.op("dve", lambda e: e.tensor_scalar(out=hsave[:, :, :], in0=_rev(estg[:, 0, :, :]), scalar1=pflag[:, 1:2], scalar2=None, op0=ALU.mult),
         reads=[B_estg, B_const], writes=[B_hs])
    TP = 448
    ptiles = [(a, min(a + TP, OWN)) for a in range(0, OWN, TP)]
    XPt = [[Buf(f"xp{t}_{c}") for c in range(8)] for t in range(len(ptiles) + 1)]
    S.op("dve", lambda e: e.scalar_tensor_tensor(out=xres[:, :, OWN:OWN + 8], in0=_rev(estg[:, 1, :, :]), scalar=pflag[:, 0:1], in1=hsave[:, :, :],
                                                 op0=ALU.mult, op1=ALU.add),
         reads=[B_estg, B_hs, B_const], writes=XPt[len(ptiles)])
    def pool_tile(ti, a, b):
        XP = XPt[ti]
        XN = XPt[ti + 1]
        n_o = b - a
        first = (ti == 0)
        ua = 0 if first else a - 8
        ub = b + 8
        n_u = ub - ua
        loff = 8 if first else 0
        for c in range(8):
            if first:
                S.op("act", (lambda c: lambda e: e.activation(out=p_xb[:, c, 0:n_u], in_=xres[:, c, ua:ub], func=AF.Copy))(c), reads=[XP[c], XN[c]], writes=[B_pxb[c]])
            else:
                S.op("act", (lambda c: lambda e: e.activation(out=p_xb[:, c, 0:8], in_=hsave[:, c, :], func=AF.Copy))(c), reads=[B_hs], writes=[B_pxb[c]])
                S.op("act", (lambda c: lambda e: e.activation(out=p_xb[:, c, 8:n_u], in_=xres[:, c, a:ub], func=AF.Copy))(c), reads=[XP[c], XN[c]], writes=[B_pxb[c]])
        S.op("dve", lambda e: e.tensor_copy(out=hsave[:, :, :], in_=xres[:, :, b - 8:b]), reads=XP + B_pxb, writes=[B_hs])
        for c in range(8):
            pp = c % 4

            def mmu(e, c=c, pp=pp):
                for kc in range(8):
                    ins = e.matmul(PS[pp][:, 0:n_u], lhsT=p_win[:, kc, c * 128:(c + 1) * 128], rhs=p_xb[:, kc, 0:n_u], start=(kc == 0), stop=(kc == 7))
                return ins
            S.op("pe", mmu, reads=[B_pw] + B_pxb, writes=[PSB[pp]])
            S.op("act", (lambda c, pp: lambda e: e.activation(out=p_u[:, c, loff:loff + n_u], in_=PS[pp][:, 0:n_u], func=AF.Copy))(c, pp),
                 reads=[PSB[pp]], writes=[B_pu[c]])
        n_loc = n_u + loff
        lo0 = 8
        for c in range(8):
            g = c // 2
            h = 1 << g
            w = 2 * h
            src = p_u[:, c, :]
            srcb = B_pu[c]
            ln = 1
            k = 0
            while ln < w:
                dst = p_t[k % 2]
                cnt = n_loc - 2 * ln + 1
                S.op("dve", (lambda src, dst, ln, cnt: lambda e: e.tensor_tensor(out=dst[:, 0:cnt], in0=src[:, 0:cnt], in1=src[:, ln:ln + cnt], op=ALU.add))(src, dst, ln, cnt),
                     reads=[srcb], writes=[B_pt[k % 2]])
                src = dst
                srcb = B_pt[k % 2]
                ln *= 2
                k += 1
            s0 = lo0 - h
            S.op("dve", (lambda src, s0: lambda e: e.tensor_scalar(out=p_wn[:, 0:n_o], in0=src[:, s0 + 1:s0 + 1 + n_o], scalar1=pflag[:, 1:2], scalar2=None, op0=ALU.mult))(src, s0),
                 reads=[srcb, B_const], writes=[B_pwn])
            S.op("dve", (lambda src, s0: lambda e: e.scalar_tensor_tensor(out=p_wn[:, 0:n_o], in0=src[:, s0:s0 + n_o], scalar=pflag[:, 0:1], in1=p_wn[:, 0:n_o],
                                                                            op0=ALU.mult, op1=ALU.add))(src, s0),
                 reads=[srcb, B_const, B_pwn], writes=[B_pwn])
            S.op("dve", (lambda c, w: lambda e: e.scalar_tensor_tensor(out=p_mx[:, c, 0:n_o], in0=p_wn[:, 0:n_o], scalar=1.0 / w, in1=p_u[:, c, lo0:lo0 + n_o],
                                                                         op0=ALU.mult, op1=ALU.subtract))(c, w),
                 reads=[B_pwn, B_pu[c]], writes=[B_pmx[c]])
            if first:
                S.op("dve", (lambda c: lambda e: e.tensor_tensor(out=p_wn[:, 0:8], in0=p_wn[:, 0:8], in1=pinv[:, c * 8:c * 8 + 8], op=ALU.mult))(c),
                     reads=[B_pwn, B_const], writes=[B_pwn])
                S.op("dve", (lambda c: lambda e: e.tensor_tensor(out=p_mx[:, c, 0:8], in0=p_wn[:, 0:8], in1=p_u[:, c, lo0:lo0 + 8], op=ALU.subtract))(c),
                     reads=[B_pwn, B_pu[c]], writes=[B_pmx[c]])
        for ec in range(8):
            g = ec // 2
            eh = ec % 2
            pp = ec % 4

            def mmg(e, g=g, eh=eh, pp=pp):
                for cc in range(2):
                    ins = e.matmul(PS[pp][:, 0:n_o], lhsT=p_wgrp[:, g, cc, eh * 128:(eh + 1) * 128], rhs=p_mx[:, 2 * g + cc, 0:n_o], start=(cc == 0), stop=(cc == 1))
                return ins
            S.op("pe", mmg, reads=[B_pw, B_pmx[2 * g], B_pmx[2 * g + 1]], writes=[PSB[pp]])
            S.op("act", (lambda ec, pp: lambda e: e.activation(out=p_yb[:, ec, 0:n_o], in_=PS[pp][:, 0:n_o], func=AF.Identity, scale=pscale[:, ec:ec + 1]))(ec, pp),
                 reads=[PSB[pp], B_const], writes=[B_pyb[ec]])
        for dc in range(8):
            pp = dc % 4

            def mmo(e, dc=dc, pp=pp):
                for ec in range(8):
                    ins = e.matmul(PS[pp][:, 0:n_o], lhsT=p_wout[:, ec, dc * 128:(dc + 1) * 128], rhs=p_yb[:, ec, 0:n_o], start=(ec == 0), stop=(ec == 7))
                return ins
            S.op("pe", mmo, reads=[B_pw] + B_pyb, writes=[PSB[pp]])
            S.op("dve", (lambda dc, pp: lambda e: e.scalar_tensor_tensor(out=xres[:, dc, a:b], in0=PS[pp][:, 0:n_o], scalar=1.0 / ALPHA, in1=xres[:, dc, a:b],
                                                                           op0=ALU.mult, op1=ALU.add))(dc, pp),
                 reads=[PSB[pp], XP[dc], B_hs], writes=[XP[dc]])
        return None
    for ti, (a, b) in enumerate(ptiles):
        pool_tile(ti, a, b)
        if ti > 0:
            pa, pb = ptiles[ti - 1]
            ln_cols(0, 1, pa, pb - pa, XPt[ti - 1])
    pa, pb = ptiles[-1]
    ln_cols(0, 1, pa, pb - pa, XPt[len(ptiles) - 1])
    S.barrier()
    if dbg == "l0pool":
        return dump_and_finish()

    if dbg == "l0":
        ffn_jobs([(0, 1, 2, t) for t in range(2)])
        return dump_and_finish()
    xc_in = nc.dram_tensor("xc_in", [128, 8192], BF16)
    xc_out = nc.dram_tensor("xc_out", [256, 8192], BF16)
    B_inb, B_outb = Buf("xc_in"), Buf("xc_out")

    def start_exchange():
        S.dma("pool", lambda e: e.dma_start(out=xc_in.ap().rearrange("p (c n) -> p c n", c=8), in_=xres[:, :, 1024:2048]), "xc",
              reads=XR[1], writes=[B_inb])
        S.op("pool", lambda e: e.collective_compute("AllGather", ALU.bypass, replica_groups=[[0, 1], [2, 3], [4, 5], [6, 7]],
                                                    ins=[xc_in.ap().opt()], outs=[xc_out.ap().opt()]),
             reads=[B_inb], writes=[B_outb])
    ffn_jobs([(0, 1, 2, 1), (0, 1, 2, 0), (1, 0, 0, 1), (1, 0, 0, 0)], extra_after={2: [start_exchange]})
    S.barrier()
    if dbg == "l1ffn1":
        return dump_and_finish()

    B_x1b = [Buf(f"x1b{c}") for c in range(8)]
    for c in range(8):
        if c % 2 == 0:
            S.op("dve", (lambda c: lambda e: e.tensor_copy(out=x1b[:, c, 0:OWN], in_=xres[:, c, 0:OWN]))(c), reads=ALLX, writes=[B_x1b[c]])
        else:
            S.op("act", (lambda c: lambda e: e.activation(out=x1b[:, c, 0:OWN], in_=xres[:, c, 0:OWN], func=AF.Copy))(c), reads=ALLX, writes=[B_x1b[c]])
    S.wait_all("sp", ["pe", "act", "dve"])
    stg = [at(f"stg{i}", (128, 2, 1024), BF16, OFF_RING + i * 4096) for i in range(2)]
    xtmp = at("xtmp", (128, 1024), BF16, OFF_RING + 8192)
    B_stg = [Buf("stg0"), Buf("stg1")]
    B_xtmp = Buf("xtmp")
    outv = xc_out.ap().rearrange("(r p) (c n) -> p r c n", r=2, c=8)

    for c in range(8):
        sl = c % 2
        S.dma("sp", (lambda c, sl: lambda e: e.dma_start(out=stg[sl][:], in_=outv[:, :, c, :]))(c, sl), f"xs{sl}", reads=[B_outb], writes=[B_stg[sl]])
        S.op("dve", (lambda sl: lambda e: e.tensor_scalar(out=xtmp[:, :], in0=_rev(stg[sl][:, 0, :]), scalar1=pflag[:, 1:2], scalar2=None, op0=ALU.mult))(sl),
             reads=[B_stg[sl], B_const], writes=[B_xtmp])
        S.op("dve", (lambda c, sl: lambda e: e.scalar_tensor_tensor(out=x1b[:, c, OWN:TH], in0=_rev(stg[sl][:, 1, :]), scalar=pflag[:, 0:1], in1=xtmp[:, :],
                                                                     op0=ALU.mult, op1=ALU.add))(c, sl),
             reads=[B_stg[sl], B_xtmp, B_const], writes=[B_x1b[c]])
    S.barrier()
    B_vaug = Buf("vaug")
    S.op("dve", lambda e: e.memset(vaug[:, :, 64:128], 1.0), writes=[B_vaug])
    B_wq = [Buf("wqkv0"), Buf("wqkv1")]
    B_q = Buf("qT")
    B_k = Buf("kT")
    B_v = Buf("vT")
    B_pt2 = [Buf("PT0"), Buf("PT1"), Buf("PT2")]
    B_tt = [Buf("tt0"), Buf("tt1")]
    B_os = [[Buf(f"os{h}_{q}") for q in range(4)] for h in range(2)]
    B_rd = [Buf("rden0"), Buf("rden1")]
    B_ohp = Buf("ohp")
    B_wao = Buf("wao")
    XO = [[Buf(f"xo{dc}_{q}") for q in range(4)] for dc in range(8)]
    PSA = [Buf(f"psa{i}") for i in range(8)]
    psT = PSHI[:, :].bitcast(BF16)

    def osum(hd, qt):
        return os_t[:, hd * 4 + qt, 0:512]

    slopes = 2.0 ** (-8.0 * np.arange(1, 49) / 48.0)
    slopes = slopes.reshape(3, 16)
    rot = {"wk": 0, "tt": 0, "pt": 0, "wq": 0}

    def wk_slot():
        b = rot["wk"] % 4
        rot["wk"] += 1
        return b

    def attn_proj(hp, g, win, d):
        Lq = OWN // d
        Lk = Lq + 64
        ntok_k = Lk * d
        ws = rot["wq"] % 2
        rot["wq"] += 1
        wq = wqkv2[ws]
        S.dma("pool", (lambda idx, wq: lambda e: e.dma_start(out=wq[:], in_=w_qkv_d[idx].rearrange("p (m k f) -> p m k f", m=3, k=8)))(hp * 3 + g, wq),
              f"aw{ws}", writes=[B_wq[ws]])
        ev = 0
        for m, (dstT, dstB, ntok) in enumerate(((qT, B_q, OWN), (kT, B_k, ntok_k), (vT, B_v, ntok_k))):
            dview = dstT[:, 0:ntok].rearrange("p (r i) -> p r i", r=d)
            for j0 in range(0, ntok, 512):
                n = min(512, ntok - j0)
                bk = wk_slot()
                ph = bk * 512

                def mmp(e, m=m, j0=j0, n=n, ph=ph):
                    for kc in range(8):
                        ins = e.matmul(PSHI[:, ph:ph + n], lhsT=wq[:, m, kc, :], rhs=x1b[:, kc, j0:j0 + n], start=(kc == 0), stop=(kc == 7))
                    return ins
                S.op("pe", mmp, reads=[B_wq[ws]] + B_x1b, writes=[PSA[4 + bk]])
                i0 = j0 // d
                ni = n // d
                src = PSHI[:, ph:ph + n].rearrange("p (i r) -> p r i", r=d)
                if ev % 2 == 0:
                    S.op("act", (lambda dview, src, i0, ni: lambda e: e.activation(out=dview[:, :, i0:i0 + ni], in_=src, func=AF.Copy))(dview, src, i0, ni),
                         reads=[PSA[4 + bk]], writes=[dstB])
                else:
                    S.op("dve", (lambda dview, src, i0, ni: lambda e: e.tensor_copy(out=dview[:, :, i0:i0 + ni], in_=src))(dview, src, i0, ni),
                         reads=[PSA[4 + bk]], writes=[dstB])
                ev += 1
        chunks = []
        for r in range(d):
            for k0 in range(0, Lk, 128):
                chunks.append((r, k0, min(128, Lk - k0)))
        for c0 in range(0, len(chunks), 4):
            grp = chunks[c0:c0 + 4]
            ng = len(grp)
            bk = wk_slot()
            pbase = bk * 1024

            def tr(e, grp=grp, pbase=pbase):
                for q, (r, k0, nk) in enumerate(grp):
                    ins = e.transpose(psT[0:nk, pbase + q * 128:pbase + (q + 1) * 128], vT[:, r * Lk + k0:r * Lk + k0 + nk], ident_bf[:, :])
                return ins
            S.op("pe", tr, reads=[B_v, B_ident], writes=[PSA[4 + bk]])
            srcv = psT[:, pbase:pbase + ng * 128].rearrange("p (q f) -> p q f", f=128)
            if (c0 // 4) % 2 == 0:
                S.op("act", (lambda c0, ng, srcv: lambda e: e.activation(out=vaug[:, c0:c0 + ng, 0:64], in_=srcv[:, :, 0:64], func=AF.Copy))(c0, ng, srcv),
                     reads=[PSA[4 + bk]], writes=[B_vaug])
                S.op("act", (lambda c0, ng, srcv: lambda e: e.activation(out=vaug[:, c0:c0 + ng, 128:192], in_=srcv[:, :, 64:128], func=AF.Copy))(c0, ng, srcv),
                     reads=[PSA[4 + bk]], writes=[B_vaug])
            else:
                S.op("dve", (lambda c0, ng, srcv: lambda e: e.tensor_copy(out=vaug[:, c0:c0 + ng, 0:64], in_=srcv[:, :, 0:64]))(c0, ng, srcv),
                     reads=[PSA[4 + bk]], writes=[B_vaug])
                S.op("dve", (lambda c0, ng, srcv: lambda e: e.tensor_copy(out=vaug[:, c0:c0 + ng, 128:192], in_=srcv[:, :, 64:128]))(c0, ng, srcv),
                     reads=[PSA[4 + bk]], writes=[B_vaug])
        return {"Lq": Lq, "Lk": Lk, "chunks": chunks}

    def attn_core(hp, g, win, d, ctx):
        Lq, Lk, chunks = ctx["Lq"], ctx["Lk"], ctx["chunks"]
        items = []
        for hd in range(2):
            work = []
            for ci, (r, k0, nk) in enumerate(chunks):
                i0 = max(0, k0 - 64)
                i1 = min(Lq, k0 + nk + 64)
                if i1 > i0:
                    work.append((ci, r, k0, nk, i0, i1))
            npair = (len(work) + 1) // 2
            for pi in range(npair):
                items.append({"hd": hd, "pair": work[2 * pi:2 * pi + 2], "last": pi == npair - 1})
        started = {0: [False] * 4, 1: [False] * 4}
        accv = PSLO[:, :].rearrange("p (r i) -> p i r", r=d)

        def emit_scores(it):
            hd = it["hd"]
            pair = it["pair"]
            hg = 2 * hp + hd
            cneg = -float(slopes[g, hg]) * d * 8.0
            hrow = slice(64 * hd, 64 * hd + 64)
            bk = wk_slot()
            ph = bk * 512

            def mms(e, pair=pair, ph=ph, hrow=hrow):
                for q, (ci, r, k0, nk, i0, i1) in enumerate(pair):
                    doff = i0 - (k0 - 64)
                    nq = i1 - i0
                    ins = e.matmul(PSHI[0:nk, ph + q * 256 + doff:ph + q * 256 + doff + nq], lhsT=kT[hrow, r * Lk + k0:r * Lk + k0 + nk],
                                   rhs=qT[hrow, r * Lq + i0:r * Lq + i0 + nq], start=True, stop=True, skip_group_check=True)
                return ins
            S.op("pe", mms, reads=[B_k, B_q], writes=[PSA[4 + bk]])
            ts = rot["tt"] % 2
            rot["tt"] += 1
            ps_ = rot["pt"] % 3
            rot["pt"] += 1
            it["ps"] = ps_
            S.op("dve", (lambda ph, ts, cneg: lambda e: e.scalar_tensor_tensor(
                out=tt[ts][:, :], in0=dtile[:, :], scalar=cneg, in1=PSHI[:, ph:ph + 512], op0=ALU.mult, op1=ALU.add))(ph, ts, cneg),
                reads=[B_const, PSA[4 + bk]], writes=[B_tt[ts]])
            S.op("act", (lambda ts, ps_: lambda e: e.activation(out=PTt[ps_][:, :], in_=tt[ts][:, :], func=AF.Exp, scale=0.125))(ts, ps_),
                 reads=[B_tt[ts]], writes=[B_pt2[ps_]])

        def emit_pv(it):
            hd = it["hd"]
            pair = it["pair"]
            ps_ = it["ps"]
            vsel = (slice(0, 128) if hd == 0 else slice(64, 192))
            st = started[hd]
            plan = []
            wb = set()
            for q, (ci, r, k0, nk, i0, i1) in enumerate(pair):
                doff = i0 - (k0 - 64)
                pos = r * Lq + i0
                end = r * Lq + i1
                while pos < end:
                    bb = pos // 512
                    nx = min(end, (bb + 1) * 512)
                    plan.append((ci, nk, q * 256 + doff + (pos - (r * Lq + i0)), pos, nx - pos, not st[bb]))
                    st[bb] = True
                    wb.add(bb)
                    pos = nx

            def mmv(e, plan=plan, ps_=ps_, vsel=vsel):
                for (ci, nk, pcol, pos, n, stt) in plan:
                    ins = e.matmul(PSLO[:, pos:pos + n], lhsT=vaug[0:nk, ci, vsel], rhs=PTt[ps_][0:nk, pcol:pcol + n], start=stt, stop=True,
                                   skip_group_check=True)
                return ins
            S.op("pe", mmv, reads=[B_vaug, B_pt2[ps_]], writes=[PSA[bb] for bb in sorted(wb)])
            if it["last"]:
                for qt in range(4):
                    ia = 512 * qt // d
                    nn = 512 // d
                    srcp = accv[:, ia:ia + nn, :]
                    dst = osum(hd, qt).rearrange("p (i r) -> p i r", r=d)
                    rb = [PSA[qt]] if d == 1 else [PSA[0], PSA[1], PSA[2], PSA[3]]
                    if g == 0:
                        S.op("act", (lambda dst, srcp: lambda e: e.activation(out=dst, in_=srcp, func=AF.Copy))(dst, srcp),
                             reads=rb, writes=[B_os[hd][qt]])
                    else:
                        S.op("dve", (lambda dst, srcp: lambda e: e.tensor_tensor(out=dst, in0=dst, in1=srcp, op=ALU.add))(dst, srcp),
                             reads=rb + [B_os[hd][qt]], writes=[B_os[hd][qt]])

        LA = 2
        for idx in range(len(items) + LA):
            if idx < len(items):
                emit_scores(items[idx])
            if idx - LA >= 0:
                emit_pv(items[idx - LA])

    def attn_finish(hp):
        for hd in range(2):
            nrow = slice(0, 64) if hd == 0 else slice(64, 128)
            drow = slice(64, 128) if hd == 0 else slice(0, 64)
            osv = os_t[:, hd * 4:hd * 4 + 4, 0:512]
            rdv = os_t[:, hd * 4:hd * 4 + 4, 512:1024]
            ohv = o_hp[:, :].rearrange("p (q n) -> p q n", q=4)
            allos = [B_os[hd][q] for q in range(4)]
            S.op("act", (lambda osv, rdv, nrow, drow: lambda e: e.activation(out=rdv[nrow], in_=osv[drow], func=AF.Ln))(osv, rdv, nrow, drow),
                 reads=allos, writes=[B_rd[hd]])
            S.op("act", (lambda rdv, nrow: lambda e: e.activation(out=rdv[nrow], in_=rdv[nrow], func=AF.Exp, scale=-1.0))(rdv, nrow),
                 reads=[B_rd[hd]], writes=[B_rd[hd]])
            S.op("dve", (lambda osv, rdv, ohv, nrow: lambda e: e.tensor_tensor(out=ohv[nrow], in0=osv[nrow], in1=rdv[nrow], op=ALU.mult))(osv, rdv, ohv, nrow),
                 reads=allos + [B_rd[hd]], writes=[B_ohp])

    def attn_oproj(hp):
        S.dma("pool", (lambda hp: lambda e: e.dma_start(out=wao[:], in_=w_ao_d[hp]))(hp), "ao", writes=[B_wao])
        for dc in range(8):
            for qt in range(4):
                bk = wk_slot()
                ph = bk * 512

                def mmo2(e, dc=dc, qt=qt, ph=ph):
                    return e.matmul(PSHI[:, ph:ph + 512], lhsT=wao[:, dc * 128:(dc + 1) * 128], rhs=o_hp[:, qt * 512:(qt + 1) * 512], start=True, stop=True)
                S.op("pe", mmo2, reads=[B_wao, B_ohp], writes=[PSA[4 + bk]])
                S.op("dve", (lambda dc, qt, ph: lambda e: e.scalar_tensor_tensor(out=xres[:, dc, qt * 512:(qt + 1) * 512], in0=PSHI[:, ph:ph + 512], scalar=1.0 / ALPHA,
                                                                               in1=xres[:, dc, qt * 512:(qt + 1) * 512], op0=ALU.mult, op1=ALU.add))(dc, qt, ph),
                     reads=[PSA[4 + bk], XO[dc][qt]], writes=[XO[dc][qt]])
    ctx_next = attn_proj(0, 0, DIL[0][0], DIL[0][1])
    for hp in range(8):
        for g, (win, d) in enumerate(DIL):
            ctx = ctx_next
            attn_core(hp, g, win, d, ctx)
            if g < 2:
                ctx_next = attn_proj(hp, g + 1, DIL[g + 1][0], DIL[g + 1][1])
        attn_finish(hp)
        if hp < 7:
            ctx_next = attn_proj(hp + 1, 0, DIL[0][0], DIL[0][1])
        attn_oproj(hp)
    S.barrier()
    if dbg == "l1attn":
        for t in range(2):
            t0, nt = TILES[t]
            ln_cols(1, 1, t0, nt, XR[t])
        S.barrier()
        return dump_and_finish()
    ln_cols(1, 1, 0, 1024, XR[0])
    ffn_cast(0)
    for c in range(8):
        ln_stats_chunk(c, 1024, 1024, XR[1])
    ln_finish_head(1024)
    tail1 = [(lambda c: lambda: ln_finish_chunk(1, 1, c, 1024, 1024, XR[1]))(c) for c in range(8)]
    ffn_jobs([(1, 1, 2, 0), (1, 1, 2, 1)], first_pend=tail1, first_cast_done=True)
    return dump_and_finish()


def _prep_shared(ffn1_w_gate, ffn1_w_up, ffn1_w_down, ffn2_w_gate, ffn2_w_up, ffn2_w_down,
                 ln_gain, ln_bias, pool_w_in, pool_w_group, pool_scale, pool_w_out, attn_w_qkv, attn_w_out):
    f = np.float32
    gates = [ffn1_w_gate, ffn2_w_gate]
    ups = [ffn1_w_up, ffn2_w_up]
    downs = [ffn1_w_down, ffn2_w_down]
    w_gu = np.empty((2, 2, NFC, 128, 2, 8, 128), f)
    w_d = np.empty((2, 2, 8, 128, NFC, 128), f)
    for l in range(2):
        for fi in range(2):
            g = np.asarray(gates[fi][l], f).reshape(8, 128, NFC, 128)
            u = np.asarray(ups[fi][l], f).reshape(8, 128, NFC, 128)
            w_gu[l, fi, :, :, 0] = g.transpose(2, 1, 0, 3)
            w_gu[l, fi, :, :, 1] = u.transpose(2, 1, 0, 3)
            dn = np.asarray(downs[fi][l], f).reshape(NFC, 128, 8, 128)
            w_d[l, fi] = dn.transpose(2, 1, 0, 3)
    lnp = np.empty((128, 2, 3, 2, 8), f)
    lnp[:, :, :, 0, :] = np.asarray(ln_gain, f).reshape(2, 3, 8, 128).transpose(3, 0, 1, 2)
    lnp[:, :, :, 1, :] = np.asarray(ln_bias, f).reshape(2, 3, 8, 128).transpose(3, 0, 1, 2)
    w_pin = np.asarray(pool_w_in[0], f).reshape(8, 128, 1024).transpose(1, 0, 2)
    w_pout = np.asarray(pool_w_out[0], f).reshape(8, 128, 1024).transpose(1, 0, 2)
    w_pgrp = np.asarray(pool_w_group[0], f).reshape(4, 2, 128, 256).transpose(2, 0, 1, 3)
    pscale = np.asarray(pool_scale[0], f).reshape(8, 128).T
    wq = np.asarray(attn_w_qkv[0], f).reshape(8, 128, 3, 3, 8, 128)
    w_qkv = wq.transpose(4, 2, 1, 3, 0, 5)
    w_ao = np.asarray(attn_w_out[0], f).reshape(8, 128, 1024)
    r = np.arange(128)[:, None]
    i = np.arange(256)[None, :]
    dd = np.abs(r - i + 64).astype(f)
    dtile = np.where(dd <= 64, dd, BIGD).astype(f)
    dtile = np.concatenate([dtile, dtile], axis=1)
    c = np.ascontiguousarray
    return {
        "lnp": c(lnp.reshape(128, 96)), "pscale": c(pscale), "dtile": c(dtile), "ident": np.eye(128, dtype=f),
        "w_gu": c(w_gu.reshape(88, 128, 2048)), "w_d": c(w_d.reshape(32, 128, 2816)),
        "w_pin": c(w_pin.reshape(128, 8192)), "w_pout": c(w_pout.reshape(128, 8192)), "w_pgrp": c(w_pgrp.reshape(128, 2048)),
        "w_qkv": c(w_qkv.reshape(24, 128, 3072)), "w_ao": c(w_ao),
    }


def _prep_core(x, core):
    f = np.float32
    b, half = core // 2, core % 2
    xs = np.asarray(x[b], f)
    if half == 0:
        loc = xs[0:TC]
    else:
        loc = xs[::-1][0:TC]
    xT = np.ascontiguousarray(loc.T.reshape(8, 128, TC).transpose(1, 0, 2))
    pflag = np.zeros((128, 2), f)
    pflag[:, half] = 1.0
    pinv = np.empty((128, 8, 8), f)
    for cidx in range(8):
        h = 1 << (cidx // 2)
        w = 2 * h
        for jo in range(8):
            if half == 0:
                cnt = h + min(h, jo)
            else:
                cnt = h + min(h, jo + 1)
            pinv[:, cidx, jo] = 1.0 / cnt
    return {"xT": xT, "pflag": pflag, "pinv": np.ascontiguousarray(pinv.reshape(128, 64))}


_NC_CACHE = {}


def _get_nc(dbg=None):
    if dbg not in _NC_CACHE:
        _NC_CACHE[dbg] = build_program(dbg)
    return _NC_CACHE[dbg]


def run_cores(inputs, dbg=None, cores=range(8), trace=False):
    x = np.asarray(inputs["x"], np.float32)
    shared = _prep_shared(**{k: v for k, v in inputs.items() if k != "x"})
    in_maps = []
    for core in cores:
        m = dict(shared)
        m.update(_prep_core(x, core))
        in_maps.append(m)
    nc = _get_nc(dbg)
    res = run_bass_kernel_spmd(nc, in_maps, core_ids=list(range(len(in_maps))), **({"trace": True} if trace else {}))
    return res


def kernel(**inputs):
    res = run_cores(inputs)
    out = np.empty((BATCH, SEQ, D_MODEL), np.float32)
    for core in range(8):
        yT = np.asarray(res.results[core]["yT"], np.float32)
        y = yT.transpose(2, 1, 0).reshape(OWN, D_MODEL)
        b, half = core // 2, core % 2
        if half == 0:
            out[b, 0:OWN] = y
        else:
            out[b, OWN:SEQ] = y[::-1]
    return out
```
